# Optimizing a Trainium2 kernel written in Bass

```python
import math
import jax, jax.numpy as jnp
from jax import lax
import numpy as np

D_MODEL = 1024
BATCH = 8
SEQ = 2048
DEPTH = 4
DEC_BATCH = 16
DEC_SEQ = 32
PAST_LEN = 4096

CHUNK = 64
N_A_LAYERS = DEPTH // 2
N_B_LAYERS = DEPTH - N_A_LAYERS
A_HEADS = 8
A_DK = 128
A_DV = 128
A_QK = A_HEADS * A_DK
A_VW = A_HEADS * A_DV
CONV_W = 4
CONV_DIM = 2 * A_QK + A_VW
A_IN = CONV_DIM + A_VW + 2 * A_HEADS
B_Q_HEADS = 16
B_KV_HEADS = 4
B_GROUP = B_Q_HEADS // B_KV_HEADS
B_HD = 64
B_QW = B_Q_HEADS * B_HD
B_KVW = B_KV_HEADS * B_HD
B_IN = 2 * B_QW
WINDOW = 128
WINDOW_CHUNKS = WINDOW // CHUNK
ROPE_DIMS = B_HD // 4
ROPE_THETA = 500000.0
DN_ALPHA = (2 * DEPTH) ** 0.25
DN_BETA = (8 * DEPTH) ** -0.25
LN_EPS = 1e-5
RMS_EPS = 1e-6

kernel_name = 'yoco_gdn_swa_sink_stream_step'


def layer_norm(x, g, b):
    xf = x.astype(jnp.float32)
    mu = xf.mean(-1, keepdims=True)
    var = jnp.square(xf - mu).mean(-1, keepdims=True)
    return ((xf - mu) * lax.rsqrt(var + LN_EPS) * g + b).astype(x.dtype)


def l2norm(x):
    xf = x.astype(jnp.float32)
    return xf * lax.rsqrt(jnp.sum(xf * xf, -1, keepdims=True) + RMS_EPS)


def rope_partial(x, pos):
    half = ROPE_DIMS // 2
    inv = ROPE_THETA ** (-jnp.arange(half, dtype=jnp.float32) * 2.0 / ROPE_DIMS)
    ang = pos.astype(jnp.float32)[:, None] * inv[None, :]
    cos = jnp.cos(ang)[None, :, None, :]
    sin = jnp.sin(ang)[None, :, None, :]
    xf = x.astype(jnp.float32)
    x1, x2 = xf[..., :half], xf[..., half:ROPE_DIMS]
    rot = jnp.concatenate([x1 * cos - x2 * sin, x2 * cos + x1 * sin], -1)
    return jnp.concatenate([rot, xf[..., ROPE_DIMS:]], -1).astype(x.dtype)


def causal_conv(x, buf, w):
    l = x.shape[1]
    xp = jnp.concatenate([buf.astype(x.dtype), x], axis=1)
    y = xp[:, 0:l] * w[0]
    for j in range(1, CONV_W):
        y = y + xp[:, j:j + l] * w[j]
    return jax.nn.silu(y), xp[:, -(CONV_W - 1):]


def gated_delta_rule(q, k, v, g, beta, s0):
    b, l, h, _ = q.shape
    c = min(CHUNK, l)
    n = l // c

    def blocks(t):
        t = t.reshape((b, n, c) + t.shape[2:])
        return jnp.swapaxes(jnp.moveaxis(t, 1, 0), 2, 3)

    qc, kc, vc, gb, bc = blocks(q), blocks(k), blocks(v), blocks(g), blocks(beta)
    gcum = jnp.cumsum(gb, axis=-1)
    incl = jnp.tril(jnp.ones((c, c), dtype=bool))
    strict = jnp.tril(jnp.ones((c, c), dtype=bool), -1)
    decay = jnp.exp(jnp.where(incl, gcum[..., :, None] - gcum[..., None, :], -jnp.inf))
    kbeta = kc * bc[..., None]
    lower = jnp.where(strict, jnp.einsum('nbhik,nbhjk->nbhij', kbeta, kc) * decay, 0.0)
    tmat = lower + jnp.eye(c, dtype=lower.dtype)
    u = lax.linalg.triangular_solve(tmat, vc * bc[..., None], left_side=True, lower=True, unit_diagonal=True)
    w = lax.linalg.triangular_solve(tmat, kbeta * jnp.exp(gcum)[..., None], left_side=True, lower=True,
                                    unit_diagonal=True)
    intra = jnp.einsum('nbhik,nbhjk->nbhij', qc, kc) * decay
    qdec = qc * jnp.exp(gcum)[..., None]
    kdec = kc * jnp.exp(gcum[..., -1:] - gcum)[..., None]
    glast = jnp.exp(gcum[..., -1])

    def step(s, xs):
        qd, kd, ui, wi, ai, gl = xs
        v_new = ui - jnp.einsum('bhck,bhkv->bhcv', wi, s)
        o = jnp.einsum('bhck,bhkv->bhcv', qd, s) + jnp.einsum('bhij,bhjv->bhiv', ai, v_new)
        s = s * gl[..., None, None] + jnp.einsum('bhck,bhcv->bhkv', kd, v_new)
        return s, o

    s_fin, o = lax.scan(step, s0, (qdec, kdec, u, w, intra, glast))
    o = jnp.moveaxis(jnp.swapaxes(o, 2, 3), 0, 1).reshape(b, l, h, o.shape[-1])
    return o, s_fin


def delta_layer(x, conv_buf, s0, w_in, conv_w, a_log, dt_bias, norm_w, w_out, ln_g, ln_b):
    b, l, _ = x.shape
    proj = x @ w_in
    qkv, new_buf = causal_conv(proj[..., :CONV_DIM], conv_buf, conv_w)
    z = proj[..., CONV_DIM:CONV_DIM + A_VW].reshape(b, l, A_HEADS, A_DV)
    a_in = proj[..., CONV_DIM + A_VW:CONV_DIM + A_VW + A_HEADS].astype(jnp.float32)
    b_in = proj[..., CONV_DIM + A_VW + A_HEADS:].astype(jnp.float32)
    q = l2norm(qkv[..., :A_QK].reshape(b, l, A_HEADS, A_DK)) * (A_DK ** -0.5)
    k = l2norm(qkv[..., A_QK:2 * A_QK].reshape(b, l, A_HEADS, A_DK))
    v = qkv[..., 2 * A_QK:].reshape(b, l, A_HEADS, A_DV).astype(jnp.float32)
    beta = jax.nn.sigmoid(b_in)
    g = -jnp.exp(a_log.astype(jnp.float32)) * jax.nn.softplus(a_in + dt_bias.astype(jnp.float32))
    o, s_new = gated_delta_rule(q, k, v, g, beta, s0.astype(jnp.float32))
    o = o * lax.rsqrt(jnp.mean(o * o, -1, keepdims=True) + RMS_EPS) * norm_w * jax.nn.silu(z.astype(jnp.float32))
    out = o.reshape(b, l, A_VW).astype(x.dtype) @ w_out
    return layer_norm(DN_ALPHA * x + out, ln_g, ln_b), new_buf, s_new.astype(x.dtype)


def shared_kv(h, w_kv, pos):
    b, l, _ = h.shape
    kv = h @ w_kv
    k = rope_partial(kv[..., :B_KVW].reshape(b, l, B_KV_HEADS, B_HD), pos)
    v = kv[..., B_KVW:].reshape(b, l, B_KV_HEADS, B_HD)
    return k, v


def sink_softmax(s, sinks):
    sk = sinks.astype(jnp.float32)[..., None, None]
    m = jnp.maximum(s.max(-1, keepdims=True), sk)
    p = jnp.exp(s - m)
    return p / (p.sum(-1, keepdims=True) + jnp.exp(sk - m))


def banded_attention(q, k, v, sinks):
    b, l = q.shape[:2]
    n = l // CHUNK
    qb = q.reshape(b, n, CHUNK, B_KV_HEADS, B_GROUP, B_HD)

    def band(t):
        t = t.reshape(b, n, CHUNK, B_KV_HEADS, B_HD)
        t = jnp.pad(t, ((0, 0), (WINDOW_CHUNKS, 0), (0, 0), (0, 0), (0, 0)))
        return jnp.concatenate([t[:, j:j + n] for j in range(WINDOW_CHUNKS + 1)], axis=2)

    kb, vb = band(k), band(v)
    s = jnp.einsum('bnqkgd,bnskd->bnkgqs', qb, kb, preferred_element_type=jnp.float32)
    key_chunk = jnp.arange(n)[:, None] + (jnp.arange((WINDOW_CHUNKS + 1) * CHUNK) // CHUNK)[None, :] - WINDOW_CHUNKS
    s = jnp.where((key_chunk >= 0)[None, :, None, None, None, :], s, -jnp.inf)
    p = sink_softmax(s, sinks)
    o = jnp.einsum('bnkgqs,bnskd->bnqkgd', p.astype(v.dtype), vb)
    return o.reshape(b, l, B_QW)


def cached_attention(q, k, v, sinks):
    b, l = q.shape[:2]
    s = jnp.einsum('bqkgd,bskd->bkgqs', q, k, preferred_element_type=jnp.float32)
    p = sink_softmax(s, sinks)
    o = jnp.einsum('bkgqs,bskd->bqkgd', p.astype(v.dtype), v)
    return o.reshape(b, l, B_QW)


def swa_layer(x, k, v, pos, cached, w_in, sinks, w_out, ln_g, ln_b):
    b, l, _ = x.shape
    proj = x @ w_in
    q = rope_partial(proj[..., :B_QW].reshape(b, l, B_Q_HEADS, B_HD), pos) * (B_HD ** -0.5)
    q = q.reshape(b, l, B_KV_HEADS, B_GROUP, B_HD)
    z = proj[..., B_QW:]
    sk = sinks.reshape(B_KV_HEADS, B_GROUP)
    o = cached_attention(q, k, v, sk) if cached else banded_attention(q, k, v, sk)
    out = (o * jax.nn.silu(z)) @ w_out
    return layer_norm(DN_ALPHA * x + out, ln_g, ln_b)


def trunk(x, pos, conv_state, delta_state, past_k, past_v, a_w_in, a_conv_w, a_log, a_dt_bias, a_norm_w, a_w_out,
          a_ln_g, a_ln_b, b_w_kv, b_w_in, b_sinks, b_w_out, b_ln_g, b_ln_b):
    bsz = x.shape[0]
    cached = past_k is not None
    new_conv, new_delta = [], []
    for layer in range(DEPTH):
        if layer < N_A_LAYERS:
            i = layer
            if cached:
                cbuf, s0 = conv_state[i], delta_state[i]
            else:
                cbuf = jnp.zeros((bsz, CONV_W - 1, CONV_DIM), x.dtype)
                s0 = jnp.zeros((bsz, A_HEADS, A_DK, A_DV), jnp.float32)
            x, cb, st = delta_layer(x, cbuf, s0, a_w_in[i], a_conv_w[i], a_log[i], a_dt_bias[i], a_norm_w[i],
                                    a_w_out[i], a_ln_g[i], a_ln_b[i])
            new_conv.append(cb)
            new_delta.append(st)
            if layer == N_A_LAYERS - 1:
                k, v = shared_kv(x, b_w_kv, pos)
                if cached:
                    k = jnp.concatenate([past_k.astype(k.dtype), k], axis=1)
                    v = jnp.concatenate([past_v.astype(v.dtype), v], axis=1)
                new_k, new_v = k[:, -WINDOW:], v[:, -WINDOW:]
        else:
            j = layer - N_A_LAYERS
            x = swa_layer(x, k, v, pos, cached, b_w_in[j], b_sinks[j], b_w_out[j], b_ln_g[j], b_ln_b[j])
    return x, jnp.stack(new_conv), jnp.stack(new_delta), new_k, new_v


def setup_inputs(seed: int = 0) -> dict:
    key = jax.random.key(seed)
    ks = jax.random.split(key, 24)

    def nrm(k, shape, scale):
        return jax.random.normal(k, shape, jnp.float32) * scale

    cache_rows = min(WINDOW, PAST_LEN)
    dt = jnp.exp(jax.random.uniform(ks[9], (N_A_LAYERS, A_HEADS), jnp.float32,
                                    minval=math.log(1e-3), maxval=math.log(1e-1)))
    return {
        'x_prompt': nrm(ks[0], (BATCH, SEQ, D_MODEL), 1.0),
        'x_sample': nrm(ks[1], (DEC_BATCH, DEC_SEQ, D_MODEL), 1.0),
        'state_delta': nrm(ks[2], (N_A_LAYERS, DEC_BATCH, A_HEADS, A_DK, A_DV), A_DK ** -0.5),
        'state_conv': nrm(ks[3], (N_A_LAYERS, DEC_BATCH, CONV_W - 1, CONV_DIM), 1.0),
        'cache_k': nrm(ks[4], (DEC_BATCH, cache_rows, B_KV_HEADS, B_HD), 1.0),
        'cache_v': nrm(ks[5], (DEC_BATCH, cache_rows, B_KV_HEADS, B_HD), 1.0),
        'a_w_in': nrm(ks[6], (N_A_LAYERS, D_MODEL, A_IN), D_MODEL ** -0.5),
        'a_conv_w': nrm(ks[7], (N_A_LAYERS, CONV_W, CONV_DIM), CONV_W ** -0.5),
        'a_log': jnp.log(jax.random.uniform(ks[8], (N_A_LAYERS, A_HEADS), jnp.float32, minval=1.0, maxval=16.0)),
        'a_dt_bias': dt + jnp.log(-jnp.expm1(-dt)),
        'a_norm_w': 1.0 + nrm(ks[10], (N_A_LAYERS, A_DV), 0.02),
        'a_w_out': nrm(ks[11], (N_A_LAYERS, A_VW, D_MODEL), (A_VW ** -0.5) * DN_BETA),
        'a_ln_g': 1.0 + nrm(ks[12], (N_A_LAYERS, D_MODEL), 0.02),
        'a_ln_b': nrm(ks[13], (N_A_LAYERS, D_MODEL), 0.02),
        'b_w_kv': nrm(ks[14], (D_MODEL, 2 * B_KVW), D_MODEL ** -0.5),
        'b_w_in': nrm(ks[15], (N_B_LAYERS, D_MODEL, B_IN), D_MODEL ** -0.5),
        'b_sinks': nrm(ks[16], (N_B_LAYERS, B_Q_HEADS), 0.5),
        'b_w_out': nrm(ks[17], (N_B_LAYERS, B_QW, D_MODEL), (B_QW ** -0.5) * DN_BETA),
        'b_ln_g': 1.0 + nrm(ks[18], (N_B_LAYERS, D_MODEL), 0.02),
        'b_ln_b': nrm(ks[19], (N_B_LAYERS, D_MODEL), 0.02),
    }


def reference(x_prompt, x_sample, state_delta, state_conv, cache_k, cache_v, a_w_in, a_conv_w, a_log, a_dt_bias,
              a_norm_w, a_w_out, a_ln_g, a_ln_b, b_w_kv, b_w_in, b_sinks, b_w_out, b_ln_g, b_ln_b):
    pos_prompt = jnp.arange(x_prompt.shape[1], dtype=jnp.int32)
    pos_sample = PAST_LEN + jnp.arange(x_sample.shape[1], dtype=jnp.int32)
    y_prompt, p_conv, p_delta, p_k, p_v = trunk(
        x_prompt, pos_prompt, None, None, None, None, a_w_in, a_conv_w, a_log, a_dt_bias, a_norm_w, a_w_out,
        a_ln_g, a_ln_b, b_w_kv, b_w_in, b_sinks, b_w_out, b_ln_g, b_ln_b)
    y_sample, s_conv, s_delta, s_k, s_v = trunk(
        x_sample, pos_sample, state_conv, state_delta, cache_k, cache_v, a_w_in, a_conv_w, a_log, a_dt_bias,
        a_norm_w, a_w_out, a_ln_g, a_ln_b, b_w_kv, b_w_in, b_sinks, b_w_out, b_ln_g, b_ln_b)
    return (y_prompt, y_sample, p_delta, p_conv, p_k, p_v, s_delta, s_conv, s_k, s_v)
```

```python
import numpy as np
import ml_dtypes
from contextlib import ExitStack
import concourse.bass as bass
import concourse.mybir as mybir
from concourse.bass_utils import run_bass_kernel_spmd

F32 = mybir.dt.float32
BF16 = mybir.dt.bfloat16
F32R = mybir.dt.float32r
AF = mybir.ActivationFunctionType
ALU = mybir.AluOpType
AX = mybir.AxisListType

NCORES = 8
D = 1024
SEQ = 2048
NPT = 16
NT = 17
ROWS = SEQ + 64
AIN = 4112
DN_ALPHA = 8.0 ** 0.25
LN_EPS = 1e-5
RMS_EPS = 1e-6
NEG = -30000.0
PAST = 4096


class Trk:
    NDS = 20

    def __init__(self, nc, es):
        self.nc = nc
        self.eng = {'pe': nc.tensor, 'act': nc.scalar, 'dve': nc.vector, 'pool': nc.gpsimd, 'sp': nc.sync}
        self.semh = {}
        for e in ['pe', 'act', 'dve', 'pool']:
            self.semh[e] = es.enter_context(nc.semaphore("s_" + e))
        for i in range(self.NDS):
            self.semh[('d', i)] = es.enter_context(nc.semaphore("s_d%d" % i))
            self.semh[('w', i)] = es.enter_context(nc.semaphore("s_w%d" % i))
        self.cnt = {k: 0 for k in self.semh}
        self.seen = {e: {} for e in self.eng}
        self.clock = {}
        self.lastw = {}
        self.readers = {}
        self.dnext = {'sp': 0, 'pool': 0}
        self.ninstr = {e: 0 for e in self.eng}
        self.nwait = 0
        self.prog = {e: [] for e in self.eng}

    def _merge(self, e, stamp):
        s = self.seen[e]
        k, v = stamp
        if s.get(k, 0) < v:
            s[k] = v
        for kk, vv in self.clock.get(stamp, {}).items():
            if s.get(kk, 0) < vv:
                s[kk] = vv

    def _deps(self, e, reads, writes, extra=()):
        deps = {}

        def add(st):
            if st is None:
                return
            k, v = st
            if k == e and e == 'pe':
                return
            if deps.get(k, 0) < v:
                deps[k] = v
        for r in reads:
            add(self.lastw.get(r))
        for w in writes:
            add(self.lastw.get(w))
            for k, v in self.readers.get(w, {}).items():
                add((k, v))
        for st in extra:
            add(st)
        need = [(k, v) for k, v in deps.items() if self.seen[e].get(k, 0) < v]
        return need

    def _emit(self, e, fn, need):
        eng = self.eng[e]
        for (k, v) in need[:-1]:
            eng.wait_ge(self.semh[k], v)
            self.nwait += 1
        ins = fn()
        if need:
            k, v = need[-1]
            ins._wait_ge(self.semh[k], v)
        self.prog[e].append([list(need), None])
        for st in need:
            self._merge(e, st)
        self.ninstr[e] += 1
        return ins

    def _finish(self, e, stamp, reads, writes):
        self.clock[stamp] = dict(self.seen[e])
        for w in writes:
            self.lastw[w] = stamp
            self.readers[w] = {}
        for r in reads:
            d = self.readers.setdefault(r, {})
            k, v = stamp
            if d.get(k, 0) < v:
                d[k] = v

    def op(self, e, fn, reads=(), writes=()):
        psr = [r for r in reads if isinstance(r, tuple) and r[0] == 'ps' and r not in writes]
        if psr:
            writes = list(writes) + psr
        need = self._deps(e, reads, writes)
        ins = self._emit(e, fn, need)
        self.cnt[e] += 1
        ins.then_inc(self.semh[e], 1)
        self.prog[e][-1][1] = (e, 1)
        stamp = (e, self.cnt[e])
        self.seen[e][e] = self.cnt[e] if e == 'pe' else self.seen[e].get(e, 0)
        self._finish(e, stamp, reads, writes)
        return ins

    def dma(self, q, out, in_, reads=(), writes=(), **kw):
        key = ('d' if q == 'sp' else 'w', self.dnext[q])
        self.dnext[q] = (self.dnext[q] + 1) % self.NDS
        prev = (key, self.cnt[key]) if self.cnt[key] > 0 else None
        need = self._deps(q, reads, writes, extra=(prev,) if prev else ())
        ins = self._emit(q, lambda: self.eng[q].dma_start(out=out, in_=in_, **kw), need)
        self.cnt[key] += 16
        ins.then_inc(self.semh[key], 16)
        self.prog[q][-1][1] = (key, 16)
        stamp = (key, self.cnt[key])
        self._finish(q, stamp, reads, writes)
        return stamp

    def simulate(self):
        sem = {k: 0 for k in self.semh}
        pc = {e: 0 for e in self.eng}
        progress = True
        while progress:
            progress = False
            for e in self.eng:
                while pc[e] < len(self.prog[e]):
                    waits, inc = self.prog[e][pc[e]]
                    if all(sem[k] >= v for k, v in waits):
                        if inc:
                            sem[inc[0]] += inc[1]
                        pc[e] += 1
                        progress = True
                    else:
                        break
        stuck = {e: (pc[e], len(self.prog[e]), self.prog[e][pc[e]][0] if pc[e] < len(self.prog[e]) else None) for e in self.eng}
        return stuck, {str(k): v for k, v in sem.items() if v}

    def full_sync(self):
        for e in self.eng:
            for k in list(self.semh.keys()):
                v = self.cnt[k]
                if v > 0 and k != e and self.seen[e].get(k, 0) < v:
                    self.eng[e].wait_ge(self.semh[k], v)
                    self.prog[e].append([[(k, v)], None])
                    self.seen[e][k] = v

    def barrier(self):
        for e in self.eng:
            for k in list(self.semh.keys()):
                v = self.cnt[k]
                if v > 0 and k != e:
                    self.eng[e].wait_ge(self.semh[k], v)

    def wait_all(self, e, stamps):
        for (k, v) in stamps:
            if self.seen[e].get(k, 0) < v:
                self.eng[e].wait_ge(self.semh[k], v)
                self.seen[e][k] = v


class TileInfo:
    def __init__(self, idx):
        self.idx = idx
        self.samp = idx == NPT
        self.T = 64 if self.samp else 128
        self.nseq = 2 if self.samp else 1
        self.L = self.T // self.nseq
        self.C = 32 if self.samp else 64
        self.nch = self.T // self.C
        self.m = 1 if self.samp else 0


def host_consts():
    c = {}
    c['identf'] = np.eye(128, dtype=np.float32)
    c['onesf'] = np.ones((128, 128), dtype=np.float32)
    masks = np.zeros((2, 5, 128, 128), dtype=np.float32)
    for m, (T, C) in enumerate([(128, 64), (64, 32)]):
        p = np.arange(T)[:, None]
        f = np.arange(T)[None, :]
        same = (p // C) == (f // C)
        masks[m, 0, :T, :T] = np.where(same & (f >= p), 0.0, NEG)
        masks[m, 1, :T, :T] = np.where(same & (f < p), 0.0, NEG)
        masks[m, 2, :T, :T] = np.where(p != f, 1.0, 0.0)
        masks[m, 3, :T, :T] = np.where(same & (p <= f), 1.0, 0.0)
        masks[m, 4, :T, :T] = np.where(same, 1.0, 0.0)
    c['masks'] = masks.transpose(2, 0, 1, 3).copy()
    half = 8
    inv = 500000.0 ** (-np.arange(half, dtype=np.float32) * 2.0 / 16.0)
    pos = np.zeros((NT, 128), dtype=np.float32)
    for t in range(NPT):
        pos[t] = t * 128 + np.arange(128)
    pos[NPT, :64] = PAST + (np.arange(64) % 32)
    ang = pos[:, :, None].astype(np.float32) * inv[None, None, :].astype(np.float32)
    cs = np.stack([np.cos(ang), np.sin(ang)], axis=2).astype(np.float32)
    c['rope'] = cs.transpose(1, 0, 2, 3).copy()
    return c


WITH_B = True
import os as _os
HRATIO = int(_os.environ.get('HRATIO', '1'))
HPER = int(_os.environ.get('HPER', '1'))
SPLITLEV = int(_os.environ.get('SPLITLEV', '1'))
HSKIP = [int(x) for x in _os.environ.get('HSKIP', '').split(',') if x]


def build(stop_after=None, taps=()):
    nc = bass.Bass("TRN2", target_bir_lowering=False)
    es = ExitStack()
    K = Trk(nc, es)
    tapped = {}

    in_names = []
    build.in_names = in_names

    def din(name, shape, dt=F32):
        in_names.append(name)
        return nc.dram_tensor(name, list(shape), dt, kind="ExternalInput").ap()

    def dout(name, shape, dt=F32):
        return nc.dram_tensor(name, list(shape), dt, kind="ExternalOutput").ap()

    x_d = din("x", [ROWS, D])
    a_w_in = din("a_w_in", [2, D, AIN])
    a_w_out = din("a_w_out", [2, D, D])
    if WITH_B:
        b_w_kv = din("b_w_kv", [D, 512])
    if WITH_B:
        b_w_in = din("b_w_in", [2, D, 2048])
    if WITH_B:
        b_w_out = din("b_w_out", [2, D, D])
    convw_d = din("convw", [2, 128, 24, 4])
    alog_d = din("alog_bc", [2, 128, 8])
    dtb_d = din("dtb_bc", [2, 128, 8])
    normw_d = din("normw_bc", [2, 128, 128])
    lng_d = din("lng_bc", [4, 128, D])
    lnb_d = din("lnb_bc", [4, 128, D])
    if WITH_B:
        sink_d = din("sink_bc", [2, 128, 16])
    sdel_d = din("sdelta", [2, 2, 8, 128, 128])
    sconv_d = din("sconv", [2, 2, 128, 24, 3])
    if WITH_B:
        ck_d = din("ck", [2, 128, 256])
    if WITH_B:
        cv_d = din("cv", [2, 128, 256])
    identf_d = din("identf", [128, 128])
    onesf_d = din("onesf", [128, 128])
    masks_d = din("masks", [128, 2, 5, 128])
    if WITH_B:
        rope_d = din("rope", [128, NT, 2, 8])

    y_d = dout("y", [ROWS, D])
    pdel_d = dout("pdelta", [2, 8, 128, 128])
    sdelo_d = dout("sdelta_o", [2, 2, 8, 128, 128])
    pconv_d = dout("pconv", [2, 128, 24, 3])
    sconvo_d = dout("sconv_o", [2, 2, 128, 24, 3])
    if WITH_B:
        pk_d = dout("pk", [128, 256])
    if WITH_B:
        pv_d = dout("pv", [128, 256])
    if WITH_B:
        sk_d = dout("sk", [2, 128, 256])
    if WITH_B:
        sv_d = dout("sv", [2, 128, 256])
    tap_d = {}

    def sb(name, shape, dt=F32):
        return nc.alloc_sbuf_tensor("sb_" + name, list(shape), dt)

    xres = sb("xres", [128, NT, D])
    wbig = sb("wbig", [128, 8, AIN], BF16)
    wring = sb("wring", [128, 4, D], BF16)
    identf = sb("identf", [128, 128])
    identb = sb("identb", [128, 128], BF16)
    onesf = sb("onesf", [128, 128])
    masks = sb("masks", [128, 5, 128])
    lng = sb("lng", [128, D])
    lnb = sb("lnb", [128, D])
    xT = sb("xT", [128, 8, 128], BF16)
    ogT = xT
    og = sb("og", [128, D], BF16)
    on = sb("on", [128, 8, 128])
    res = on[:, :, :].rearrange("p h d -> p (h d)")
    bst = sb("bst", [128, 2, 6])
    mv = sb("mv", [128, 2])
    rstd = sb("rstd", [128, 1])
    Aq = sb("Aq", [128, 4, 128], F32R)
    BPq = sb("BPq", [128, 4, 256], F32R)
    ARENA = 10580
    arena = sb("arena", [128, ARENA])
    apos = [0]

    def ar(shape, dt=F32):
        n = int(np.prod(shape[1:]))
        nf = n if dt == F32 or dt == F32R else (n + 1) // 2
        nf = (nf + 7) // 8 * 8
        o = apos[0]
        apos[0] += nf
        assert apos[0] <= ARENA, ("arena overflow", apos[0])
        v = arena[:, o:o + nf]
        if dt != F32:
            v = v.bitcast(dt)
        v = v[:, 0:n]
        if len(shape) == 3:
            v = v.rearrange("p (a b) -> p a b", a=shape[1])
        elif len(shape) == 4:
            v = v.rearrange("p (a b c) -> p a b c", a=shape[1], b=shape[2])
        return v

    convw = ar([128, 24, 4])
    alog = ar([128, 8])
    nega = ar([128, 8])
    dtb = ar([128, 8])
    normw = ar([128, 128])
    carry = ar([128, 24 * 2 * 3])
    cb = [ar([128, 4 * 132])] * 2
    acc = ar([128, 4, 128])
    ebuf = ar([128, 4, 128])
    yfm = acc
    ssq = ar([128, 16])
    rn = ar([128, 16])
    sc_k = ar([128, 8])
    tmb = ar([128, 4, 128], BF16)
    qT2 = [ar([128, 8, 128], BF16) for _ in range(2)]
    kT2 = [ar([128, 8, 128], BF16) for _ in range(2)]
    kd2 = [ar([128, 8, 128], BF16) for _ in range(2)]
    vp2 = [ar([128, 8, 128], BF16) for _ in range(2)]
    gsm2 = [ar([128, 12, 8]) for _ in range(2)]
    glbc2 = [ar([128, 8, 2]) for _ in range(2)]
    REG = ar([128, 1024])
    DTi = [REG[:, i * 128:(i + 1) * 128] for i in (0, 1)]
    Ds = [REG[:, i * 128:(i + 1) * 128] for i in (2, 3)]
    DTs = [REG[:, i * 128:(i + 1) * 128] for i in (4, 5)]
    _rb = REG[:, :].bitcast(BF16)
    Rp = [_rb[:, i * 512:(i + 1) * 512].rearrange("p (h d) -> p h d", h=4) for i in (0, 1)]
    Yb = [_rb[:, i * 512:(i + 1) * 512].rearrange("p (h d) -> p h d", h=4) for i in (2, 3)]
    intraT = ar([128, 8, 128], BF16)
    XT = ar([128, 8, 128], BF16)
    S = ar([128, 8, 128])
    Sb = ar([128, 8, 128], BF16)
    osq = ebuf[:, 0, :]
    oss = ar([128, 8])
    print("arena used (A):", apos[0])

    PSALL = nc.alloc_psum_tensor("psall", [128, 4096], F32)

    class _Bank:
        def __init__(self, i, n=1):
            self.i, self.n = i, n

        def __getitem__(self, key):
            return PSALL[:, self.i * 512:(self.i + self.n) * 512][key]

    PS = [_Bank(i) for i in range(8)]
    psn = [0]

    reserved = set()

    def bank():
        for _ in range(16):
            i = psn[0]
            psn[0] = (i + 1) % 8
            if i not in reserved:
                return i
        raise RuntimeError("no free PSUM bank: reserved=%s" % sorted(reserved))

    def pk(i):
        return ('ps', i)

    def bank2():
        for _ in range(16):
            i = psn[0]
            if i + 1 < 8 and i not in reserved and (i + 1) not in reserved:
                psn[0] = (i + 2) % 8
                return i
            psn[0] = (i + 1) % 8
        raise RuntimeError("no free PSUM bank pair: reserved=%s" % sorted(reserved))

    def PS2(i):
        return _Bank(i, 2)

    out_stamps = []
    dbg_outs = {}

    def tap(name, ap_sb, keys):
        if name not in taps:
            return
        d = dout("tap_" + name, list(ap_sb.shape), F32 if ap_sb.dtype == F32R else ap_sb.dtype)
        src = ap_sb.bitcast(F32) if ap_sb.dtype == F32R else ap_sb
        out_stamps.append(K.dma('sp', d, src, reads=keys, writes=[('tap', name)]))

    K.dma('sp', identf[:], identf_d, writes=['identf'])
    K.dma('sp', onesf[:], onesf_d, writes=['onesf'])
    K.dma('pool', identb[:], identf_d, writes=['identb'])
    for t in range(NT):
        T = 64 if t == NPT else 128
        K.dma('sp', xres[:T, t, :], x_d[t * 128:t * 128 + T, :], writes=[('x', t)])

    def load_w(dst, src_ap, ncols, key, c0=0):
        for kc in range(8):
            K.dma('pool', dst[:, kc, c0:c0 + ncols], src_ap[kc * 128:(kc + 1) * 128, :], writes=[(key, kc)])

    XTK = [('xT', 0), ('xT', 1)]
    ONK = [('on', h) for h in range(8)]

    def make_xT(ti):
        T = ti.T
        for half in range(2):
            b = bank()
            for q in range(4):
                kc = half * 4 + q
                K.op('pe', lambda: nc.tensor.transpose(PS[b][:, q * T:(q + 1) * T], xres[:T, ti.idx, kc * 128:(kc + 1) * 128],
                                                       identf[:T, :T]),
                     reads=[('x', ti.idx), 'identf'], writes=[pk(b)])
            K.op('act', lambda: nc.scalar.copy(out=xT[:, half * 4:half * 4 + 4, :T],
                                               in_=PS[b][:, 0:4 * T].rearrange("p (a t) -> p a t", a=4)),
                 reads=[pk(b)], writes=[('xT', half)])

    pending_tail = []

    def tail_step():
        if pending_tail:
            pending_tail.pop(0)()

    def tail_flush():
        while pending_tail:
            pending_tail.pop(0)()

    wr_state = {'use': 0, 'iss': 0}
    WOUT_SRC = [a_w_out[0], a_w_out[1]] + ([b_w_out[0], b_w_out[1]] if WITH_B else [])

    def wr_issue():
        c = wr_state['iss']
        if c >= len(WOUT_SRC) * NT * 8:
            return
        wr_state['iss'] += 1
        lay = c // (NT * 8)
        kc = c % 8
        K.dma('pool', wring[:, c % 4, :], WOUT_SRC[lay][kc * 128:(kc + 1) * 128, :], writes=[('wr', c % 4)])

    for _ in range(4):
        wr_issue()

    def out_proj(ti, final, ogT_ap, ogk, immediate=0):
        T = ti.T
        t = ti.idx
        st = {}

        def s1():
            b = bank()
            psb = PS[b][:, :].bitcast(BF16)
            for kc in range(8):
                K.op('pe', lambda: nc.tensor.transpose(psb[:, kc * T:(kc + 1) * T], og[:T, kc * 128:(kc + 1) * 128], identb[:T, :T]),
                     reads=['og', 'identb'], writes=[pk(b)])
            K.op('act', lambda: nc.scalar.copy(out=ogT_ap[:, :, :T], in_=psb[:, 0:8 * T].rearrange("p (a t) -> p a t", a=8)),
                 reads=[pk(b)], writes=ogk)

        def s2half(kcs):
            if 'pb' not in st:
                st['pb'] = [bank(), bank()]
                reserved.update(st['pb'])
            pb = st['pb']
            for kc in kcs:
                c = wr_state['use']
                wr_state['use'] += 1
                slot = c % 4
                for h2 in range(2):
                    K.op('pe', lambda: nc.tensor.matmul(PS[pb[h2]][:T, :], ogT_ap[:, kc, :T], wring[:, slot, h2 * 512:(h2 + 1) * 512],
                                                        start=(kc == 0), stop=(kc == 7)),
                         reads=ogk + [('wr', slot)], writes=[pk(pb[h2])])
            for _ in kcs:
                wr_issue()

        def s2a():
            s2half(range(0, 4))

        def s2b():
            s2half(range(4, 8))

        def s3():
            pb = st['pb']
            for h2 in range(2):
                rk = ONK[h2 * 4:h2 * 4 + 4]
                sl = slice(h2 * 512, (h2 + 1) * 512)
                K.op('dve', lambda: nc.vector.scalar_tensor_tensor(out=res[:T, sl], in0=xres[:T, t, sl], scalar=DN_ALPHA,
                                                                   in1=PS[pb[h2]][:T, :], op0=ALU.mult, op1=ALU.add),
                     reads=[('x', t), pk(pb[h2])], writes=rk)
                K.op('dve', lambda: nc.vector.bn_stats(out=bst[:T, h2, :], in_=res[:T, sl]), reads=rk, writes=[('bst', h2)])
            K.op('dve', lambda: nc.vector.bn_aggr(out=mv[:T, :], in_=bst[:T, :, :]), reads=[('bst', 0), ('bst', 1)], writes=['mv'])
            K.op('act', lambda: nc.scalar.activation(out=rstd[:T, :], in_=mv[:T, 1:2], func=AF.Ln, bias=epsln[:T, 0:1], scale=1.0),
                 reads=['mv', 'eps'], writes=['rstd'])
            K.op('act', lambda: nc.scalar.activation(out=rstd[:T, :], in_=rstd[:T, :], func=AF.Exp, scale=-0.5),
                 reads=['rstd'], writes=['rstd'])
            reserved.discard(st['pb'][0])
            reserved.discard(st['pb'][1])

        def s4():
            for h2 in range(2):
                rk = ONK[h2 * 4:h2 * 4 + 4]
                sl = slice(h2 * 512, (h2 + 1) * 512)
                K.op('dve', lambda: nc.vector.tensor_scalar(out=res[:T, sl], in0=res[:T, sl], scalar1=mv[:T, 0:1], scalar2=rstd[:T, 0:1],
                                                            op0=ALU.subtract, op1=ALU.mult),
                     reads=rk + ['mv', 'rstd'], writes=rk)
                K.op('pool', lambda: nc.gpsimd.tensor_tensor(out=res[:T, sl], in0=res[:T, sl], in1=lng[:T, sl], op=ALU.mult),
                     reads=rk + ['lng'], writes=rk)

        def s5():
            for h2 in range(2):
                rk = ONK[h2 * 4:h2 * 4 + 4]
                sl = slice(h2 * 512, (h2 + 1) * 512)
                K.op('pool', lambda: nc.gpsimd.tensor_tensor(out=xres[:T, t, sl], in0=res[:T, sl], in1=lnb[:T, sl], op=ALU.add),
                     reads=rk + ['lnb'], writes=[('x', t)])
            if final:
                out_stamps.append(K.dma('sp', y_d[t * 128:t * 128 + T, :], xres[:T, t, :], reads=[('x', t)], writes=[('y', t)]))
        stages = [s1, s2a, s2b, s3, s4, s5]
        for f_ in stages[:immediate]:
            f_()
        pending_tail.extend(stages[immediate:])

    def silu_from(out_ap, in_ap, tmp_ap, rkeys, wkeys, tkeys):
        P_ = tmp_ap.shape[0]
        K.op('act', lambda: nc.scalar.activation(out=tmp_ap, in_=in_ap, func=AF.Exp, scale=-1.0), reads=rkeys, writes=tkeys)
        K.op('act', lambda: nc.scalar.activation(out=tmp_ap, in_=tmp_ap, func=AF.Ln, bias=epsln[:P_, 2:3], scale=1.0), reads=tkeys + ['eps'], writes=tkeys)
        K.op('act', lambda: nc.scalar.activation(out=tmp_ap, in_=tmp_ap, func=AF.Exp, scale=-1.0), reads=tkeys, writes=tkeys)
        K.op('dve', lambda: nc.vector.tensor_tensor(out=out_ap, in0=in_ap, in1=tmp_ap, op=ALU.mult), reads=rkeys + tkeys, writes=wkeys)

    epsln = sb("epsln", [128, 4])
    K.op('pool', lambda: nc.gpsimd.memset(epsln[:, 0:1], LN_EPS), writes=['eps'])
    K.op('pool', lambda: nc.gpsimd.memset(epsln[:, 1:2], RMS_EPS), writes=['eps'])
    K.op('pool', lambda: nc.gpsimd.memset(epsln[:, 2:3], 1.0), writes=['eps'])
    K.op('pool', lambda: nc.gpsimd.memset(epsln[:, 3:4], float(np.log(128.0 ** -0.5))), writes=['eps'])
    EPS_LN, EPS_RMS, ONE_B, LOGQ = epsln[:, 0:1], epsln[:, 1:2], epsln[:, 2:3], epsln[:, 3:4]

    def a_layer(l):
        for (bname, c0, c1) in [('g', 4096, 4112), ('q0', 0, 512), ('q1', 512, 1024), ('k', 1024, 2048), ('v', 2048, 3072), ('z', 3072, 4096)]:
            K.dma('pool', wbig[:, :, c0:c1], a_w_in[l][:, c0:c1].rearrange("(kc p) n -> p kc n", p=128), writes=[('wbig', bname)])
        K.dma('sp', masks[:], masks_d[:, 0], writes=['masks'])
        K.dma('sp', convw, convw_d[l], writes=['convw'])
        K.dma('sp', alog, alog_d[l], writes=['alog'])
        K.dma('sp', dtb, dtb_d[l], writes=['dtb'])
        K.dma('sp', normw, normw_d[l], writes=['normw'])
        K.dma('sp', lng[:], lng_d[l], writes=['lng'])
        K.dma('sp', lnb[:], lnb_d[l], writes=['lnb'])
        K.op('act', lambda: nc.scalar.activation(out=nega, in_=alog, func=AF.Exp), reads=['alog'], writes=['nega'])
        K.op('dve', lambda: nc.vector.tensor_scalar(out=nega, in0=nega, scalar1=-1.0, scalar2=None, op0=ALU.mult),
             reads=['nega'], writes=['nega'])
        K.op('pool', lambda: nc.gpsimd.memset(carry, 0.0), writes=['carry'])
        K.op('pool', lambda: nc.gpsimd.memset(S, 0.0), writes=[('S', 0), ('S', 1)])
        K.op('pool', lambda: nc.gpsimd.memset(Sb, 0.0), writes=[('Sb', 0), ('Sb', 1)])
        gens = [a_tile(l, TileInfo(t), t % 2) for t in range(NT)]

        def adv(g, until):
            while True:
                v = next(g)
                if v == until:
                    return

        adv(gens[0], 'Qdone')
        adv(gens[0], 'EZdone')
        npt_ = stop_after[1] if isinstance(stop_after, tuple) else NPT
        for t in range(npt_):
            g, gn = gens[t], (gens[t + 1] if t + 1 < npt_ else None)
            hdone = False
            cnt_ = [0]
            if gn is not None:
                adv(gn, 'F')
                while True:
                    v = next(gn)
                    if v == 'Qdone':
                        break
                    cnt_[0] += 1
                    if cnt_[0] % HRATIO == 0 and (cnt_[0] % 5) not in HSKIP:
                        for _ in range(HPER):
                            if not hdone and next(g) == 'Hdone':
                                hdone = True
            while not hdone:
                hdone = next(g) == 'Hdone'
            adv(g, 'Sdone')
            if gn is not None:
                adv(gn, 'EZdone')
            tail_step()
            tail_step()
        if isinstance(stop_after, tuple):
            tail_flush()
            return True
        gs = gens[NPT]
        adv(gs, 'Sdone')
        tail_flush()
        return False

    def a_tile(l, ti, p):
        T, nseq, L, C, nch, t = ti.T, ti.nseq, ti.L, ti.C, ti.nch, ti.idx
        W = 3 + L
        qT, kT, kd, vp, gsm, glbc = qT2[p], kT2[p], kd2[p], vp2[p], gsm2[p], glbc2[p]
        if ti.samp:
            K.dma('sp', masks[:], masks_d[:, 1], writes=['masks'])
            cview = carry[:, 0:24 * 2 * 3].rearrange("p (c s j) -> p c s j", c=24, s=2)
            for s_ in range(2):
                K.dma('sp', cview[:, :, s_, :], sconv_d[l, s_], writes=['carry'])
        K.op('pool', lambda: nc.gpsimd.memset(ssq, 0.0), writes=[('ssq', g, c_) for g in range(4) for c_ in range(4)])
        make_xT(ti)
        cv4 = carry[:, 0:24 * nseq * 3].rearrange("p (c s j) -> p c s j", c=24, s=nseq)

        bg = bank()
        for kc in range(8):
            K.op('pe', lambda: nc.tensor.matmul(PS[bg][:T, 0:16], xT[:, kc, :T], wbig[:, kc, 4096:4112], start=(kc == 0), stop=(kc == 7)),
                 reads=XTK + [('wbig', 'g')], writes=[pk(bg)])
        G_ = lambda i: gsm[:T, i, :]
        K.op('dve', lambda: nc.vector.tensor_tensor(out=G_(0), in0=PS[bg][:T, 0:8], in1=dtb[:T, :], op=ALU.add),
             reads=[pk(bg), 'dtb'], writes=[('g', p, 0)])
        K.op('act', lambda: nc.scalar.activation(out=G_(0), in_=G_(0), func=AF.Exp), reads=[('g', p, 0)], writes=[('g', p, 0)])
        K.op('act', lambda: nc.scalar.activation(out=G_(0), in_=G_(0), func=AF.Ln, bias=ONE_B[:T, :], scale=1.0), reads=[('g', p, 0), 'eps'], writes=[('g', p, 0)])
        K.op('dve', lambda: nc.vector.tensor_tensor(out=G_(1), in0=G_(0), in1=nega[:T, :], op=ALU.mult), reads=[('g', p, 0), 'nega'], writes=[('g', p, 1)])
        K.op('act', lambda: nc.scalar.activation(out=G_(2), in_=PS[bg][:T, 8:16], func=AF.Exp, scale=-1.0), reads=[pk(bg)], writes=[('g', p, 2)])
        K.op('act', lambda: nc.scalar.activation(out=G_(2), in_=G_(2), func=AF.Ln, bias=ONE_B[:T, :], scale=1.0), reads=[('g', p, 2), 'eps'], writes=[('g', p, 2)])
        K.op('act', lambda: nc.scalar.activation(out=G_(4), in_=G_(2), func=AF.Exp, scale=-0.5), reads=[('g', p, 2)], writes=[('g', p, 4)])
        bG = bank()
        K.op('pe', lambda: nc.tensor.matmul(PS[bG][:T, 0:8], masks[:T, 3, :T], G_(1), start=True, stop=True),
             reads=['masks', ('g', p, 1)], writes=[pk(bG)])
        K.op('pe', lambda: nc.tensor.matmul(PS[bG][:T, 8:16], masks[:T, 4, :T], G_(1), start=True, stop=True),
             reads=['masks', ('g', p, 1)], writes=[pk(bG)])
        K.op('act', lambda: nc.scalar.copy(out=G_(5), in_=PS[bG][:T, 0:8]), reads=[pk(bG)], writes=[('g', p, 5)])
        K.op('dve', lambda: nc.vector.tensor_tensor(out=G_(6), in0=PS[bG][:T, 8:16], in1=G_(5), op=ALU.subtract),
             reads=[pk(bG), ('g', p, 5)], writes=[('g', p, 6)])
        K.op('act', lambda: nc.scalar.activation(out=G_(6), in_=G_(6), func=AF.Exp), reads=[('g', p, 6)], writes=[('g', p, 6)])
        K.op('act', lambda: nc.scalar.activation(out=G_(7), in_=G_(5), func=AF.Exp), reads=[('g', p, 5)], writes=[('g', p, 7)])
        K.op('dve', lambda: nc.vector.tensor_scalar(out=G_(8), in0=G_(7), scalar1=-1.0, scalar2=None, op0=ALU.mult),
             reads=[('g', p, 7)], writes=[('g', p, 8)])
        yield 'F'
        qkv_banks = {}

        def qkv_proj(grp):
            b = bank()
            reserved.add(b)
            qkv_banks[grp] = b
            for c4 in range(4):
                ct = grp * 4 + c4
                for kc in range(8):
                    K.op('pe', lambda: nc.tensor.matmul(PS[b][:, c4 * T:(c4 + 1) * T], wbig[:, kc, ct * 128:(ct + 1) * 128], xT[:, kc, :T],
                                                        start=(kc == 0), stop=(kc == 7)),
                         reads=XTK + [('wbig', ['q0', 'q1', 'k', 'k', 'v', 'v'][grp])], writes=[pk(b)])
        cbk = 'cb'
        cbv = cb[0][:, 0:4 * nseq * W].rearrange("p (c s w) -> p c s w", c=4, s=nseq)
        acck = [('acc', i) for i in range(4)]
        junk = tmb[:, :, :].rearrange("p a b -> p (a b)").bitcast(F32)

        def st1(grp):
            b = qkv_banks[grp]
            K.op('pool', lambda: nc.gpsimd.tensor_copy(out=cbv[:, :, :, 0:3], in_=cv4[:, grp * 4:grp * 4 + 4, :, :]),
                 reads=['carry'], writes=[cbk])
            K.op('act', lambda: nc.scalar.copy(out=cbv[:, :, :, 3:3 + L],
                                               in_=PS[b][:, 0:4 * T].rearrange("p (c s w) -> p c s w", c=4, s=nseq)),
                 reads=[pk(b)], writes=[cbk])
            reserved.discard(b)
            K.op('pool', lambda: nc.gpsimd.tensor_copy(out=cv4[:, grp * 4:grp * 4 + 4, :, :], in_=cbv[:, :, :, L:L + 3]),
                 reads=[cbk], writes=['carry'])

        def st2(grp):
            avs = [acc[:, c4, 0:T].rearrange("p (s w) -> p s w", s=nseq) for c4 in range(4)]
            for c4 in range(4):
                ct = grp * 4 + c4
                K.op('dve', lambda: nc.vector.tensor_scalar(out=avs[c4], in0=cbv[:, c4, :, 0:L], scalar1=convw[:, ct, 0:1], scalar2=None, op0=ALU.mult),
                     reads=[cbk, 'convw'], writes=[('acc', c4)])
            for j in range(1, 4):
                for c4 in range(4):
                    ct = grp * 4 + c4
                    K.op('dve', lambda: nc.vector.scalar_tensor_tensor(out=avs[c4], in0=cbv[:, c4, :, j:j + L], scalar=convw[:, ct, j:j + 1], in1=avs[c4],
                                                                       op0=ALU.mult, op1=ALU.add),
                         reads=[cbk, 'convw', ('acc', c4)], writes=[('acc', c4)])

        def st3(grp):
            silu_from(ebuf[:, :, :T], acc[:, :, :T], ebuf[:, :, :T], acck, ['ebuf'], ['ebuf'])

        st4 = {}
        st4c_b = {}

        def st4c(grp):
            if grp < 0 or grp >= 4:
                return
            b3 = bank()
            reserved.add(b3)
            st4c_b[grp] = b3
            psb = PS[b3][:, :].bitcast(BF16)
            for c4 in range(4):
                K.op('pe', lambda: nc.tensor.transpose(psb[:, c4 * T:(c4 + 1) * T], tmb[:T, c4, :], identb[:T, :T]),
                     reads=['tmb', 'identb'], writes=[pk(b3)])

        def st4d(grp):
            if grp < 0 or grp >= 4:
                return
            b3 = st4c_b[grp]
            psb = PS[b3][:, :].bitcast(BF16)
            isq = grp < 2
            h0 = (grp % 2) * 4
            dst = qT if isq else kT
            dk_ = ('qT' if isq else 'kT', p, grp % 2)
            K.op('act', lambda: nc.scalar.copy(out=dst[:, h0:h0 + 4, :T], in_=psb[:, 0:4 * T].rearrange("p (a t) -> p a t", a=4)),
                 reads=[pk(b3)], writes=[dk_])
            reserved.discard(b3)

        def st4a(grp):
            b2 = bank()
            reserved.add(b2)
            st4[grp] = b2
            for c4 in range(4):
                K.op('pe', lambda: nc.tensor.transpose(PS[b2][:T, c4 * 128:(c4 + 1) * 128], ebuf[:, c4, :T], identf[:, :]),
                     reads=['ebuf', 'identf'], writes=[pk(b2)])
            h0 = (grp % 2) * 4
            if grp < 4:
                isq = grp < 2
                col0 = (0 if isq else 8) + h0
                for c4 in range(4):
                    K.op('act', lambda: nc.scalar.activation(out=junk[:T, (c4 % 2) * 128:(c4 % 2) * 128 + 128], in_=PS[b2][:T, c4 * 128:(c4 + 1) * 128], func=AF.Square,
                                                             accum_out=ssq[:T, col0 + c4:col0 + c4 + 1]),
                         reads=[pk(b2)], writes=[('ssq', grp, c4), ('junk', c4 % 2)] + (['tmb'] if c4 < 2 else []))
                K.op('act', lambda: nc.scalar.activation(out=rn[:T, col0:col0 + 4], in_=ssq[:T, col0:col0 + 4], func=AF.Ln, bias=EPS_RMS[:T, :], scale=1.0),
                     reads=[('ssq', grp, c_) for c_ in range(4)] + ['eps'], writes=[('rn', grp)])
                if isq:
                    K.op('act', lambda: nc.scalar.activation(out=rn[:T, col0:col0 + 4], in_=rn[:T, col0:col0 + 4], func=AF.Exp, scale=-0.5,
                                                             bias=LOGQ[:T, :]),
                         reads=[('rn', grp), 'eps'], writes=[('rn', grp)])
                else:
                    K.op('act', lambda: nc.scalar.activation(out=rn[:T, col0:col0 + 4], in_=rn[:T, col0:col0 + 4], func=AF.Exp, scale=-0.5),
                         reads=[('rn', grp)], writes=[('rn', grp)])

        def st4b(grp):
            b2 = st4[grp]
            h0 = (grp % 2) * 4
            pv3 = PS[b2][:T, :].rearrange("p (h d) -> p h d", h=4)
            if grp < 4:
                isq = grp < 2
                col0 = (0 if isq else 8) + h0
                if isq:
                    scl = rn[:T, col0:col0 + 4]
                    sk_ = [('rn', grp)]
                else:
                    K.op('dve', lambda: nc.vector.tensor_tensor(out=sc_k[:T, h0:h0 + 4], in0=rn[:T, col0:col0 + 4], in1=gsm[:T, 4, h0:h0 + 4], op=ALU.mult),
                         reads=[('rn', grp), ('g', p, 4)], writes=[('sck', grp)])
                    scl = sc_k[:T, h0:h0 + 4]
                    sk_ = [('sck', grp)]
                K.op('dve', lambda: nc.vector.tensor_tensor(out=tmb[:T, :, :], in0=pv3, in1=scl.unsqueeze(2).broadcast_to([T, 4, 128]), op=ALU.mult),
                     reads=[pk(b2)] + sk_, writes=['tmb'])
                reserved.discard(b2)
                if not isq:
                    K.op('pool', lambda: nc.gpsimd.tensor_tensor(out=kd[:T, h0:h0 + 4, :], in0=tmb[:T, :, :],
                                                                 in1=gsm[:T, 6, h0:h0 + 4].unsqueeze(2).broadcast_to([T, 4, 128]), op=ALU.mult),
                         reads=['tmb', ('g', p, 6)], writes=[('kd', p, grp % 2)])
            else:
                K.op('dve', lambda: nc.vector.tensor_tensor(out=vp[:T, h0:h0 + 4, :], in0=pv3,
                                                            in1=gsm[:T, 4, h0:h0 + 4].unsqueeze(2).broadcast_to([T, 4, 128]), op=ALU.mult),
                     reads=[pk(b2), ('g', p, 4)], writes=[('vp', p, grp % 2)])
                reserved.discard(b2)

        qkv_proj(0)
        qkv_proj(1)
        st1(0)
        st2(0)
        for grp in range(6):
            st4c(grp - 1)
            if grp + 2 < 6:
                qkv_proj(grp + 2)
            if grp + 1 < 6:
                st1(grp + 1)
            yield 'Qit'
            st3(grp)
            st4d(grp - 1)
            yield 'Qit'
            st4a(grp)
            yield 'Qit'
            if grp + 1 < 6:
                st2(grp + 1)
            yield 'Qit'
            st4b(grp)
            tail_step()
            yield 'Qit'
        if ti.idx == NPT - 1:
            out_stamps.append(K.dma('sp', pconv_d[l], cv4[:, :, 0, :], reads=['carry'], writes=[('pconv', l)]))
        if ti.samp:
            for s_ in range(2):
                out_stamps.append(K.dma('sp', sconvo_d[l, s_], cv4[:, :, s_, :], reads=['carry'], writes=[('sconvo', l, s_)]))
        tail_flush()
        yield 'Qdone'
        cbf = cb[0]
        for h2 in range(2):
            b = bank()
            for kc in range(8):
                K.op('pe', lambda: nc.tensor.matmul(PS[b][:T, :], xT[:, kc, :T], wbig[:, kc, 3072 + h2 * 512:3072 + (h2 + 1) * 512],
                                                    start=(kc == 0), stop=(kc == 7)),
                     reads=XTK + [('wbig', 'z')], writes=[pk(b)])
            silu_from(og[:T, h2 * 512:(h2 + 1) * 512], PS[b][:T, :], cbf[:T, 0:512], [pk(b)], ['og'], ['cb'])
        yield 'EZdone'
        nlev = 5 if C == 64 else 4
        first_reg = [True]

        def regkeys():
            if first_reg[0]:
                first_reg[0] = False
                return [], ['REG']
            return ['REG'], []
        DTi4 = REG[:, 0:512].rearrange("p (h c) -> p h c", h=4)
        Ds4 = REG[:, 512:1024].rearrange("p (h c) -> p h c", h=4)
        hstate = {}
        AK = [('Aq', i) for i in range(4)]
        BK = [('Bq', i) for i in range(4)]
        PK = [('Pq', i) for i in range(4)]

        def h_prep(gq):
            hs = slice(gq * 4, gq * 4 + 4)
            rr, rw = regkeys()
            K.op('dve', lambda: nc.vector.tensor_tensor(out=DTi4[:T, :, :T], in0=identf[:T, :T].unsqueeze(1).broadcast_to([T, 4, T]),
                                                        in1=gsm[:T, 5, hs].unsqueeze(2).broadcast_to([T, 4, T]), op=ALU.mult),
                 reads=['identf', ('g', p, 5)] + rr, writes=['DTi'] + rw)
            bgq = bank()
            reserved.add(bgq)
            K.op('pe', lambda: nc.tensor.matmul(PS[bgq][:, 0:4 * T], onesf[:T, :], DTi4[:T, :, :T], start=True, stop=True),
                 reads=['onesf', 'DTi', 'REG'], writes=[pk(bgq)])
            gv = PS[bgq][:, 0:4 * T].rearrange("p (h c j) -> p h c j", h=4, c=nch)
            K.op('act', lambda: nc.scalar.activation(out=glbc[:, hs, :nch], in_=gv[:, :, :, C - 1], func=AF.Exp),
                 reads=[pk(bgq)], writes=[('glbc', p, gq)])
            gps = PS[bgq][:T, 0:4 * T].rearrange("p (h c) -> p h c", h=4)
            K.op('dve', lambda: nc.vector.tensor_tensor(out=DTi4[:T, :, :T], in0=gps, in1=gsm[:T, 5, hs].unsqueeze(2).broadcast_to([T, 4, T]), op=ALU.subtract),
                 reads=[pk(bgq), ('g', p, 5), 'REG'], writes=['DTi'])
            reserved.discard(bgq)
            K.op('dve', lambda: nc.vector.tensor_tensor(out=Ds4[:T, :, :T], in0=DTi4[:T, :, :T], in1=masks[:T, 1, :T].unsqueeze(1).broadcast_to([T, 4, T]), op=ALU.subtract),
                 reads=['DTi', 'masks', 'REG'], writes=['Ds'])
            K.op('dve', lambda: nc.vector.tensor_tensor(out=DTi4[:T, :, :T], in0=DTi4[:T, :, :T], in1=masks[:T, 0, :T].unsqueeze(1).broadcast_to([T, 4, T]), op=ALU.add),
                 reads=['DTi', 'masks', 'REG'], writes=['DTi'])
            K.op('act', lambda: nc.scalar.activation(out=Ds4[:T, :, :T], in_=Ds4[:T, :, :T], func=AF.Exp, scale=-1.0), reads=['Ds', 'REG'], writes=['Ds'])
            K.op('act', lambda: nc.scalar.activation(out=DTi4[:T, :, :T], in_=DTi4[:T, :, :T], func=AF.Exp), reads=['DTi', 'REG'], writes=['DTi'])
            bq = bank2()
            for h4 in range(4):
                h = gq * 4 + h4
                K.op('pe', lambda: nc.tensor.matmul(PS2(bq)[:T, h4 * 256:h4 * 256 + T], kT[:, h, :T], kT[:, h, :T], start=True, stop=True),
                     reads=[('kT', p, gq)], writes=[pk(bq), pk(bq + 1)])
                K.op('pe', lambda: nc.tensor.matmul(PS2(bq)[:T, h4 * 256 + T:h4 * 256 + 2 * T], kT[:, h, :T], qT[:, h, :T], start=True, stop=True),
                     reads=[('kT', p, gq), ('qT', p, gq)], writes=[pk(bq), pk(bq + 1)])
            pq = PS2(bq)[:T, :].rearrange("p (h c) -> p h c", h=4)
            K.op('dve', lambda: nc.vector.tensor_tensor(out=intraT[:T, hs, :T], in0=pq[:, :, T:2 * T], in1=DTi4[:T, :, :T], op=ALU.mult),
                 reads=[pk(bq), pk(bq + 1), 'DTi', 'REG'], writes=[('intraT', h_) for h_ in range(gq * 4, gq * 4 + 4)])
            hstate[gq] = (bq, pq)
            reserved.update([bq, bq + 1])

        def h_fin(gq):
            bq, pq = hstate[gq]
            K.op('dve', lambda: nc.vector.scalar_tensor_tensor(out=Aq[:T, :, :T], in0=pq[:, :, 0:T], scalar=-1.0, in1=Ds4[:T, :, :T], op0=ALU.mult, op1=ALU.mult),
                 reads=[pk(bq), pk(bq + 1), 'Ds', 'REG'], writes=AK)
            K.op('dve', lambda: nc.vector.scalar_tensor_tensor(out=BPq[:T, :, 0:T], in0=pq[:, :, 0:T], scalar=-1.0, in1=DTi4[:T, :, :T], op0=ALU.mult, op1=ALU.mult),
                 reads=[pk(bq), pk(bq + 1), 'DTi', 'REG'], writes=BK)
            reserved.discard(bq)
            reserved.discard(bq + 1)
            K.op('pool', lambda: nc.gpsimd.tensor_tensor(out=BPq[:T, :, 0:T], in0=BPq[:T, :, 0:T].bitcast(F32), in1=masks[:T, 2, :T].unsqueeze(1).broadcast_to([T, 4, T]), op=ALU.mult),
                 reads=BK + ['masks'], writes=BK)
            K.op('pool', lambda: nc.gpsimd.tensor_tensor(out=BPq[:T, :, T:2 * T], in0=BPq[:T, :, 0:T].bitcast(F32), in1=identf[:T, :T].unsqueeze(1).broadcast_to([T, 4, T]), op=ALU.add),
                 reads=BK + ['identf'], writes=PK)

        def h_chain(gq, hook=None):
            for lev in range(nlev + 1):
                last = lev == nlev
                if lev == 1 and hook is not None:
                    hook()
                if not last:
                    b2 = bank2()
                    b3 = bank()
                    for h4 in range(4):
                        ncols = T if lev == 0 else 2 * T
                        K.op('pe', lambda: nc.tensor.matmul(PS2(b2)[:T, h4 * 256:h4 * 256 + ncols], Aq[:T, h4, :T], BPq[:T, h4, 0:ncols], start=True, stop=True),
                             reads=[AK[h4], BK[h4]] + ([PK[h4]] if lev > 0 else []), writes=[pk(b2), pk(b2 + 1)])
                        K.op('pe', lambda: nc.tensor.matmul(PS[b3][:T, h4 * T:(h4 + 1) * T], BPq[:T, h4, 0:T], Aq[:T, h4, :T], start=True, stop=True),
                             reads=[AK[h4], BK[h4]], writes=[pk(b3)])
                    if SPLITLEV:
                        reserved.update([b2, b2 + 1, b3])
                        yield 'lev'
                        reserved.difference_update([b2, b2 + 1, b3])
                    pv = PS2(b2)[:T, :].rearrange("p (h c) -> p h c", h=4)
                    if lev > 0:
                        K.op('dve', lambda: nc.vector.tensor_tensor(out=BPq[:T, :, T:2 * T], in0=pv[:, :, T:2 * T], in1=BPq[:T, :, T:2 * T].bitcast(F32), op=ALU.add),
                             reads=[pk(b2), pk(b2 + 1)] + PK, writes=PK)
                    K.op('act', lambda: nc.scalar.copy(out=BPq[:T, :, 0:T], in_=pv[:, :, 0:T]), reads=[pk(b2), pk(b2 + 1)], writes=BK)
                    K.op('act', lambda: nc.scalar.copy(out=Aq[:T, :, :T], in_=PS[b3][:T, 0:4 * T].rearrange("p (h c) -> p h c", h=4)),
                         reads=[pk(b3)], writes=AK)
                    yield 'lev'
                else:
                    b3 = bank()
                    for h4 in range(4):
                        K.op('pe', lambda: nc.tensor.matmul(PS[b3][:T, h4 * T:(h4 + 1) * T], Aq[:T, h4, :T], BPq[:T, h4, T:2 * T], start=True, stop=True),
                             reads=[AK[h4], PK[h4]], writes=[pk(b3)])
                    K.op('dve', lambda: nc.vector.tensor_tensor(out=XT[:T, gq * 4:gq * 4 + 4, :T], in0=PS[b3][:T, 0:4 * T].rearrange("p (h c) -> p h c", h=4),
                                                                in1=BPq[:T, :, T:2 * T].bitcast(F32), op=ALU.add),
                         reads=[pk(b3)] + PK, writes=[('XT', gq)])
                    yield 'lev'

        h_prep(0)
        yield 'H'
        h_fin(0)
        yield 'H'
        for _ in h_chain(0, hook=lambda: h_prep(1)):
            yield 'H'
        h_fin(1)
        yield 'H'
        for _ in h_chain(1):
            yield 'H'
        yield 'Hdone'
        first_scan = [True]
        tail_flush()
        K.op('pool', lambda: nc.gpsimd.memset(oss, 0.0), writes=['oss'] + [('oss', h_) for h_ in range(8)])
        for c in range(nch):
            r0 = c * C
            rs = slice(r0, r0 + C)
            if ti.samp:
                for h in range(8):
                    K.dma('sp', S[:, h, :], sdel_d[l, c, h], writes=[('S', h // 4)])
                for gq in range(2):
                    K.op('act', lambda: nc.scalar.copy(out=Sb[:, gq * 4:gq * 4 + 4, :], in_=S[:, gq * 4:gq * 4 + 4, :]), reads=[('S', gq)], writes=[('Sb', gq)])
            GQ = (0, 1)
            hsl = [slice(gq * 4, gq * 4 + 4) for gq in GQ]
            bk_ = {}
            for gq in GQ:
                ba, bb_ = bank(), bank()
                bk_[('a', gq)], bk_[('b', gq)] = ba, bb_
                for h4 in range(4):
                    h = gq * 4 + h4
                    K.op('pe', lambda: nc.tensor.matmul(PS[ba][:T, h4 * 128:(h4 + 1) * 128], kT[:, h, :T], Sb[:, h, :], start=True, stop=True),
                         reads=[('kT', p, gq), ('Sb', gq)], writes=[pk(ba)])
                for h4 in range(4):
                    h = gq * 4 + h4
                    K.op('pe', lambda: nc.tensor.matmul(PS[bb_][:T, h4 * 128:(h4 + 1) * 128], qT[:, h, :T], Sb[:, h, :], start=True, stop=True),
                         reads=[('qT', p, gq), ('Sb', gq)], writes=[pk(bb_)])
            for gq in GQ:
                ba, bb_ = bk_[('a', gq)], bk_[('b', gq)]
                for h4 in range(4):
                    h = gq * 4 + h4
                    if first_scan[0]:
                        first_scan[0] = False
                        rr, rw = [], ['REG']
                    else:
                        rr, rw = ['REG'], []
                    K.op('dve', lambda: nc.vector.scalar_tensor_tensor(out=Rp[gq][rs, h4, :], in0=PS[ba][rs, h4 * 128:(h4 + 1) * 128], scalar=gsm[rs, 8, h:h + 1],
                                                                       in1=vp[rs, h, :], op0=ALU.mult, op1=ALU.add),
                         reads=[pk(ba), ('g', p, 8), ('vp', p, gq)] + rr, writes=[('Rp', gq, h4)] + rw)
                K.op('dve', lambda: nc.vector.tensor_tensor(out=on[rs, hsl[gq], :], in0=PS[bb_][rs, :].rearrange("p (h d) -> p h d", h=4),
                                                            in1=gsm[rs, 7, hsl[gq]].unsqueeze(2).broadcast_to([C, 4, 128]), op=ALU.mult),
                     reads=[pk(bb_), ('g', p, 7)], writes=ONK[gq * 4:gq * 4 + 4])
            for gq in GQ:
                bc_ = bank()
                bk_[('c', gq)] = bc_
                for h4 in range(4):
                    h = gq * 4 + h4
                    K.op('pe', lambda: nc.tensor.matmul(PS[bc_][:T, h4 * 128:(h4 + 1) * 128], XT[rs, h, :T], Rp[gq][rs, h4, :], start=True, stop=True),
                         reads=[('XT', gq), ('Rp', gq, h4), 'REG'], writes=[pk(bc_)])
            for gq in GQ:
                bc_ = bk_[('c', gq)]
                K.op('act', lambda: nc.scalar.copy(out=Yb[gq][rs, :, :], in_=PS[bc_][rs, :].rearrange("p (h d) -> p h d", h=4)),
                     reads=[pk(bc_), 'REG'], writes=[('Yb', gq)])
                K.op('pool', lambda: nc.gpsimd.tensor_tensor(out=S[:, hsl[gq], :], in0=S[:, hsl[gq], :], in1=glbc[:, hsl[gq], c:c + 1].broadcast_to([128, 4, 128]), op=ALU.mult),
                     reads=[('S', gq), ('glbc', p, gq)], writes=[('S', gq)])
            for gq in GQ:
                bd, be = bank(), bank()
                bk_[('d', gq)], bk_[('e', gq)] = bd, be
                for h4 in range(4):
                    h = gq * 4 + h4
                    K.op('pe', lambda: nc.tensor.matmul(PS[be][:, h4 * 128:(h4 + 1) * 128], kd[rs, h, :], Yb[gq][rs, h4, :], start=True, stop=True),
                         reads=[('kd', p, gq), ('Yb', gq), 'REG'], writes=[pk(be)])
                for h4 in range(4):
                    h = gq * 4 + h4
                    K.op('pe', lambda: nc.tensor.matmul(PS[bd][:T, h4 * 128:(h4 + 1) * 128], intraT[rs, h, :T], Yb[gq][rs, h4, :], start=True, stop=True),
                         reads=[('intraT', h), ('Yb', gq), 'REG'], writes=[pk(bd)])
            for gq in GQ:
                bd, be = bk_[('d', gq)], bk_[('e', gq)]
                K.op('dve', lambda: nc.vector.tensor_tensor(out=S[:, hsl[gq], :], in0=PS[be][:, :].rearrange("p (h d) -> p h d", h=4), in1=S[:, hsl[gq], :], op=ALU.add),
                     reads=[pk(be), ('S', gq)], writes=[('S', gq)])
                K.op('act', lambda: nc.scalar.copy(out=Sb[:, hsl[gq], :], in_=S[:, hsl[gq], :]), reads=[('S', gq)], writes=[('Sb', gq)])
                K.op('dve', lambda: nc.vector.tensor_tensor(out=on[rs, hsl[gq], :], in0=PS[bd][rs, :].rearrange("p (h d) -> p h d", h=4), in1=on[rs, hsl[gq], :], op=ALU.add),
                     reads=[pk(bd)] + ONK[gq * 4:gq * 4 + 4], writes=ONK[gq * 4:gq * 4 + 4])
            if ti.samp:
                for h in range(8):
                    out_stamps.append(K.dma('sp', sdelo_d[l, c, h], S[:, h, :], reads=[('S', h // 4)], writes=[('sdelo', l, c, h)]))
        if ti.idx == NPT - 1:
            for h in range(8):
                out_stamps.append(K.dma('sp', pdel_d[l, h], S[:, h, :], reads=[('S', h // 4)], writes=[('pdel', l, h)]))
        for h in range(8):
            K.op('act', lambda: nc.scalar.activation(out=tmb[:, :, :].rearrange("p a b -> p (a b)").bitcast(F32)[:T, (h % 2) * 128:(h % 2) * 128 + 128], in_=on[:T, h, :], func=AF.Square, accum_out=oss[:T, h:h + 1]),
                 reads=[('on', h)], writes=[('oss', h), ('junk', h % 2)] + (['tmb'] if h < 2 else []))
        K.op('act', lambda: nc.scalar.activation(out=oss[:T, :], in_=oss[:T, :], func=AF.Ln, bias=EPS_RMS[:T, :], scale=1.0 / 128.0),
             reads=[('oss', h_) for h_ in range(8)] + ['eps'], writes=['oss'])
        K.op('act', lambda: nc.scalar.activation(out=oss[:T, :], in_=oss[:T, :], func=AF.Exp, scale=-0.5), reads=['oss'], writes=['oss'])
        K.op('dve', lambda: nc.vector.tensor_tensor(out=on[:T, :, :], in0=on[:T, :, :], in1=oss[:T, :].unsqueeze(2).broadcast_to([T, 8, 128]), op=ALU.mult),
             reads=ONK + ['oss'], writes=ONK)
        K.op('pool', lambda: nc.gpsimd.tensor_tensor(out=on[:T, :, :], in0=on[:T, :, :], in1=normw[:T, :].unsqueeze(1).broadcast_to([T, 8, 128]), op=ALU.mult),
             reads=ONK + ['normw'], writes=ONK)
        for h2 in range(2):
            sl = slice(h2 * 512, (h2 + 1) * 512)
            K.op('dve', lambda: nc.vector.tensor_tensor(out=og[:T, sl], in0=on[:T, h2 * 4:h2 * 4 + 4, :].rearrange("p h d -> p (h d)"), in1=og[:T, sl], op=ALU.mult),
                 reads=ONK + ['og'], writes=['og'])
        out_proj(ti, False, XT, [('XT', 0), ('XT', 1)], immediate=2)
        yield 'Sdone'

    def b_setup():
        K.full_sync()
        apos[0] = 0
        B = {}
        wflat = wbig[:, :, :].rearrange("p a b -> p (a b)")
        B['wflat'] = wflat
        B['KT'] = wflat[0:65, 20480:20480 + 4 * 2112].rearrange("p (g t) -> p g t", g=4)
        B['KTc'] = wflat[0:65, 28928:28928 + 1024].rearrange("p (g s t) -> p g s t", g=4, s=2)
        B['VEc'] = wflat[:, 29952:29952 + 520].rearrange("p (s g d) -> p s g d", s=2, g=4)
        B['PTn'] = wflat[0:64, 30472:30472 + 256].rearrange("p (h q) -> p h q", h=4)
        B['ogT'] = wflat[:, 30728:30728 + 1024].rearrange("p (a t) -> p a t", a=8)
        B['VE'] = ar([128, NT, 4, 65], BF16)
        B['kvf'] = ar([128, 512])
        B['kext'] = ar([128, 4, 65], BF16)
        B['qf'] = ar([128, 16, 64])
        B['qext'] = ar([128, 16, 65], BF16)
        B['QT'] = ar([128, 16, 128], BF16)
        B['rope'] = ar([128, NT, 2, 8])
        B['PTp'] = [ar([128, 4, 128], BF16) for _ in range(2)]
        B['PTc'] = [ar([128, 4, 128], BF16) for _ in range(2)]
        B['PTs'] = [[ar([128, 4, 64], BF16) for _ in range(2)] for _ in range(2)]
        B['ztz'] = ar([128, 1024])
        B['zt'] = B['ztz'][:, 0:512]
        B['zs'] = B['ztz'][:, 512:1024]
        B['sum8'] = ar([128, NT + 3])
        B['ksm'] = ar([128, 8, 4])
        B['qsm'] = ar([128, 6, 16])
        B['nshb'] = ar([128, 16], BF16)
        B['sinkb'] = ar([128, 16])
        B['den'] = ar([128, 2, 4])
        B['zsb'] = ar([128, D], BF16)
        print("arena used (B):", apos[0])
        return B

    def b_layer(j, B):
        wflat = B['wflat']
        wb3 = wflat[:, 0:16384].rearrange("p (kc n) -> p kc n", kc=8)
        if j == 0:
            K.dma('pool', wflat[:, 16384:16384 + 4096].rearrange("p (kc n) -> p kc n", kc=8), b_w_kv.rearrange("(kc p) n -> p kc n", p=128), writes=['wkv'])
        for (bname, c0, c1) in [('q0', 0, 512), ('q1', 512, 1024), ('z', 1024, 2048)]:
            K.dma('pool', wb3[:, :, c0:c1], b_w_in[j][:, c0:c1].rearrange("(kc p) n -> p kc n", p=128), writes=[('wbin', bname)])
        K.dma('sp', lng[:], lng_d[2 + j], writes=['lng'])
        K.dma('sp', lnb[:], lnb_d[2 + j], writes=['lnb'])
        K.dma('sp', B['sinkb'], sink_d[j], writes=['sinkb'])
        if j == 0:
            b_init(B)
        for t in range(NT):
            b_tile(j, TileInfo(t), B)
        tail_flush()

    def ksum8(B, kv3, T, col, scratch):
        ksm = B['ksm']
        scratch = B['zt'][:, 0:256].rearrange("p (g d) -> p g d", g=4)
        K.op('dve', lambda: nc.vector.tensor_tensor(out=scratch[:T, :, :], in0=kv3, in1=kv3, op=ALU.mult),
             reads=['kvf'], writes=['zt'])
        K.op('dve', lambda: nc.vector.tensor_reduce(out=ksm[:T, 0, :], in_=scratch[:T, :, :], axis=AX.X, op=ALU.add),
             reads=['zt'], writes=['ksm'])
        K.op('dve', lambda: nc.vector.tensor_reduce(out=ksm[:T, 1, 0:1], in_=ksm[:T, 0, :], axis=AX.X, op=ALU.max),
             reads=['ksm'], writes=['ksm'])
        K.op('dve', lambda: nc.vector.tensor_tensor(out=ksm[:T, 2, 0:1], in0=ksm[:T, 1, 0:1], in1=ksm[:T, 1, 0:1], op=ALU.mult),
             reads=['ksm'], writes=['ksm'])
        K.op('dve', lambda: nc.vector.tensor_tensor(out=ksm[:T, 3, 0:1], in0=ksm[:T, 2, 0:1], in1=ksm[:T, 2, 0:1], op=ALU.mult),
             reads=['ksm'], writes=['ksm'])
        b = bank()
        K.op('pe', lambda: nc.tensor.matmul(PS[b][:, 0:1], onesf[:T, :], ksm[:T, 3, 0:1], start=True, stop=True),
             reads=['onesf', 'ksm'], writes=[pk(b)])
        K.op('act', lambda: nc.scalar.copy(out=B['sum8'][:, col:col + 1], in_=PS[b][:, 0:1]), reads=[pk(b)], writes=[('sum8', col)])

    def kt_store(B, T, dst_fn, rkeys, wkey):
        b = bank()
        psb = PS[b][:, :].bitcast(BF16)
        for g in range(4):
            K.op('pe', lambda: nc.tensor.transpose(psb[0:65, g * T:(g + 1) * T], B['kext'][:T, g, :], identb[:T, :T]),
                 reads=['kext', 'identb'], writes=[pk(b)])
        for g in range(4):
            K.op('act', lambda: nc.scalar.copy(out=dst_fn(g), in_=psb[0:65, g * T:(g + 1) * T]), reads=[pk(b)], writes=[wkey])

    def b_init(B):
        K.dma('sp', B['rope'], rope_d, writes=['rope'])
        K.op('pool', lambda: nc.gpsimd.memset(B['kext'][:, :, 64:65], 1.0), writes=['kext'])
        K.op('pool', lambda: nc.gpsimd.memset(B['VE'][:, :, :, 64:65], 1.0), writes=['VE1'])
        K.op('pool', lambda: nc.gpsimd.memset(B['VEc'][:, :, :, 64:65], 1.0), writes=['VEc'])
        for i in range(2):
            K.op('pool', lambda: nc.gpsimd.memset(B['PTp'][i], 0.0), writes=[('PTp', i)])
            K.op('pool', lambda: nc.gpsimd.memset(B['PTc'][i], 0.0), writes=[('PTc', i)])
            for s_ in range(2):
                K.op('pool', lambda: nc.gpsimd.memset(B['PTs'][i][s_], 0.0), writes=[('PTs', i, s_)])
        K.op('pool', lambda: nc.gpsimd.memset(B['PTn'], 0.0), writes=['PTn'])
        K.op('pool', lambda: nc.gpsimd.memset(B['sum8'], 0.0), writes=[('sum8', c) for c in range(NT + 3)])
        ckf = B['qf'][:, 0:8, :].rearrange("p a b -> p (a b)")
        cvf = B['qf'][:, 8:16, :].rearrange("p a b -> p (a b)")
        for s_ in range(2):
            K.dma('sp', ckf[:, s_ * 256:(s_ + 1) * 256], ck_d[s_], writes=['qf'])
            K.dma('sp', cvf[:, s_ * 256:(s_ + 1) * 256], cv_d[s_], writes=['qf'])
        for s_ in range(2):
            out_stamps.append(K.dma('sp', sk_d[s_, 0:96, :], ckf[32:128, s_ * 256:(s_ + 1) * 256], reads=['qf'], writes=[('sk', s_, 0)]))
            out_stamps.append(K.dma('sp', sv_d[s_, 0:96, :], cvf[32:128, s_ * 256:(s_ + 1) * 256], reads=['qf'], writes=[('sv', s_, 0)]))
            K.op('act', lambda: nc.scalar.copy(out=B['kext'][:, :, 0:64], in_=ckf[:, s_ * 256:(s_ + 1) * 256].rearrange("p (g d) -> p g d", g=4)),
                 reads=['qf'], writes=['kext'])
            kt_store(B, 128, lambda g: B['KTc'][:, g, s_, :], ['kext'], 'KTc')
            K.op('act', lambda: nc.scalar.copy(out=B['VEc'][:, s_, :, 0:64], in_=cvf[:, s_ * 256:(s_ + 1) * 256].rearrange("p (g d) -> p g d", g=4)),
                 reads=['qf'], writes=['VEc'])
        ksm = B['ksm']
        scr = on
        kv8 = ckf.rearrange("p (a d) -> p a d", a=8)
        K.op('dve', lambda: nc.vector.tensor_tensor(out=scr[:, :, 0:64], in0=kv8, in1=kv8, op=ALU.mult), reads=['qf'], writes=ONK)
        K.op('dve', lambda: nc.vector.tensor_reduce(out=ksm[:, 4:6, :].rearrange("p a b -> p (a b)"), in_=scr[:, :, 0:64], axis=AX.X, op=ALU.add),
             reads=ONK, writes=['ksm'])
        K.op('dve', lambda: nc.vector.tensor_reduce(out=ksm[:, 1, 0:1], in_=ksm[:, 4:6, :].rearrange("p a b -> p (a b)"), axis=AX.X, op=ALU.max),
             reads=['ksm'], writes=['ksm'])
        K.op('dve', lambda: nc.vector.tensor_tensor(out=ksm[:, 2, 0:1], in0=ksm[:, 1, 0:1], in1=ksm[:, 1, 0:1], op=ALU.mult), reads=['ksm'], writes=['ksm'])
        K.op('dve', lambda: nc.vector.tensor_tensor(out=ksm[:, 3, 0:1], in0=ksm[:, 2, 0:1], in1=ksm[:, 2, 0:1], op=ALU.mult), reads=['ksm'], writes=['ksm'])
        b = bank()
        K.op('pe', lambda: nc.tensor.matmul(PS[b][:, 0:1], onesf[:, :], ksm[:, 3, 0:1], start=True, stop=True), reads=['onesf', 'ksm'], writes=[pk(b)])
        K.op('act', lambda: nc.scalar.copy(out=B['sum8'][:, NT:NT + 1], in_=PS[b][:, 0:1]), reads=[pk(b)], writes=[('sum8', NT)])

    def rope_inplace(B, v4, nh, T, t, key):
        cos = B['rope'][:T, t, 0, :].unsqueeze(1).broadcast_to([T, nh, 8])
        sin = B['rope'][:T, t, 1, :].unsqueeze(1).broadcast_to([T, nh, 8])
        rt = B['zt'][:, :].rearrange("p (a h d) -> p a h d", a=4, h=16)
        x1 = v4[:, :, 0:8]
        x2 = v4[:, :, 8:16]
        K.op('dve', lambda: nc.vector.tensor_tensor(out=rt[:T, 0, 0:nh, :], in0=x1, in1=cos, op=ALU.mult), reads=[key, 'rope'], writes=['zt'])
        K.op('dve', lambda: nc.vector.tensor_tensor(out=rt[:T, 1, 0:nh, :], in0=x2, in1=sin, op=ALU.mult), reads=[key, 'rope'], writes=[('zt', 1)])
        K.op('dve', lambda: nc.vector.tensor_tensor(out=rt[:T, 2, 0:nh, :], in0=x2, in1=cos, op=ALU.mult), reads=[key, 'rope'], writes=[('zt', 2)])
        K.op('dve', lambda: nc.vector.tensor_tensor(out=rt[:T, 3, 0:nh, :], in0=x1, in1=sin, op=ALU.mult), reads=[key, 'rope'], writes=[('zt', 3)])
        K.op('dve', lambda: nc.vector.tensor_tensor(out=x1, in0=rt[:T, 0, 0:nh, :], in1=rt[:T, 1, 0:nh, :], op=ALU.subtract), reads=['zt', ('zt', 1), ('zt', 2), ('zt', 3)], writes=[key])
        K.op('dve', lambda: nc.vector.tensor_tensor(out=x2, in0=rt[:T, 2, 0:nh, :], in1=rt[:T, 3, 0:nh, :], op=ALU.add), reads=['zt', ('zt', 1), ('zt', 2), ('zt', 3)], writes=[key])

    def b_tile(j, ti, B):
        T, t = ti.T, ti.idx
        wflat = B['wflat']
        KT, VE, QT, qf, qext, kvf, kext = B['KT'], B['VE'], B['QT'], B['qf'], B['qext'], B['kvf'], B['kext']
        tok0 = t * 128
        make_xT(ti)
        tail_step()
        if j == 0:
            b = bank()
            for kc in range(8):
                K.op('pe', lambda: nc.tensor.matmul(PS[b][:T, :], xT[:, kc, :T], wflat[:, 16384 + kc * 512:16384 + (kc + 1) * 512],
                                                    start=(kc == 0), stop=(kc == 7)),
                     reads=XTK + ['wkv'], writes=[pk(b)])
            K.op('act', lambda: nc.scalar.copy(out=kvf[:T, :], in_=PS[b][:T, :]), reads=[pk(b)], writes=['kvf'])
            kv3 = kvf[:T, 0:256].rearrange("p (g d) -> p g d", g=4)
            vv3 = kvf[:T, 256:512].rearrange("p (g d) -> p g d", g=4)
            rope_inplace(B, kv3, 4, T, t, 'kvf')
            K.op('act', lambda: nc.scalar.copy(out=kext[:T, :, 0:64], in_=kv3), reads=['kvf'], writes=['kext'])
            K.op('pool', lambda: nc.gpsimd.tensor_copy(out=VE[:T, t, :, 0:64], in_=vv3), reads=['kvf'], writes=[('VE', t)])
            tail_step()
            kt_store(B, T, lambda g: KT[:, g, tok0:tok0 + T], ['kext'], ('KT', t))
            ksum8(B, kv3, T, t, None)
            if t == NPT - 1:
                out_stamps.append(K.dma('sp', pk_d, kvf[:, 0:256], reads=['kvf'], writes=['pk']))
                out_stamps.append(K.dma('sp', pv_d, kvf[:, 256:512], reads=['kvf'], writes=['pv']))
            if ti.samp:
                for s_ in range(2):
                    out_stamps.append(K.dma('sp', sk_d[s_, 96:128, :], kvf[s_ * 32:(s_ + 1) * 32, 0:256], reads=['kvf'], writes=[('sk', s_, 1)]))
                    out_stamps.append(K.dma('sp', sv_d[s_, 96:128, :], kvf[s_ * 32:(s_ + 1) * 32, 256:512], reads=['kvf'], writes=[('sv', s_, 1)]))
        for h2 in range(2):
            b = bank()
            for kc in range(8):
                K.op('pe', lambda: nc.tensor.matmul(PS[b][:T, :], xT[:, kc, :T], wflat[:, kc * 2048 + h2 * 512:kc * 2048 + (h2 + 1) * 512],
                                                    start=(kc == 0), stop=(kc == 7)),
                     reads=XTK + [('wbin', 'q%d' % h2)], writes=[pk(b)])
            K.op('act', lambda: nc.scalar.activation(out=qf[:T, h2 * 8:h2 * 8 + 8, :], in_=PS[b][:T, :].rearrange("p (h d) -> p h d", h=8),
                                                     func=AF.Copy, scale=0.125),
                 reads=[pk(b)], writes=['qf'])
        tail_step()
        rope_inplace(B, qf[:T, :, :], 16, T, t, 'qf')
        tail_step()
        K.op('pool', lambda: nc.gpsimd.tensor_copy(out=qext[:T, :, 0:64], in_=qf[:T, :, :]), reads=['qf'], writes=['qext'])
        qsm = B['qsm']
        scr = B['ztz'][:, :].rearrange("p (h d) -> p h d", h=16)
        K.op('dve', lambda: nc.vector.tensor_tensor(out=scr[:T, :, :], in0=qf[:T, :, :], in1=qf[:T, :, :], op=ALU.mult), reads=['qf'], writes=['zt', 'zs'])
        K.op('dve', lambda: nc.vector.tensor_reduce(out=qsm[:T, 0, :], in_=scr[:T, :, :], axis=AX.X, op=ALU.add), reads=['zt', 'zs'], writes=['qsm0'])
        K.op('act', lambda: nc.scalar.activation(out=qsm[:T, 0, :], in_=qsm[:T, 0, :], func=AF.Ln, bias=EPS_RMS[:T, :], scale=1.0), reads=['qsm0', 'eps'], writes=['qsm0'])
        K.op('act', lambda: nc.scalar.activation(out=qsm[:T, 0, :], in_=qsm[:T, 0, :], func=AF.Exp, scale=0.5), reads=['qsm0'], writes=['qsm0'])
        tail_step()
        cprev = NT if ti.samp else (t - 1 if t > 0 else NT + 1)
        K.op('dve', lambda: nc.vector.tensor_tensor(out=qsm[:T, 1, 0:1], in0=B['sum8'][:T, t:t + 1], in1=B['sum8'][:T, cprev:cprev + 1], op=ALU.add),
             reads=[('sum8', t), ('sum8', cprev)], writes=['qsm1'])
        K.op('act', lambda: nc.scalar.activation(out=qsm[:T, 1, 0:1], in_=qsm[:T, 1, 0:1], func=AF.Ln), reads=['qsm1'], writes=['qsm1'])
        K.op('act', lambda: nc.scalar.activation(out=qsm[:T, 1, 0:1], in_=qsm[:T, 1, 0:1], func=AF.Exp, scale=0.125), reads=['qsm1'], writes=['qsm1'])
        K.op('dve', lambda: nc.vector.tensor_scalar(out=B['nshb'][:T, :], in0=qsm[:T, 0, :], scalar1=qsm[:T, 1, 0:1], scalar2=-1.0, op0=ALU.mult, op1=ALU.mult),
             reads=['qsm0', 'qsm1'], writes=['nshb'])
        K.op('pool', lambda: nc.gpsimd.tensor_copy(out=qext[:T, :, 64:65], in_=B['nshb'][:T, :].unsqueeze(2)), reads=['nshb'], writes=['qext'])
        K.op('dve', lambda: nc.vector.tensor_tensor(out=qsm[:T, 2, :], in0=B['nshb'][:T, :], in1=B['sinkb'][:T, :], op=ALU.add),
             reads=['nshb', 'sinkb'], writes=['qsm2'])
        K.op('act', lambda: nc.scalar.activation(out=qsm[:T, 2, :], in_=qsm[:T, 2, :], func=AF.Exp), reads=['qsm2'], writes=['qsm2'])
        for h2 in range(2):
            b = bank()
            psb = PS[b][:, :].bitcast(BF16)
            for hq in range(8):
                h = h2 * 8 + hq
                K.op('pe', lambda: nc.tensor.transpose(psb[0:65, hq * T:(hq + 1) * T], qext[:T, h, :], identb[:T, :T]),
                     reads=['qext', 'identb'], writes=[pk(b)])
            K.op('act', lambda: nc.scalar.copy(out=QT[0:65, h2 * 8:h2 * 8 + 8, :T], in_=psb[0:65, 0:8 * T].rearrange("p (a t) -> p a t", a=8)),
                 reads=[pk(b)], writes=[('QT', h2)])
        tail_step()
        tail_flush()
        for h2 in range(2):
            b = bank()
            for kc in range(8):
                K.op('pe', lambda: nc.tensor.matmul(PS[b][:T, :], xT[:, kc, :T], wflat[:, kc * 2048 + 1024 + h2 * 512:kc * 2048 + 1024 + (h2 + 1) * 512],
                                                    start=(kc == 0), stop=(kc == 7)),
                     reads=XTK + [('wbin', 'z')], writes=[pk(b)])
            silu_from(B['zsb'][:T, h2 * 512:(h2 + 1) * 512], PS[b][:T, :], B['ztz'][:T, h2 * 512:(h2 + 1) * 512], [pk(b)], [('zsb', h2)], [['zt', 'zs'][h2]])
        den = B['den']

        def att_norm(g, bo):
            pb_ = g % 2
            po = PS[bo][:T, 0:260].rearrange("p (h d) -> p h d", h=4)
            K.op('dve', lambda: nc.vector.tensor_tensor(out=den[:T, pb_, :], in0=po[:, :, 64], in1=qsm[:T, 2, 4 * g:4 * g + 4], op=ALU.add),
                 reads=[pk(bo), 'qsm2'], writes=[('den', pb_)])
            K.op('dve', lambda: nc.vector.reciprocal(out=den[:T, pb_, :], in_=den[:T, pb_, :]), reads=[('den', pb_)], writes=[('den', pb_)])
            K.op('dve', lambda: nc.vector.tensor_tensor(out=qf[:T, 4 * g:4 * g + 4, :], in0=po[:, :, 0:64],
                                                        in1=den[:T, pb_, :].unsqueeze(2).broadcast_to([T, 4, 64]), op=ALU.mult),
                 reads=[pk(bo), ('den', pb_)], writes=[('ob', g)])

        if not ti.samp:
            sbk = {}

            def att_scores(g):
                qk = ('QT', g // 2)
                rhsq = QT[0:65, 4 * g:4 * g + 4, :T]
                b1 = None
                if t > 0:
                    b1 = bank()
                    reserved.add(b1)
                    K.op('pe', lambda: nc.tensor.matmul(PS[b1][:, 0:4 * T], KT[:, g, tok0 - 128:tok0], rhsq, start=True, stop=True),
                         reads=[('KT', t - 1), qk], writes=[pk(b1)])
                b2 = bank()
                reserved.add(b2)
                K.op('pe', lambda: nc.tensor.matmul(PS[b2][:, 0:4 * T], KT[:, g, tok0:tok0 + 128], rhsq, start=True, stop=True),
                     reads=[('KT', t), qk], writes=[pk(b2)])
                sbk[g] = (b1, b2)

            def att_exp(g):
                pb_ = g % 2
                PTp, PTc = B['PTp'][pb_], B['PTc'][pb_]
                b1, b2 = sbk[g]
                if t > 0:
                    v1 = PS[b1][:, 0:4 * T].rearrange("p (h q) -> p h q", h=4)
                    K.op('act', lambda: nc.scalar.activation(out=PTp[0:64, :, 0:64], in_=v1[0:64, :, 0:64], func=AF.Exp), reads=[pk(b1)], writes=[('PTp', pb_)])
                    K.op('act', lambda: nc.scalar.activation(out=PTp[64:128, :, :], in_=v1[64:128, :, :], func=AF.Exp), reads=[pk(b1)], writes=[('PTp', pb_)])
                    reserved.discard(b1)
                v2 = PS[b2][:, 0:4 * T].rearrange("p (h q) -> p h q", h=4)
                K.op('act', lambda: nc.scalar.activation(out=PTc[0:64, :, :], in_=v2[0:64, :, :], func=AF.Exp), reads=[pk(b2)], writes=[('PTc', pb_)])
                K.op('act', lambda: nc.scalar.activation(out=PTc[64:128, :, 64:128], in_=v2[64:128, :, 64:128], func=AF.Exp), reads=[pk(b2)], writes=[('PTc', pb_)])
                reserved.discard(b2)

            def att_pv(g):
                pb_ = g % 2
                PTp, PTc = B['PTp'][pb_], B['PTc'][pb_]
                bo = bank()
                for hh in range(4):
                    if t > 0:
                        K.op('pe', lambda: nc.tensor.matmul(PS[bo][:T, hh * 65:(hh + 1) * 65], PTp[:, hh, :], VE[:, t - 1, g, :], start=True, stop=False),
                             reads=[('PTp', pb_), ('VE', t - 1), 'VE1'], writes=[pk(bo)])
                    K.op('pe', lambda: nc.tensor.matmul(PS[bo][:T, hh * 65:(hh + 1) * 65], PTc[:, hh, :], VE[:, t, g, :], start=(t == 0), stop=True),
                         reads=[('PTc', pb_), ('VE', t), 'VE1'], writes=[pk(bo)])
                return bo

            att_scores(0)
            for g in range(4):
                if g + 1 < 4:
                    att_scores(g + 1)
                att_exp(g)
                if g > 0:
                    bo = att_pv(g - 1)
                    att_norm(g - 1, bo)
            bo = att_pv(3)
            att_norm(3, bo)
        else:
            for g in range(4):
                pb_ = g % 2
                qk = ('QT', g // 2)
                bo = bank()
                PTs = B['PTs'][pb_]
                PTn = B['PTn']
                for s_ in range(2):
                    b1 = bank()
                    K.op('pe', lambda: nc.tensor.matmul(PS[b1][:, 0:128], B['KTc'][:, g, s_, :], QT[0:65, 4 * g:4 * g + 4, s_ * 32:(s_ + 1) * 32], start=True, stop=True),
                         reads=['KTc', qk], writes=[pk(b1)])
                    K.op('act', lambda: nc.scalar.activation(out=PTs[s_][:, :, s_ * 32:(s_ + 1) * 32], in_=PS[b1][:, 0:128].rearrange("p (h q) -> p h q", h=4), func=AF.Exp),
                         reads=[pk(b1)], writes=[('PTs', pb_, s_)])
                b2 = bank()
                K.op('pe', lambda: nc.tensor.matmul(PS[b2][0:64, 0:256], KT[:, g, tok0:tok0 + 64], QT[0:65, 4 * g:4 * g + 4, 0:64], start=True, stop=True),
                     reads=[('KT', t), qk], writes=[pk(b2)])
                v2 = PS[b2][0:64, 0:256].rearrange("p (h q) -> p h q", h=4)
                K.op('act', lambda: nc.scalar.activation(out=PTn[0:32, :, 0:32], in_=v2[0:32, :, 0:32], func=AF.Exp), reads=[pk(b2)], writes=['PTn'])
                K.op('act', lambda: nc.scalar.activation(out=PTn[32:64, :, 32:64], in_=v2[32:64, :, 32:64], func=AF.Exp), reads=[pk(b2)], writes=['PTn'])
                for hh in range(4):
                    K.op('pe', lambda: nc.tensor.matmul(PS[bo][:T, hh * 65:(hh + 1) * 65], PTs[0][:, hh, :], B['VEc'][:, 0, g, :], start=True, stop=False),
                         reads=[('PTs', pb_, 0), 'VEc'], writes=[pk(bo)])
                    K.op('pe', lambda: nc.tensor.matmul(PS[bo][:T, hh * 65:(hh + 1) * 65], PTs[1][:, hh, :], B['VEc'][:, 1, g, :], start=False, stop=False),
                         reads=[('PTs', pb_, 1), 'VEc'], writes=[pk(bo)])
                    K.op('pe', lambda: nc.tensor.matmul(PS[bo][:T, hh * 65:(hh + 1) * 65], PTn[:, hh, :], VE[0:64, t, g, :], start=False, stop=True),
                         reads=['PTn', ('VE', t), 'VE1'], writes=[pk(bo)])
                att_norm(g, bo)
        obf = qf[:, :, :].rearrange("p a b -> p (a b)")
        for h2 in range(2):
            sl = slice(h2 * 512, (h2 + 1) * 512)
            K.op('dve', lambda: nc.vector.tensor_tensor(out=og[:T, sl], in0=obf[:T, sl], in1=B['zsb'][:T, sl], op=ALU.mult),
                 reads=[('ob', 2 * h2), ('ob', 2 * h2 + 1), ('zsb', h2)], writes=['og'])
        out_proj(ti, j == 1, B['ogT'], ['ogTb'])

    done = False
    build.marks = []
    for l in range(2):
        done = a_layer(l)
        build.marks.append(K.ninstr['pe'])
        if done:
            break
    if not done and stop_after != 'A':
        Bv = b_setup()
        for j in range(2):
            b_layer(j, Bv)
            build.marks.append(K.ninstr['pe'])
    if done or stop_after == 'A':
        for t in range(NT):
            T = 64 if t == NPT else 128
            out_stamps.append(K.dma('sp', y_d[t * 128:t * 128 + T, :], xres[:T, t, :], reads=[('x', t)], writes=[('y', t)]))
    K.wait_all('sp', out_stamps)
    K.barrier()
    es.close()
    build.stats = (dict(K.ninstr), K.nwait)
    build.sim = K.simulate()
    return nc


def prep_inputs(inputs):
    c = host_consts()
    f = lambda a: np.ascontiguousarray(np.asarray(a, dtype=np.float32))
    xp = f(inputs['x_prompt'])
    xs = f(inputs['x_sample'])
    sd = f(inputs['state_delta'])
    sc = f(inputs['state_conv'])
    ck = f(inputs['cache_k'])
    cv = f(inputs['cache_v'])
    acw = f(inputs['a_conv_w'])
    convw = np.ascontiguousarray(acw.reshape(2, 4, 24, 128).transpose(0, 3, 2, 1))
    bc = lambda v, n: np.ascontiguousarray(np.broadcast_to(f(v)[:, None, :], (v.shape[0], 128, n)))
    shared = {
        'a_w_in': f(inputs['a_w_in']), 'a_w_out': f(inputs['a_w_out']), 'b_w_kv': f(inputs['b_w_kv']),
        'b_w_in': f(inputs['b_w_in']), 'b_w_out': f(inputs['b_w_out']),
        'convw': convw, 'alog_bc': bc(inputs['a_log'], 8), 'dtb_bc': bc(inputs['a_dt_bias'], 8),
        'normw_bc': bc(inputs['a_norm_w'], 128),
        'lng_bc': np.ascontiguousarray(np.concatenate([bc(inputs['a_ln_g'], D), bc(inputs['b_ln_g'], D)], 0)),
        'lnb_bc': np.ascontiguousarray(np.concatenate([bc(inputs['a_ln_b'], D), bc(inputs['b_ln_b'], D)], 0)),
        'sink_bc': bc(inputs['b_sinks'], 16),
        'identf': c['identf'], 'onesf': c['onesf'], 'masks': c['masks'], 'rope': c['rope'],
    }
    maps = []
    for i in range(NCORES):
        m = dict(shared)
        m['x'] = np.ascontiguousarray(np.concatenate([xp[i], xs[2 * i], xs[2 * i + 1]], 0))
        m['sdelta'] = np.ascontiguousarray(sd[:, 2 * i:2 * i + 2])
        scc = sc[:, 2 * i:2 * i + 2]
        m['sconv'] = np.ascontiguousarray(scc.reshape(2, 2, 3, 24, 128).transpose(0, 1, 4, 3, 2))
        m['ck'] = np.ascontiguousarray(ck[2 * i:2 * i + 2].reshape(2, 128, 256))
        m['cv'] = np.ascontiguousarray(cv[2 * i:2 * i + 2].reshape(2, 128, 256))
        maps.append(m)
    return maps


_NC_CACHE = {}


def kernel(**inputs):
    maps = prep_inputs(inputs)
    if 'nc' not in _NC_CACHE:
        _NC_CACHE['nc'] = build()
    nc = _NC_CACHE['nc']
    maps = [{n: m[n] for n in build.in_names} for m in maps]
    r = run_bass_kernel_spmd(nc, maps, core_ids=list(range(NCORES))).results
    y = np.stack([r[i]['y'] for i in range(NCORES)])
    y_prompt = np.ascontiguousarray(y[:, :SEQ])
    y_sample = np.ascontiguousarray(y[:, SEQ:].reshape(16, 32, D))
    p_delta = np.stack([r[i]['pdelta'] for i in range(NCORES)], 1)
    s_delta = np.concatenate([r[i]['sdelta_o'] for i in range(NCORES)], 1)
    pc = np.stack([r[i]['pconv'] for i in range(NCORES)], 1)
    p_conv = np.ascontiguousarray(pc.transpose(0, 1, 4, 3, 2).reshape(2, 8, 3, 3072))
    scv = np.concatenate([r[i]['sconv_o'] for i in range(NCORES)], 1)
    s_conv = np.ascontiguousarray(scv.transpose(0, 1, 4, 3, 2).reshape(2, 16, 3, 3072))
    if WITH_B:
        p_k = np.stack([r[i]['pk'] for i in range(NCORES)]).reshape(8, 128, 4, 64)
        p_v = np.stack([r[i]['pv'] for i in range(NCORES)]).reshape(8, 128, 4, 64)
        s_k = np.concatenate([r[i]['sk'] for i in range(NCORES)]).reshape(16, 128, 4, 64)
        s_v = np.concatenate([r[i]['sv'] for i in range(NCORES)]).reshape(16, 128, 4, 64)
    else:
        p_k = np.zeros((8, 128, 4, 64), np.float32)
        p_v = np.zeros((8, 128, 4, 64), np.float32)
        s_k = np.zeros((16, 128, 4, 64), np.float32)
        s_v = np.zeros((16, 128, 4, 64), np.float32)
    outs = (y_prompt, y_sample, p_delta, p_conv, p_k, p_v, s_delta, s_conv, s_k, s_v)
    return tuple(np.ascontiguousarray(o.astype(np.float32)) for o in outs)
```

```python
import numpy as np
import ml_dtypes
from contextlib import ExitStack
import concourse.bass as bass
import concourse.mybir as mybir
from concourse.bass_utils import run_bass_kernel_spmd

F32 = mybir.dt.float32
BF16 = mybir.dt.bfloat16
F32R = mybir.dt.float32r
AF = mybir.ActivationFunctionType
ALU = mybir.AluOpType
AX = mybir.AxisListType

NCORES = 8
D = 1024
SEQ = 2048
NPT = 16
NT = 17
ROWS = SEQ + 64
AIN = 4112
DN_ALPHA = 8.0 ** 0.25
LN_EPS = 1e-5
RMS_EPS = 1e-6
NEG = -30000.0
PAST = 4096


class Trk:
    NDS = 20

    def __init__(self, nc, es):
        self.nc = nc
        self.eng = {'pe': nc.tensor, 'act': nc.scalar, 'dve': nc.vector, 'pool': nc.gpsimd, 'sp': nc.sync}
        self.semh = {}
        for e in ['pe', 'act', 'dve', 'pool']:
            self.semh[e] = es.enter_context(nc.semaphore("s_" + e))
        for i in range(self.NDS):
            self.semh[('d', i)] = es.enter_context(nc.semaphore("s_d%d" % i))
            self.semh[('w', i)] = es.enter_context(nc.semaphore("s_w%d" % i))
        self.cnt = {k: 0 for k in self.semh}
        self.seen = {e: {} for e in self.eng}
        self.clock = {}
        self.lastw = {}
        self.readers = {}
        self.dnext = {'sp': 0, 'pool': 0}
        self.ninstr = {e: 0 for e in self.eng}
        self.nwait = 0
        self.prog = {e: [] for e in self.eng}

    def _merge(self, e, stamp):
        s = self.seen[e]
        k, v = stamp
        if s.get(k, 0) < v:
            s[k] = v
        for kk, vv in self.clock.get(stamp, {}).items():
            if s.get(kk, 0) < vv:
                s[kk] = vv

    def _deps(self, e, reads, writes, extra=()):
        deps = {}

        def add(st):
            if st is None:
                return
            k, v = st
            if k == e and e == 'pe':
                return
            if deps.get(k, 0) < v:
                deps[k] = v
        for r in reads:
            add(self.lastw.get(r))
        for w in writes:
            add(self.lastw.get(w))
            for k, v in self.readers.get(w, {}).items():
                add((k, v))
        for st in extra:
            add(st)
        need = [(k, v) for k, v in deps.items() if self.seen[e].get(k, 0) < v]
        return need

    def _emit(self, e, fn, need):
        eng = self.eng[e]
        for (k, v) in need[:-1]:
            eng.wait_ge(self.semh[k], v)
            self.nwait += 1
        ins = fn()
        if need:
            k, v = need[-1]
            ins._wait_ge(self.semh[k], v)
        self.prog[e].append([list(need), None])
        for st in need:
            self._merge(e, st)
        self.ninstr[e] += 1
        return ins

    def _finish(self, e, stamp, reads, writes):
        self.clock[stamp] = dict(self.seen[e])
        for w in writes:
            self.lastw[w] = stamp
            self.readers[w] = {}
        for r in reads:
            d = self.readers.setdefault(r, {})
            k, v = stamp
            if d.get(k, 0) < v:
                d[k] = v

    def op(self, e, fn, reads=(), writes=()):
        psr = [r for r in reads if isinstance(r, tuple) and r[0] == 'ps' and r not in writes]
        if psr:
            writes = list(writes) + psr
        need = self._deps(e, reads, writes)
        ins = self._emit(e, fn, need)
        self.cnt[e] += 1
        ins.then_inc(self.semh[e], 1)
        self.prog[e][-1][1] = (e, 1)
        stamp = (e, self.cnt[e])
        self.seen[e][e] = self.cnt[e] if e == 'pe' else self.seen[e].get(e, 0)
        self._finish(e, stamp, reads, writes)
        return ins

    def dma(self, q, out, in_, reads=(), writes=(), **kw):
        key = ('d' if q == 'sp' else 'w', self.dnext[q])
        self.dnext[q] = (self.dnext[q] + 1) % self.NDS
        prev = (key, self.cnt[key]) if self.cnt[key] > 0 else None
        need = self._deps(q, reads, writes, extra=(prev,) if prev else ())
        ins = self._emit(q, lambda: self.eng[q].dma_start(out=out, in_=in_, **kw), need)
        self.cnt[key] += 16
        ins.then_inc(self.semh[key], 16)
        self.prog[q][-1][1] = (key, 16)
        stamp = (key, self.cnt[key])
        self._finish(q, stamp, reads, writes)
        return stamp

    def simulate(self):
        sem = {k: 0 for k in self.semh}
        pc = {e: 0 for e in self.eng}
        progress = True
        while progress:
            progress = False
            for e in self.eng:
                while pc[e] < len(self.prog[e]):
                    waits, inc = self.prog[e][pc[e]]
                    if all(sem[k] >= v for k, v in waits):
                        if inc:
                            sem[inc[0]] += inc[1]
                        pc[e] += 1
                        progress = True
                    else:
                        break
        stuck = {e: (pc[e], len(self.prog[e]), self.prog[e][pc[e]][0] if pc[e] < len(self.prog[e]) else None) for e in self.eng}
        return stuck, {str(k): v for k, v in sem.items() if v}

    def full_sync(self):
        for e in self.eng:
            for k in list(self.semh.keys()):
                v = self.cnt[k]
                if v > 0 and k != e and self.seen[e].get(k, 0) < v:
                    self.eng[e].wait_ge(self.semh[k], v)
                    self.prog[e].append([[(k, v)], None])
                    self.seen[e][k] = v

    def barrier(self):
        for e in self.eng:
            for k in list(self.semh.keys()):
                v = self.cnt[k]
                if v > 0 and k != e:
                    self.eng[e].wait_ge(self.semh[k], v)

    def wait_all(self, e, stamps):
        for (k, v) in stamps:
            if self.seen[e].get(k, 0) < v:
                self.eng[e].wait_ge(self.semh[k], v)
                self.seen[e][k] = v


class TileInfo:
    def __init__(self, idx):
        self.idx = idx
        self.samp = idx == NPT
        self.T = 64 if self.samp else 128
        self.nseq = 2 if self.samp else 1
        self.L = self.T // self.nseq
        self.C = 32 if self.samp else 64
        self.nch = self.T // self.C
        self.m = 1 if self.samp else 0


def host_consts():
    c = {}
    c['identf'] = np.eye(128, dtype=np.float32)
    c['onesf'] = np.ones((128, 128), dtype=np.float32)
    masks = np.zeros((2, 5, 128, 128), dtype=np.float32)
    for m, (T, C) in enumerate([(128, 64), (64, 32)]):
        p = np.arange(T)[:, None]
        f = np.arange(T)[None, :]
        same = (p // C) == (f // C)
        masks[m, 0, :T, :T] = np.where(same & (f >= p), 0.0, NEG)
        masks[m, 1, :T, :T] = np.where(same & (f < p), 0.0, NEG)
        masks[m, 2, :T, :T] = np.where(p != f, 1.0, 0.0)
        masks[m, 3, :T, :T] = np.where(same & (p <= f), 1.0, 0.0)
        masks[m, 4, :T, :T] = np.where(same, 1.0, 0.0)
    c['masks'] = masks.transpose(2, 0, 1, 3).copy()
    half = 8
    inv = 500000.0 ** (-np.arange(half, dtype=np.float32) * 2.0 / 16.0)
    pos = np.zeros((NT, 128), dtype=np.float32)
    for t in range(NPT):
        pos[t] = t * 128 + np.arange(128)
    pos[NPT, :64] = PAST + (np.arange(64) % 32)
    ang = pos[:, :, None].astype(np.float32) * inv[None, None, :].astype(np.float32)
    cs = np.stack([np.cos(ang), np.sin(ang)], axis=2).astype(np.float32)
    c['rope'] = cs.transpose(1, 0, 2, 3).copy()
    return c


WITH_B = True
import os as _os
HRATIO = int(_os.environ.get('HRATIO', '1'))
HPER = int(_os.environ.get('HPER', '1'))
SPLITLEV = int(_os.environ.get('SPLITLEV', '1'))
HSKIP = [int(x) for x in _os.environ.get('HSKIP', '').split(',') if x]


def build(stop_after=None, taps=()):
    nc = bass.Bass("TRN2", target_bir_lowering=False)
    es = ExitStack()
    K = Trk(nc, es)
    tapped = {}

    in_names = []
    build.in_names = in_names

    def din(name, shape, dt=F32):
        in_names.append(name)
        return nc.dram_tensor(name, list(shape), dt, kind="ExternalInput").ap()

    def dout(name, shape, dt=F32):
        return nc.dram_tensor(name, list(shape), dt, kind="ExternalOutput").ap()

    x_d = din("x", [ROWS, D])
    a_w_in = din("a_w_in", [2, D, AIN])
    a_w_out = din("a_w_out", [2, D, D])
    if WITH_B:
        b_w_kv = din("b_w_kv", [D, 512])
    if WITH_B:
        b_w_in = din("b_w_in", [2, D, 2048])
    if WITH_B:
        b_w_out = din("b_w_out", [2, D, D])
    convw_d = din("convw", [2, 128, 24, 4])
    alog_d = din("alog_bc", [2, 128, 8])
    dtb_d = din("dtb_bc", [2, 128, 8])
    normw_d = din("normw_bc", [2, 128, 128])
    lng_d = din("lng_bc", [4, 128, D])
    lnb_d = din("lnb_bc", [4, 128, D])
    if WITH_B:
        sink_d = din("sink_bc", [2, 128, 16])
    sdel_d = din("sdelta", [2, 2, 8, 128, 128])
    sconv_d = din("sconv", [2, 2, 128, 24, 3])
    if WITH_B:
        ck_d = din("ck", [2, 128, 256])
    if WITH_B:
        cv_d = din("cv", [2, 128, 256])
    identf_d = din("identf", [128, 128])
    onesf_d = din("onesf", [128, 128])
    masks_d = din("masks", [128, 2, 5, 128])
    if WITH_B:
        rope_d = din("rope", [128, NT, 2, 8])

    y_d = dout("y", [ROWS, D])
    pdel_d = dout("pdelta", [2, 8, 128, 128])
    sdelo_d = dout("sdelta_o", [2, 2, 8, 128, 128])
    pconv_d = dout("pconv", [2, 128, 24, 3])
    sconvo_d = dout("sconv_o", [2, 2, 128, 24, 3])
    if WITH_B:
        pk_d = dout("pk", [128, 256])
    if WITH_B:
        pv_d = dout("pv", [128, 256])
    if WITH_B:
        sk_d = dout("sk", [2, 128, 256])
    if WITH_B:
        sv_d = dout("sv", [2, 128, 256])
    tap_d = {}

    def sb(name, shape, dt=F32):
        return nc.alloc_sbuf_tensor("sb_" + name, list(shape), dt)

    xres = sb("xres", [128, NT, D])
    wbig = sb("wbig", [128, 8, AIN], BF16)
    wring = sb("wring", [128, 4, D], BF16)
    identf = sb("identf", [128, 128])
    identb = sb("identb", [128, 128], BF16)
    onesf = sb("onesf", [128, 128])
    masks = sb("masks", [128, 5, 128])
    lng = sb("lng", [128, D])
    lnb = sb("lnb", [128, D])
    xT = sb("xT", [128, 8, 128], BF16)
    ogT = xT
    og = sb("og", [128, D], BF16)
    on = sb("on", [128, 8, 128])
    res = on[:, :, :].rearrange("p h d -> p (h d)")
    bst = sb("bst", [128, 2, 6])
    mv = sb("mv", [128, 2])
    rstd = sb("rstd", [128, 1])
    Aq = sb("Aq", [128, 4, 128], F32R)
    BPq = sb("BPq", [128, 4, 256], F32R)
    ARENA = 10580
    arena = sb("arena", [128, ARENA])
    apos = [0]

    def ar(shape, dt=F32):
        n = int(np.prod(shape[1:]))
        nf = n if dt == F32 or dt == F32R else (n + 1) // 2
        nf = (nf + 7) // 8 * 8
        o = apos[0]
        apos[0] += nf
        assert apos[0] <= ARENA, ("arena overflow", apos[0])
        v = arena[:, o:o + nf]
        if dt != F32:
            v = v.bitcast(dt)
        v = v[:, 0:n]
        if len(shape) == 3:
            v = v.rearrange("p (a b) -> p a b", a=shape[1])
        elif len(shape) == 4:
            v = v.rearrange("p (a b c) -> p a b c", a=shape[1], b=shape[2])
        return v

    convw = ar([128, 24, 4])
    alog = ar([128, 8])
    nega = ar([128, 8])
    dtb = ar([128, 8])
    normw = ar([128, 128])
    carry = ar([128, 24 * 2 * 3])
    cb = [ar([128, 4 * 132])] * 2
    acc = ar([128, 4, 128])
    ebuf = ar([128, 4, 128])
    yfm = acc
    ssq = ar([128, 16])
    rn = ar([128, 16])
    sc_k = ar([128, 8])
    tmb = ar([128, 4, 128], BF16)
    qT2 = [ar([128, 8, 128], BF16) for _ in range(2)]
    kT2 = [ar([128, 8, 128], BF16) for _ in range(2)]
    kd2 = [ar([128, 8, 128], BF16) for _ in range(2)]
    vp2 = [ar([128, 8, 128], BF16) for _ in range(2)]
    gsm2 = [ar([128, 12, 8]) for _ in range(2)]
    glbc2 = [ar([128, 8, 2]) for _ in range(2)]
    REG = ar([128, 1024])
    DTi = [REG[:, i * 128:(i + 1) * 128] for i in (0, 1)]
    Ds = [REG[:, i * 128:(i + 1) * 128] for i in (2, 3)]
    DTs = [REG[:, i * 128:(i + 1) * 128] for i in (4, 5)]
    _rb = REG[:, :].bitcast(BF16)
    Rp = [_rb[:, i * 512:(i + 1) * 512].rearrange("p (h d) -> p h d", h=4) for i in (0, 1)]
    Yb = [_rb[:, i * 512:(i + 1) * 512].rearrange("p (h d) -> p h d", h=4) for i in (2, 3)]
    intraT = ar([128, 8, 128], BF16)
    XT = ar([128, 8, 128], BF16)
    S = ar([128, 8, 128])
    Sb = ar([128, 8, 128], BF16)
    osq = ebuf[:, 0, :]
    oss = ar([128, 8])
    print("arena used (A):", apos[0])

    PSALL = nc.alloc_psum_tensor("psall", [128, 4096], F32)

    class _Bank:
        def __init__(self, i, n=1):
            self.i, self.n = i, n

        def __getitem__(self, key):
            return PSALL[:, self.i * 512:(self.i + self.n) * 512][key]

    PS = [_Bank(i) for i in range(8)]
    psn = [0]

    reserved = set()

    def bank():
        for _ in range(16):
            i = psn[0]
            psn[0] = (i + 1) % 8
            if i not in reserved:
                return i
        raise RuntimeError("no free PSUM bank: reserved=%s" % sorted(reserved))

    def pk(i):
        return ('ps', i)

    def bank2():
        for _ in range(16):
            i = psn[0]
            if i + 1 < 8 and i not in reserved and (i + 1) not in reserved:
                psn[0] = (i + 2) % 8
                return i
            psn[0] = (i + 1) % 8
        raise RuntimeError("no free PSUM bank pair: reserved=%s" % sorted(reserved))

    def PS2(i):
        return _Bank(i, 2)

    out_stamps = []
    dbg_outs = {}

    def tap(name, ap_sb, keys):
        if name not in taps:
            return
        d = dout("tap_" + name, list(ap_sb.shape), F32 if ap_sb.dtype == F32R else ap_sb.dtype)
        src = ap_sb.bitcast(F32) if ap_sb.dtype == F32R else ap_sb
        out_stamps.append(K.dma('sp', d, src, reads=keys, writes=[('tap', name)]))

    K.dma('sp', identf[:], identf_d, writes=['identf'])
    K.dma('sp', onesf[:], onesf_d, writes=['onesf'])
    K.dma('pool', identb[:], identf_d, writes=['identb'])
    for t in range(NT):
        T = 64 if t == NPT else 128
        K.dma('sp', xres[:T, t, :], x_d[t * 128:t * 128 + T, :], writes=[('x', t)])

    def load_w(dst, src_ap, ncols, key, c0=0):
        for kc in range(8):
            K.dma('pool', dst[:, kc, c0:c0 + ncols], src_ap[kc * 128:(kc + 1) * 128, :], writes=[(key, kc)])

    XTK = [('xT', 0), ('xT', 1)]
    ONK = [('on', h) for h in range(8)]

    def make_xT(ti):
        T = ti.T
        for half in range(2):
            b = bank()
            for q in range(4):
                kc = half * 4 + q
                K.op('pe', lambda: nc.tensor.transpose(PS[b][:, q * T:(q + 1) * T], xres[:T, ti.idx, kc * 128:(kc + 1) * 128],
                                                       identf[:T, :T]),
                     reads=[('x', ti.idx), 'identf'], writes=[pk(b)])
            K.op('act', lambda: nc.scalar.copy(out=xT[:, half * 4:half * 4 + 4, :T],
                                               in_=PS[b][:, 0:4 * T].rearrange("p (a t) -> p a t", a=4)),
                 reads=[pk(b)], writes=[('xT', half)])

    pending_tail = []

    def tail_step():
        if pending_tail:
            pending_tail.pop(0)()

    def tail_flush():
        while pending_tail:
            pending_tail.pop(0)()

    wr_state = {'use': 0, 'iss': 0}
    WOUT_SRC = [a_w_out[0], a_w_out[1]] + ([b_w_out[0], b_w_out[1]] if WITH_B else [])

    def wr_issue():
        c = wr_state['iss']
        if c >= len(WOUT_SRC) * NT * 8:
            return
        wr_state['iss'] += 1
        lay = c // (NT * 8)
        kc = c % 8
        K.dma('pool', wring[:, c % 4, :], WOUT_SRC[lay][kc * 128:(kc + 1) * 128, :], writes=[('wr', c % 4)])

    for _ in range(4):
        wr_issue()

    def out_proj(ti, final, ogT_ap, ogk, immediate=0):
        T = ti.T
        t = ti.idx
        st = {}

        def s1():
            b = bank()
            psb = PS[b][:, :].bitcast(BF16)
            for kc in range(8):
                K.op('pe', lambda: nc.tensor.transpose(psb[:, kc * T:(kc + 1) * T], og[:T, kc * 128:(kc + 1) * 128], identb[:T, :T]),
                     reads=['og', 'identb'], writes=[pk(b)])
            K.op('act', lambda: nc.scalar.copy(out=ogT_ap[:, :, :T], in_=psb[:, 0:8 * T].rearrange("p (a t) -> p a t", a=8)),
                 reads=[pk(b)], writes=ogk)

        def s2half(kcs):
            if 'pb' not in st:
                st['pb'] = [bank(), bank()]
                reserved.update(st['pb'])
            pb = st['pb']
            for kc in kcs:
                c = wr_state['use']
                wr_state['use'] += 1
                slot = c % 4
                for h2 in range(2):
                    K.op('pe', lambda: nc.tensor.matmul(PS[pb[h2]][:T, :], ogT_ap[:, kc, :T], wring[:, slot, h2 * 512:(h2 + 1) * 512],
                                                        start=(kc == 0), stop=(kc == 7)),
                         reads=ogk + [('wr', slot)], writes=[pk(pb[h2])])
            for _ in kcs:
                wr_issue()

        def s2a():
            s2half(range(0, 4))

        def s2b():
            s2half(range(4, 8))

        def s3():
            pb = st['pb']
            for h2 in range(2):
                rk = ONK[h2 * 4:h2 * 4 + 4]
                sl = slice(h2 * 512, (h2 + 1) * 512)
                K.op('dve', lambda: nc.vector.scalar_tensor_tensor(out=res[:T, sl], in0=xres[:T, t, sl], scalar=DN_ALPHA,
                                                                   in1=PS[pb[h2]][:T, :], op0=ALU.mult, op1=ALU.add),
                     reads=[('x', t), pk(pb[h2])], writes=rk)
                K.op('dve', lambda: nc.vector.bn_stats(out=bst[:T, h2, :], in_=res[:T, sl]), reads=rk, writes=[('bst', h2)])
            K.op('dve', lambda: nc.vector.bn_aggr(out=mv[:T, :], in_=bst[:T, :, :]), reads=[('bst', 0), ('bst', 1)], writes=['mv'])
            K.op('act', lambda: nc.scalar.activation(out=rstd[:T, :], in_=mv[:T, 1:2], func=AF.Ln, bias=epsln[:T, 0:1], scale=1.0),
                 reads=['mv', 'eps'], writes=['rstd'])
            K.op('act', lambda: nc.scalar.activation(out=rstd[:T, :], in_=rstd[:T, :], func=AF.Exp, scale=-0.5),
                 reads=['rstd'], writes=['rstd'])
            reserved.discard(st['pb'][0])
            reserved.discard(st['pb'][1])

        def s4():
            for h2 in range(2):
                rk = ONK[h2 * 4:h2 * 4 + 4]
                sl = slice(h2 * 512, (h2 + 1) * 512)
                K.op('dve', lambda: nc.vector.tensor_scalar(out=res[:T, sl], in0=res[:T, sl], scalar1=mv[:T, 0:1], scalar2=rstd[:T, 0:1],
                                                            op0=ALU.subtract, op1=ALU.mult),
                     reads=rk + ['mv', 'rstd'], writes=rk)
                K.op('pool', lambda: nc.gpsimd.tensor_tensor(out=res[:T, sl], in0=res[:T, sl], in1=lng[:T, sl], op=ALU.mult),
                     reads=rk + ['lng'], writes=rk)

        def s5():
            for h2 in range(2):
                rk = ONK[h2 * 4:h2 * 4 + 4]
                sl = slice(h2 * 512, (h2 + 1) * 512)
                K.op('pool', lambda: nc.gpsimd.tensor_tensor(out=xres[:T, t, sl], in0=res[:T, sl], in1=lnb[:T, sl], op=ALU.add),
                     reads=rk + ['lnb'], writes=[('x', t)])
            if final:
                out_stamps.append(K.dma('sp', y_d[t * 128:t * 128 + T, :], xres[:T, t, :], reads=[('x', t)], writes=[('y', t)]))
        stages = [s1, s2a, s2b, s3, s4, s5]
        for f_ in stages[:immediate]:
            f_()
        pending_tail.extend(stages[immediate:])

    def silu_from(out_ap, in_ap, tmp_ap, rkeys, wkeys, tkeys):
        P_ = tmp_ap.shape[0]
        K.op('act', lambda: nc.scalar.activation(out=tmp_ap, in_=in_ap, func=AF.Exp, scale=-1.0), reads=rkeys, writes=tkeys)
        K.op('act', lambda: nc.scalar.activation(out=tmp_ap, in_=tmp_ap, func=AF.Ln, bias=epsln[:P_, 2:3], scale=1.0), reads=tkeys + ['eps'], writes=tkeys)
        K.op('act', lambda: nc.scalar.activation(out=tmp_ap, in_=tmp_ap, func=AF.Exp, scale=-1.0), reads=tkeys, writes=tkeys)
        K.op('dve', lambda: nc.vector.tensor_tensor(out=out_ap, in0=in_ap, in1=tmp_ap, op=ALU.mult), reads=rkeys + tkeys, writes=wkeys)

    epsln = sb("epsln", [128, 4])
    K.op('pool', lambda: nc.gpsimd.memset(epsln[:, 0:1], LN_EPS), writes=['eps'])
    K.op('pool', lambda: nc.gpsimd.memset(epsln[:, 1:2], RMS_EPS), writes=['eps'])
    K.op('pool', lambda: nc.gpsimd.memset(epsln[:, 2:3], 1.0), writes=['eps'])
    K.op('pool', lambda: nc.gpsimd.memset(epsln[:, 3:4], float(np.log(128.0 ** -0.5))), writes=['eps'])
    EPS_LN, EPS_RMS, ONE_B, LOGQ = epsln[:, 0:1], epsln[:, 1:2], epsln[:, 2:3], epsln[:, 3:4]

    def a_layer(l):
        for (bname, c0, c1) in [('g', 4096, 4112), ('q0', 0, 512), ('q1', 512, 1024), ('k', 1024, 2048), ('v', 2048, 3072), ('z', 3072, 4096)]:
            K.dma('pool', wbig[:, :, c0:c1], a_w_in[l][:, c0:c1].rearrange("(kc p) n -> p kc n", p=128), writes=[('wbig', bname)])
        K.dma('sp', masks[:], masks_d[:, 0], writes=['masks'])
        K.dma('sp', convw, convw_d[l], writes=['convw'])
        K.dma('sp', alog, alog_d[l], writes=['alog'])
        K.dma('sp', dtb, dtb_d[l], writes=['dtb'])
        K.dma('sp', normw, normw_d[l], writes=['normw'])
        K.dma('sp', lng[:], lng_d[l], writes=['lng'])
        K.dma('sp', lnb[:], lnb_d[l], writes=['lnb'])
        K.op('act', lambda: nc.scalar.activation(out=nega, in_=alog, func=AF.Exp), reads=['alog'], writes=['nega'])
        K.op('dve', lambda: nc.vector.tensor_scalar(out=nega, in0=nega, scalar1=-1.0, scalar2=None, op0=ALU.mult),
             reads=['nega'], writes=['nega'])
        K.op('pool', lambda: nc.gpsimd.memset(carry, 0.0), writes=['carry'])
        K.op('pool', lambda: nc.gpsimd.memset(S, 0.0), writes=[('S', 0), ('S', 1)])
        K.op('pool', lambda: nc.gpsimd.memset(Sb, 0.0), writes=[('Sb', 0), ('Sb', 1)])
        gens = [a_tile(l, TileInfo(t), t % 2) for t in range(NT)]

        def adv(g, until):
            while True:
                v = next(g)
                if v == until:
                    return

        adv(gens[0], 'Qdone')
        adv(gens[0], 'EZdone')
        npt_ = stop_after[1] if isinstance(stop_after, tuple) else NPT
        for t in range(npt_):
            g, gn = gens[t], (gens[t + 1] if t + 1 < npt_ else None)
            hdone = False
            cnt_ = [0]
            if gn is not None:
                adv(gn, 'F')
                while True:
                    v = next(gn)
                    if v == 'Qdone':
                        break
                    cnt_[0] += 1
                    if cnt_[0] % HRATIO == 0 and (cnt_[0] % 5) not in HSKIP:
                        for _ in range(HPER):
                            if not hdone and next(g) == 'Hdone':
                                hdone = True
            while not hdone:
                hdone = next(g) == 'Hdone'
            adv(g, 'Sdone')
            if gn is not None:
                adv(gn, 'EZdone')
            tail_step()
            tail_step()
        if isinstance(stop_after, tuple):
            tail_flush()
            return True
        gs = gens[NPT]
        adv(gs, 'Sdone')
        tail_flush()
        return False

    def a_tile(l, ti, p):
        T, nseq, L, C, nch, t = ti.T, ti.nseq, ti.L, ti.C, ti.nch, ti.idx
        W = 3 + L
        qT, kT, kd, vp, gsm, glbc = qT2[p], kT2[p], kd2[p], vp2[p], gsm2[p], glbc2[p]
        if ti.samp:
            K.dma('sp', masks[:], masks_d[:, 1], writes=['masks'])
            cview = carry[:, 0:24 * 2 * 3].rearrange("p (c s j) -> p c s j", c=24, s=2)
            for s_ in range(2):
                K.dma('sp', cview[:, :, s_, :], sconv_d[l, s_], writes=['carry'])
        K.op('pool', lambda: nc.gpsimd.memset(ssq, 0.0), writes=[('ssq', g, c_) for g in range(4) for c_ in range(4)])
        make_xT(ti)
        cv4 = carry[:, 0:24 * nseq * 3].rearrange("p (c s j) -> p c s j", c=24, s=nseq)

        bg = bank()
        for kc in range(8):
            K.op('pe', lambda: nc.tensor.matmul(PS[bg][:T, 0:16], xT[:, kc, :T], wbig[:, kc, 4096:4112], start=(kc == 0), stop=(kc == 7)),
                 reads=XTK + [('wbig', 'g')], writes=[pk(bg)])
        G_ = lambda i: gsm[:T, i, :]
        K.op('dve', lambda: nc.vector.tensor_tensor(out=G_(0), in0=PS[bg][:T, 0:8], in1=dtb[:T, :], op=ALU.add),
             reads=[pk(bg), 'dtb'], writes=[('g', p, 0)])
        K.op('act', lambda: nc.scalar.activation(out=G_(0), in_=G_(0), func=AF.Exp), reads=[('g', p, 0)], writes=[('g', p, 0)])
        K.op('act', lambda: nc.scalar.activation(out=G_(0), in_=G_(0), func=AF.Ln, bias=ONE_B[:T, :], scale=1.0), reads=[('g', p, 0), 'eps'], writes=[('g', p, 0)])
        K.op('dve', lambda: nc.vector.tensor_tensor(out=G_(1), in0=G_(0), in1=nega[:T, :], op=ALU.mult), reads=[('g', p, 0), 'nega'], writes=[('g', p, 1)])
        K.op('act', lambda: nc.scalar.activation(out=G_(2), in_=PS[bg][:T, 8:16], func=AF.Exp, scale=-1.0), reads=[pk(bg)], writes=[('g', p, 2)])
        K.op('act', lambda: nc.scalar.activation(out=G_(2), in_=G_(2), func=AF.Ln, bias=ONE_B[:T, :], scale=1.0), reads=[('g', p, 2), 'eps'], writes=[('g', p, 2)])
        K.op('act', lambda: nc.scalar.activation(out=G_(4), in_=G_(2), func=AF.Exp, scale=-0.5), reads=[('g', p, 2)], writes=[('g', p, 4)])
        bG = bank()
        K.op('pe', lambda: nc.tensor.matmul(PS[bG][:T, 0:8], masks[:T, 3, :T], G_(1), start=True, stop=True),
             reads=['masks', ('g', p, 1)], writes=[pk(bG)])
        K.op('pe', lambda: nc.tensor.matmul(PS[bG][:T, 8:16], masks[:T, 4, :T], G_(1), start=True, stop=True),
             reads=['masks', ('g', p, 1)], writes=[pk(bG)])
        K.op('act', lambda: nc.scalar.copy(out=G_(5), in_=PS[bG][:T, 0:8]), reads=[pk(bG)], writes=[('g', p, 5)])
        K.op('dve', lambda: nc.vector.tensor_tensor(out=G_(6), in0=PS[bG][:T, 8:16], in1=G_(5), op=ALU.subtract),
             reads=[pk(bG), ('g', p, 5)], writes=[('g', p, 6)])
        K.op('act', lambda: nc.scalar.activation(out=G_(6), in_=G_(6), func=AF.Exp), reads=[('g', p, 6)], writes=[('g', p, 6)])
        K.op('act', lambda: nc.scalar.activation(out=G_(7), in_=G_(5), func=AF.Exp), reads=[('g', p, 5)], writes=[('g', p, 7)])
        K.op('dve', lambda: nc.vector.tensor_scalar(out=G_(8), in0=G_(7), scalar1=-1.0, scalar2=None, op0=ALU.mult),
             reads=[('g', p, 7)], writes=[('g', p, 8)])
        yield 'F'
        qkv_banks = {}

        def qkv_proj(grp):
            b = bank()
            reserved.add(b)
            qkv_banks[grp] = b
            for c4 in range(4):
                ct = grp * 4 + c4
                for kc in range(8):
                    K.op('pe', lambda: nc.tensor.matmul(PS[b][:, c4 * T:(c4 + 1) * T], wbig[:, kc, ct * 128:(ct + 1) * 128], xT[:, kc, :T],
                                                        start=(kc == 0), stop=(kc == 7)),
                         reads=XTK + [('wbig', ['q0', 'q1', 'k', 'k', 'v', 'v'][grp])], writes=[pk(b)])
        cbk = 'cb'
        cbv = cb[0][:, 0:4 * nseq * W].rearrange("p (c s w) -> p c s w", c=4, s=nseq)
        acck = [('acc', i) for i in range(4)]
        junk = tmb[:, :, :].rearrange("p a b -> p (a b)").bitcast(F32)

        def st1(grp):
            b = qkv_banks[grp]
            K.op('pool', lambda: nc.gpsimd.tensor_copy(out=cbv[:, :, :, 0:3], in_=cv4[:, grp * 4:grp * 4 + 4, :, :]),
                 reads=['carry'], writes=[cbk])
            K.op('act', lambda: nc.scalar.copy(out=cbv[:, :, :, 3:3 + L],
                                               in_=PS[b][:, 0:4 * T].rearrange("p (c s w) -> p c s w", c=4, s=nseq)),
                 reads=[pk(b)], writes=[cbk])
            reserved.discard(b)
            K.op('pool', lambda: nc.gpsimd.tensor_copy(out=cv4[:, grp * 4:grp * 4 + 4, :, :], in_=cbv[:, :, :, L:L + 3]),
                 reads=[cbk], writes=['carry'])

        def st2(grp):
            avs = [acc[:, c4, 0:T].rearrange("p (s w) -> p s w", s=nseq) for c4 in range(4)]
            for c4 in range(4):
                ct = grp * 4 + c4
                K.op('dve', lambda: nc.vector.tensor_scalar(out=avs[c4], in0=cbv[:, c4, :, 0:L], scalar1=convw[:, ct, 0:1], scalar2=None, op0=ALU.mult),
                     reads=[cbk, 'convw'], writes=[('acc', c4)])
            for j in range(1, 4):
                for c4 in range(4):
                    ct = grp * 4 + c4
                    K.op('dve', lambda: nc.vector.scalar_tensor_tensor(out=avs[c4], in0=cbv[:, c4, :, j:j + L], scalar=convw[:, ct, j:j + 1], in1=avs[c4],
                                                                       op0=ALU.mult, op1=ALU.add),
                         reads=[cbk, 'convw', ('acc', c4)], writes=[('acc', c4)])

        def st3(grp):
            silu_from(ebuf[:, :, :T], acc[:, :, :T], ebuf[:, :, :T], acck, ['ebuf'], ['ebuf'])

        st4 = {}
        st4c_b = {}

        def st4c(grp):
            if grp < 0 or grp >= 4:
                return
            b3 = bank()
            reserved.add(b3)
            st4c_b[grp] = b3
            psb = PS[b3][:, :].bitcast(BF16)
            for c4 in range(4):
                K.op('pe', lambda: nc.tensor.transpose(psb[:, c4 * T:(c4 + 1) * T], tmb[:T, c4, :], identb[:T, :T]),
                     reads=['tmb', 'identb'], writes=[pk(b3)])

        def st4d(grp):
            if grp < 0 or grp >= 4:
                return
            b3 = st4c_b[grp]
            psb = PS[b3][:, :].bitcast(BF16)
            isq = grp < 2
            h0 = (grp % 2) * 4
            dst = qT if isq else kT
            dk_ = ('qT' if isq else 'kT', p, grp % 2)
            K.op('act', lambda: nc.scalar.copy(out=dst[:, h0:h0 + 4, :T], in_=psb[:, 0:4 * T].rearrange("p (a t) -> p a t", a=4)),
                 reads=[pk(b3)], writes=[dk_])
            reserved.discard(b3)

        def st4a(grp):
            b2 = bank()
            reserved.add(b2)
            st4[grp] = b2
            for c4 in range(4):
                K.op('pe', lambda: nc.tensor.transpose(PS[b2][:T, c4 * 128:(c4 + 1) * 128], ebuf[:, c4, :T], identf[:, :]),
                     reads=['ebuf', 'identf'], writes=[pk(b2)])
            h0 = (grp % 2) * 4
            if grp < 4:
                isq = grp < 2
                col0 = (0 if isq else 8) + h0
                for c4 in range(4):
                    K.op('act', lambda: nc.scalar.activation(out=junk[:T, (c4 % 2) * 128:(c4 % 2) * 128 + 128], in_=PS[b2][:T, c4 * 128:(c4 + 1) * 128], func=AF.Square,
                                                             accum_out=ssq[:T, col0 + c4:col0 + c4 + 1]),
                         reads=[pk(b2)], writes=[('ssq', grp, c4), ('junk', c4 % 2)] + (['tmb'] if c4 < 2 else []))
                K.op('act', lambda: nc.scalar.activation(out=rn[:T, col0:col0 + 4], in_=ssq[:T, col0:col0 + 4], func=AF.Ln, bias=EPS_RMS[:T, :], scale=1.0),
                     reads=[('ssq', grp, c_) for c_ in range(4)] + ['eps'], writes=[('rn', grp)])
                if isq:
                    K.op('act', lambda: nc.scalar.activation(out=rn[:T, col0:col0 + 4], in_=rn[:T, col0:col0 + 4], func=AF.Exp, scale=-0.5,
                                                             bias=LOGQ[:T, :]),
                         reads=[('rn', grp), 'eps'], writes=[('rn', grp)])
                else:
                    K.op('act', lambda: nc.scalar.activation(out=rn[:T, col0:col0 + 4], in_=rn[:T, col0:col0 + 4], func=AF.Exp, scale=-0.5),
                         reads=[('rn', grp)], writes=[('rn', grp)])

        def st4b(grp):
            b2 = st4[grp]
            h0 = (grp % 2) * 4
            pv3 = PS[b2][:T, :].rearrange("p (h d) -> p h d", h=4)
            if grp < 4:
                isq = grp < 2
                col0 = (0 if isq else 8) + h0
                if isq:
                    scl = rn[:T, col0:col0 + 4]
                    sk_ = [('rn', grp)]
                else:
                    K.op('dve', lambda: nc.vector.tensor_tensor(out=sc_k[:T, h0:h0 + 4], in0=rn[:T, col0:col0 + 4], in1=gsm[:T, 4, h0:h0 + 4], op=ALU.mult),
                         reads=[('rn', grp), ('g', p, 4)], writes=[('sck', grp)])
                    scl = sc_k[:T, h0:h0 + 4]
                    sk_ = [('sck', grp)]
                K.op('dve', lambda: nc.vector.tensor_tensor(out=tmb[:T, :, :], in0=pv3, in1=scl.unsqueeze(2).broadcast_to([T, 4, 128]), op=ALU.mult),
                     reads=[pk(b2)] + sk_, writes=['tmb'])
                reserved.discard(b2)
                if not isq:
                    K.op('pool', lambda: nc.gpsimd.tensor_tensor(out=kd[:T, h0:h0 + 4, :], in0=tmb[:T, :, :],
                                                                 in1=gsm[:T, 6, h0:h0 + 4].unsqueeze(2).broadcast_to([T, 4, 128]), op=ALU.mult),
                         reads=['tmb', ('g', p, 6)], writes=[('kd', p, grp % 2)])
            else:
                K.op('dve', lambda: nc.vector.tensor_tensor(out=vp[:T, h0:h0 + 4, :], in0=pv3,
                                                            in1=gsm[:T, 4, h0:h0 + 4].unsqueeze(2).broadcast_to([T, 4, 128]), op=ALU.mult),
                     reads=[pk(b2), ('g', p, 4)], writes=[('vp', p, grp % 2)])
                reserved.discard(b2)

        qkv_proj(0)
        qkv_proj(1)
        st1(0)
        st2(0)
        for grp in range(6):
            st4c(grp - 1)
            if grp + 2 < 6:
                qkv_proj(grp + 2)
            if grp + 1 < 6:
                st1(grp + 1)
            yield 'Qit'
            st3(grp)
            st4d(grp - 1)
            yield 'Qit'
            st4a(grp)
            yield 'Qit'
            if grp + 1 < 6:
                st2(grp + 1)
            yield 'Qit'
            st4b(grp)
            tail_step()
            yield 'Qit'
        if ti.idx == NPT - 1:
            out_stamps.append(K.dma('sp', pconv_d[l], cv4[:, :, 0, :], reads=['carry'], writes=[('pconv', l)]))
        if ti.samp:
            for s_ in range(2):
                out_stamps.append(K.dma('sp', sconvo_d[l, s_], cv4[:, :, s_, :], reads=['carry'], writes=[('sconvo', l, s_)]))
        tail_flush()
        yield 'Qdone'
        cbf = cb[0]
        for h2 in range(2):
            b = bank()
            for kc in range(8):
                K.op('pe', lambda: nc.tensor.matmul(PS[b][:T, :], xT[:, kc, :T], wbig[:, kc, 3072 + h2 * 512:3072 + (h2 + 1) * 512],
                                                    start=(kc == 0), stop=(kc == 7)),
                     reads=XTK + [('wbig', 'z')], writes=[pk(b)])
            silu_from(og[:T, h2 * 512:(h2 + 1) * 512], PS[b][:T, :], cbf[:T, 0:512], [pk(b)], ['og'], ['cb'])
        yield 'EZdone'
        nlev = 5 if C == 64 else 4
        first_reg = [True]

        def regkeys():
            if first_reg[0]:
                first_reg[0] = False
                return [], ['REG']
            return ['REG'], []
        DTi4 = REG[:, 0:512].rearrange("p (h c) -> p h c", h=4)
        Ds4 = REG[:, 512:1024].rearrange("p (h c) -> p h c", h=4)
        hstate = {}
        AK = [('Aq', i) for i in range(4)]
        BK = [('Bq', i) for i in range(4)]
        PK = [('Pq', i) for i in range(4)]

        def h_prep(gq):
            hs = slice(gq * 4, gq * 4 + 4)
            rr, rw = regkeys()
            K.op('dve', lambda: nc.vector.tensor_tensor(out=DTi4[:T, :, :T], in0=identf[:T, :T].unsqueeze(1).broadcast_to([T, 4, T]),
                                                        in1=gsm[:T, 5, hs].unsqueeze(2).broadcast_to([T, 4, T]), op=ALU.mult),
                 reads=['identf', ('g', p, 5)] + rr, writes=['DTi'] + rw)
            bgq = bank()
            reserved.add(bgq)
            K.op('pe', lambda: nc.tensor.matmul(PS[bgq][:, 0:4 * T], onesf[:T, :], DTi4[:T, :, :T], start=True, stop=True),
                 reads=['onesf', 'DTi', 'REG'], writes=[pk(bgq)])
            gv = PS[bgq][:, 0:4 * T].rearrange("p (h c j) -> p h c j", h=4, c=nch)
            K.op('act', lambda: nc.scalar.activation(out=glbc[:, hs, :nch], in_=gv[:, :, :, C - 1], func=AF.Exp),
                 reads=[pk(bgq)], writes=[('glbc', p, gq)])
            gps = PS[bgq][:T, 0:4 * T].rearrange("p (h c) -> p h c", h=4)
            K.op('dve', lambda: nc.vector.tensor_tensor(out=DTi4[:T, :, :T], in0=gps, in1=gsm[:T, 5, hs].unsqueeze(2).broadcast_to([T, 4, T]), op=ALU.subtract),
                 reads=[pk(bgq), ('g', p, 5), 'REG'], writes=['DTi'])
            reserved.discard(bgq)
            K.op('dve', lambda: nc.vector.tensor_tensor(out=Ds4[:T, :, :T], in0=DTi4[:T, :, :T], in1=masks[:T, 1, :T].unsqueeze(1).broadcast_to([T, 4, T]), op=ALU.subtract),
                 reads=['DTi', 'masks', 'REG'], writes=['Ds'])
            K.op('dve', lambda: nc.vector.tensor_tensor(out=DTi4[:T, :, :T], in0=DTi4[:T, :, :T], in1=masks[:T, 0, :T].unsqueeze(1).broadcast_to([T, 4, T]), op=ALU.add),
                 reads=['DTi', 'masks', 'REG'], writes=['DTi'])
            K.op('act', lambda: nc.scalar.activation(out=Ds4[:T, :, :T], in_=Ds4[:T, :, :T], func=AF.Exp, scale=-1.0), reads=['Ds', 'REG'], writes=['Ds'])
            K.op('act', lambda: nc.scalar.activation(out=DTi4[:T, :, :T], in_=DTi4[:T, :, :T], func=AF.Exp), reads=['DTi', 'REG'], writes=['DTi'])
            bq = bank2()
            for h4 in range(4):
                h = gq * 4 + h4
                K.op('pe', lambda: nc.tensor.matmul(PS2(bq)[:T, h4 * 256:h4 * 256 + T], kT[:, h, :T], kT[:, h, :T], start=True, stop=True),
                     reads=[('kT', p, gq)], writes=[pk(bq), pk(bq + 1)])
                K.op('pe', lambda: nc.tensor.matmul(PS2(bq)[:T, h4 * 256 + T:h4 * 256 + 2 * T], kT[:, h, :T], qT[:, h, :T], start=True, stop=True),
                     reads=[('kT', p, gq), ('qT', p, gq)], writes=[pk(bq), pk(bq + 1)])
            pq = PS2(bq)[:T, :].rearrange("p (h c) -> p h c", h=4)
            K.op('dve', lambda: nc.vector.tensor_tensor(out=intraT[:T, hs, :T], in0=pq[:, :, T:2 * T], in1=DTi4[:T, :, :T], op=ALU.mult),
                 reads=[pk(bq), pk(bq + 1), 'DTi', 'REG'], writes=[('intraT', h_) for h_ in range(gq * 4, gq * 4 + 4)])
            hstate[gq] = (bq, pq)
            reserved.update([bq, bq + 1])

        def h_fin(gq):
            bq, pq = hstate[gq]
            K.op('dve', lambda: nc.vector.scalar_tensor_tensor(out=Aq[:T, :, :T], in0=pq[:, :, 0:T], scalar=-1.0, in1=Ds4[:T, :, :T], op0=ALU.mult, op1=ALU.mult),
                 reads=[pk(bq), pk(bq + 1), 'Ds', 'REG'], writes=AK)
            K.op('dve', lambda: nc.vector.scalar_tensor_tensor(out=BPq[:T, :, 0:T], in0=pq[:, :, 0:T], scalar=-1.0, in1=DTi4[:T, :, :T], op0=ALU.mult, op1=ALU.mult),
                 reads=[pk(bq), pk(bq + 1), 'DTi', 'REG'], writes=BK)
            reserved.discard(bq)
            reserved.discard(bq + 1)
            K.op('pool', lambda: nc.gpsimd.tensor_tensor(out=BPq[:T, :, 0:T], in0=BPq[:T, :, 0:T].bitcast(F32), in1=masks[:T, 2, :T].unsqueeze(1).broadcast_to([T, 4, T]), op=ALU.mult),
                 reads=BK + ['masks'], writes=BK)
            K.op('pool', lambda: nc.gpsimd.tensor_tensor(out=BPq[:T, :, T:2 * T], in0=BPq[:T, :, 0:T].bitcast(F32), in1=identf[:T, :T].unsqueeze(1).broadcast_to([T, 4, T]), op=ALU.add),
                 reads=BK + ['identf'], writes=PK)

        def h_chain(gq, hook=None):
            for lev in range(nlev + 1):
                last = lev == nlev
                if lev == 1 and hook is not None:
                    hook()
                if not last:
                    b2 = bank2()
                    b3 = bank()
                    for h4 in range(4):
                        ncols = T if lev == 0 else 2 * T
                        K.op('pe', lambda: nc.tensor.matmul(PS2(b2)[:T, h4 * 256:h4 * 256 + ncols], Aq[:T, h4, :T], BPq[:T, h4, 0:ncols], start=True, stop=True),
                             reads=[AK[h4], BK[h4]] + ([PK[h4]] if lev > 0 else []), writes=[pk(b2), pk(b2 + 1)])
                        K.op('pe', lambda: nc.tensor.matmul(PS[b3][:T, h4 * T:(h4 + 1) * T], BPq[:T, h4, 0:T], Aq[:T, h4, :T], start=True, stop=True),
                             reads=[AK[h4], BK[h4]], writes=[pk(b3)])
                    if SPLITLEV:
                        reserved.update([b2, b2 + 1, b3])
                        yield 'lev'
                        reserved.difference_update([b2, b2 + 1, b3])
                    pv = PS2(b2)[:T, :].rearrange("p (h c) -> p h c", h=4)
                    if lev > 0:
                        K.op('dve', lambda: nc.vector.tensor_tensor(out=BPq[:T, :, T:2 * T], in0=pv[:, :, T:2 * T], in1=BPq[:T, :, T:2 * T].bitcast(F32), op=ALU.add),
                             reads=[pk(b2), pk(b2 + 1)] + PK, writes=PK)
                    K.op('act', lambda: nc.scalar.copy(out=BPq[:T, :, 0:T], in_=pv[:, :, 0:T]), reads=[pk(b2), pk(b2 + 1)], writes=BK)
                    K.op('act', lambda: nc.scalar.copy(out=Aq[:T, :, :T], in_=PS[b3][:T, 0:4 * T].rearrange("p (h c) -> p h c", h=4)),
                         reads=[pk(b3)], writes=AK)
                    yield 'lev'
                else:
                    b3 = bank()
                    for h4 in range(4):
                        K.op('pe', lambda: nc.tensor.matmul(PS[b3][:T, h4 * T:(h4 + 1) * T], Aq[:T, h4, :T], BPq[:T, h4, T:2 * T], start=True, stop=True),
                             reads=[AK[h4], PK[h4]], writes=[pk(b3)])
                    K.op('dve', lambda: nc.vector.tensor_tensor(out=XT[:T, gq * 4:gq * 4 + 4, :T], in0=PS[b3][:T, 0:4 * T].rearrange("p (h c) -> p h c", h=4),
                                                                in1=BPq[:T, :, T:2 * T].bitcast(F32), op=ALU.add),
                         reads=[pk(b3)] + PK, writes=[('XT', gq)])
                    yield 'lev'

        h_prep(0)
        yield 'H'
        h_fin(0)
        yield 'H'
        for _ in h_chain(0, hook=lambda: h_prep(1)):
            yield 'H'
        h_fin(1)
        yield 'H'
        for _ in h_chain(1):
            yield 'H'
        yield 'Hdone'
        first_scan = [True]
        tail_flush()
        K.op('pool', lambda: nc.gpsimd.memset(oss, 0.0), writes=['oss'] + [('oss', h_) for h_ in range(8)])
        for c in range(nch):
            r0 = c * C
            rs = slice(r0, r0 + C)
            if ti.samp:
                for h in range(8):
                    K.dma('sp', S[:, h, :], sdel_d[l, c, h], writes=[('S', h // 4)])
                for gq in range(2):
                    K.op('act', lambda: nc.scalar.copy(out=Sb[:, gq * 4:gq * 4 + 4, :], in_=S[:, gq * 4:gq * 4 + 4, :]), reads=[('S', gq)], writes=[('Sb', gq)])
            GQ = (0, 1)
            hsl = [slice(gq * 4, gq * 4 + 4) for gq in GQ]
            bk_ = {}
            for gq in GQ:
                ba, bb_ = bank(), bank()
                bk_[('a', gq)], bk_[('b', gq)] = ba, bb_
                for h4 in range(4):
                    h = gq * 4 + h4
                    K.op('pe', lambda: nc.tensor.matmul(PS[ba][:T, h4 * 128:(h4 + 1) * 128], kT[:, h, :T], Sb[:, h, :], start=True, stop=True),
                         reads=[('kT', p, gq), ('Sb', gq)], writes=[pk(ba)])
                for h4 in range(4):
                    h = gq * 4 + h4
                    K.op('pe', lambda: nc.tensor.matmul(PS[bb_][:T, h4 * 128:(h4 + 1) * 128], qT[:, h, :T], Sb[:, h, :], start=True, stop=True),
                         reads=[('qT', p, gq), ('Sb', gq)], writes=[pk(bb_)])
            for gq in GQ:
                ba, bb_ = bk_[('a', gq)], bk_[('b', gq)]
                for h4 in range(4):
                    h = gq * 4 + h4
                    if first_scan[0]:
                        first_scan[0] = False
                        rr, rw = [], ['REG']
                    else:
                        rr, rw = ['REG'], []
                    K.op('dve', lambda: nc.vector.scalar_tensor_tensor(out=Rp[gq][rs, h4, :], in0=PS[ba][rs, h4 * 128:(h4 + 1) * 128], scalar=gsm[rs, 8, h:h + 1],
                                                                       in1=vp[rs, h, :], op0=ALU.mult, op1=ALU.add),
                         reads=[pk(ba), ('g', p, 8), ('vp', p, gq)] + rr, writes=[('Rp', gq, h4)] + rw)
                K.op('dve', lambda: nc.vector.tensor_tensor(out=on[rs, hsl[gq], :], in0=PS[bb_][rs, :].rearrange("p (h d) -> p h d", h=4),
                                                            in1=gsm[rs, 7, hsl[gq]].unsqueeze(2).broadcast_to([C, 4, 128]), op=ALU.mult),
                     reads=[pk(bb_), ('g', p, 7)], writes=ONK[gq * 4:gq * 4 + 4])
            for gq in GQ:
                bc_ = bank()
                bk_[('c', gq)] = bc_
                for h4 in range(4):
                    h = gq * 4 + h4
                    K.op('pe', lambda: nc.tensor.matmul(PS[bc_][:T, h4 * 128:(h4 + 1) * 128], XT[rs, h, :T], Rp[gq][rs, h4, :], start=True, stop=True),
                         reads=[('XT', gq), ('Rp', gq, h4), 'REG'], writes=[pk(bc_)])
            for gq in GQ:
                bc_ = bk_[('c', gq)]
                K.op('act', lambda: nc.scalar.copy(out=Yb[gq][rs, :, :], in_=PS[bc_][rs, :].rearrange("p (h d) -> p h d", h=4)),
                     reads=[pk(bc_), 'REG'], writes=[('Yb', gq)])
                K.op('pool', lambda: nc.gpsimd.tensor_tensor(out=S[:, hsl[gq], :], in0=S[:, hsl[gq], :], in1=glbc[:, hsl[gq], c:c + 1].broadcast_to([128, 4, 128]), op=ALU.mult),
                     reads=[('S', gq), ('glbc', p, gq)], writes=[('S', gq)])
            for gq in GQ:
                bd, be = bank(), bank()
                bk_[('d', gq)], bk_[('e', gq)] = bd, be
                for h4 in range(4):
                    h = gq * 4 + h4
                    K.op('pe', lambda: nc.tensor.matmul(PS[be][:, h4 * 128:(h4 + 1) * 128], kd[rs, h, :], Yb[gq][rs, h4, :], start=True, stop=True),
                         reads=[('kd', p, gq), ('Yb', gq), 'REG'], writes=[pk(be)])
                for h4 in range(4):
                    h = gq * 4 + h4
                    K.op('pe', lambda: nc.tensor.matmul(PS[bd][:T, h4 * 128:(h4 + 1) * 128], intraT[rs, h, :T], Yb[gq][rs, h4, :], start=True, stop=True),
                         reads=[('intraT', h), ('Yb', gq), 'REG'], writes=[pk(bd)])
            for gq in GQ:
                bd, be = bk_[('d', gq)], bk_[('e', gq)]
                K.op('dve', lambda: nc.vector.tensor_tensor(out=S[:, hsl[gq], :], in0=PS[be][:, :].rearrange("p (h d) -> p h d", h=4), in1=S[:, hsl[gq], :], op=ALU.add),
                     reads=[pk(be), ('S', gq)], writes=[('S', gq)])
                K.op('act', lambda: nc.scalar.copy(out=Sb[:, hsl[gq], :], in_=S[:, hsl[gq], :]), reads=[('S', gq)], writes=[('Sb', gq)])
                K.op('dve', lambda: nc.vector.tensor_tensor(out=on[rs, hsl[gq], :], in0=PS[bd][rs, :].rearrange("p (h d) -> p h d", h=4), in1=on[rs, hsl[gq], :], op=ALU.add),
                     reads=[pk(bd)] + ONK[gq * 4:gq * 4 + 4], writes=ONK[gq * 4:gq * 4 + 4])
            if ti.samp:
                for h in range(8):
                    out_stamps.append(K.dma('sp', sdelo_d[l, c, h], S[:, h, :], reads=[('S', h // 4)], writes=[('sdelo', l, c, h)]))
        if ti.idx == NPT - 1:
            for h in range(8):
                out_stamps.append(K.dma('sp', pdel_d[l, h], S[:, h, :], reads=[('S', h // 4)], writes=[('pdel', l, h)]))
        for h in range(8):
            K.op('act', lambda: nc.scalar.activation(out=tmb[:, :, :].rearrange("p a b -> p (a b)").bitcast(F32)[:T, (h % 2) * 128:(h % 2) * 128 + 128], in_=on[:T, h, :], func=AF.Square, accum_out=oss[:T, h:h + 1]),
                 reads=[('on', h)], writes=[('oss', h), ('junk', h % 2)] + (['tmb'] if h < 2 else []))
        K.op('act', lambda: nc.scalar.activation(out=oss[:T, :], in_=oss[:T, :], func=AF.Ln, bias=EPS_RMS[:T, :], scale=1.0 / 128.0),
             reads=[('oss', h_) for h_ in range(8)] + ['eps'], writes=['oss'])
        K.op('act', lambda: nc.scalar.activation(out=oss[:T, :], in_=oss[:T, :], func=AF.Exp, scale=-0.5), reads=['oss'], writes=['oss'])
        K.op('dve', lambda: nc.vector.tensor_tensor(out=on[:T, :, :], in0=on[:T, :, :], in1=oss[:T, :].unsqueeze(2).broadcast_to([T, 8, 128]), op=ALU.mult),
             reads=ONK + ['oss'], writes=ONK)
        K.op('dve', lambda: nc.vector.tensor_tensor(out=on[:T, :, :], in0=on[:T, :, :], in1=normw[:T, :].unsqueeze(1).broadcast_to([T, 8, 128]), op=ALU.mult),
             reads=ONK + ['normw'], writes=ONK)
        for h2 in range(2):
            sl = slice(h2 * 512, (h2 + 1) * 512)
            K.op('dve', lambda: nc.vector.tensor_tensor(out=og[:T, sl], in0=on[:T, h2 * 4:h2 * 4 + 4, :].rearrange("p h d -> p (h d)"), in1=og[:T, sl], op=ALU.mult),
                 reads=ONK + ['og'], writes=['og'])
        out_proj(ti, False, XT, [('XT', 0), ('XT', 1)], immediate=2)
        yield 'Sdone'

    def b_setup():
        K.full_sync()
        apos[0] = 0
        B = {}
        wflat = wbig[:, :, :].rearrange("p a b -> p (a b)")
        B['wflat'] = wflat
        B['KT'] = wflat[0:65, 20480:20480 + 4 * 2112].rearrange("p (g t) -> p g t", g=4)
        B['KTc'] = wflat[0:65, 28928:28928 + 1024].rearrange("p (g s t) -> p g s t", g=4, s=2)
        B['VEc'] = wflat[:, 29952:29952 + 520].rearrange("p (s g d) -> p s g d", s=2, g=4)
        B['PTn'] = wflat[0:64, 30472:30472 + 256].rearrange("p (h q) -> p h q", h=4)
        B['ogT'] = wflat[:, 30728:30728 + 1024].rearrange("p (a t) -> p a t", a=8)
        B['VE'] = ar([128, NT, 4, 65], BF16)
        B['kvf'] = ar([128, 512])
        B['kext'] = ar([128, 4, 65], BF16)
        B['qf'] = ar([128, 16, 64])
        B['qext'] = ar([128, 16, 65], BF16)
        B['QT'] = ar([128, 16, 128], BF16)
        B['rope'] = ar([128, NT, 2, 8])
        B['PTp'] = [ar([128, 4, 128], BF16) for _ in range(2)]
        B['PTc'] = [ar([128, 4, 128], BF16) for _ in range(2)]
        B['PTs'] = [[ar([128, 4, 64], BF16) for _ in range(2)] for _ in range(2)]
        B['ztz'] = ar([128, 1024])
        B['zt'] = B['ztz'][:, 0:512]
        B['zs'] = B['ztz'][:, 512:1024]
        B['sum8'] = ar([128, NT + 3])
        B['ksm'] = ar([128, 8, 4])
        B['qsm'] = ar([128, 6, 16])
        B['nshb'] = ar([128, 16], BF16)
        B['sinkb'] = ar([128, 16])
        B['den'] = ar([128, 2, 4])
        B['zsb'] = ar([128, D], BF16)
        print("arena used (B):", apos[0])
        return B

    def b_layer(j, B):
        wflat = B['wflat']
        wb3 = wflat[:, 0:16384].rearrange("p (kc n) -> p kc n", kc=8)
        if j == 0:
            K.dma('pool', wflat[:, 16384:16384 + 4096].rearrange("p (kc n) -> p kc n", kc=8), b_w_kv.rearrange("(kc p) n -> p kc n", p=128), writes=['wkv'])
        for (bname, c0, c1) in [('q0', 0, 512), ('q1', 512, 1024), ('z', 1024, 2048)]:
            K.dma('pool', wb3[:, :, c0:c1], b_w_in[j][:, c0:c1].rearrange("(kc p) n -> p kc n", p=128), writes=[('wbin', bname)])
        K.dma('sp', lng[:], lng_d[2 + j], writes=['lng'])
        K.dma('sp', lnb[:], lnb_d[2 + j], writes=['lnb'])
        K.dma('sp', B['sinkb'], sink_d[j], writes=['sinkb'])
        if j == 0:
            b_init(B)
        for t in range(NT):
            b_tile(j, TileInfo(t), B)
        tail_flush()

    def ksum8(B, kv3, T, col, scratch):
        ksm = B['ksm']
        scratch = B['zt'][:, 0:256].rearrange("p (g d) -> p g d", g=4)
        K.op('dve', lambda: nc.vector.tensor_tensor(out=scratch[:T, :, :], in0=kv3, in1=kv3, op=ALU.mult),
             reads=['kvf'], writes=['zt'])
        K.op('dve', lambda: nc.vector.tensor_reduce(out=ksm[:T, 0, :], in_=scratch[:T, :, :], axis=AX.X, op=ALU.add),
             reads=['zt'], writes=['ksm'])
        K.op('dve', lambda: nc.vector.tensor_reduce(out=ksm[:T, 1, 0:1], in_=ksm[:T, 0, :], axis=AX.X, op=ALU.max),
             reads=['ksm'], writes=['ksm'])
        K.op('dve', lambda: nc.vector.tensor_tensor(out=ksm[:T, 2, 0:1], in0=ksm[:T, 1, 0:1], in1=ksm[:T, 1, 0:1], op=ALU.mult),
             reads=['ksm'], writes=['ksm'])
        K.op('dve', lambda: nc.vector.tensor_tensor(out=ksm[:T, 3, 0:1], in0=ksm[:T, 2, 0:1], in1=ksm[:T, 2, 0:1], op=ALU.mult),
             reads=['ksm'], writes=['ksm'])
        b = bank()
        K.op('pe', lambda: nc.tensor.matmul(PS[b][:, 0:1], onesf[:T, :], ksm[:T, 3, 0:1], start=True, stop=True),
             reads=['onesf', 'ksm'], writes=[pk(b)])
        K.op('act', lambda: nc.scalar.copy(out=B['sum8'][:, col:col + 1], in_=PS[b][:, 0:1]), reads=[pk(b)], writes=[('sum8', col)])

    def kt_store(B, T, dst_fn, rkeys, wkey):
        b = bank()
        psb = PS[b][:, :].bitcast(BF16)
        for g in range(4):
            K.op('pe', lambda: nc.tensor.transpose(psb[0:65, g * T:(g + 1) * T], B['kext'][:T, g, :], identb[:T, :T]),
                 reads=['kext', 'identb'], writes=[pk(b)])
        for g in range(4):
            K.op('act', lambda: nc.scalar.copy(out=dst_fn(g), in_=psb[0:65, g * T:(g + 1) * T]), reads=[pk(b)], writes=[wkey])

    def b_init(B):
        K.dma('sp', B['rope'], rope_d, writes=['rope'])
        K.op('pool', lambda: nc.gpsimd.memset(B['kext'][:, :, 64:65], 1.0), writes=['kext'])
        K.op('pool', lambda: nc.gpsimd.memset(B['VE'][:, :, :, 64:65], 1.0), writes=['VE1'])
        K.op('pool', lambda: nc.gpsimd.memset(B['VEc'][:, :, :, 64:65], 1.0), writes=['VEc'])
        for i in range(2):
            K.op('pool', lambda: nc.gpsimd.memset(B['PTp'][i], 0.0), writes=[('PTp', i)])
            K.op('pool', lambda: nc.gpsimd.memset(B['PTc'][i], 0.0), writes=[('PTc', i)])
            for s_ in range(2):
                K.op('pool', lambda: nc.gpsimd.memset(B['PTs'][i][s_], 0.0), writes=[('PTs', i, s_)])
        K.op('pool', lambda: nc.gpsimd.memset(B['PTn'], 0.0), writes=['PTn'])
        K.op('pool', lambda: nc.gpsimd.memset(B['sum8'], 0.0), writes=[('sum8', c) for c in range(NT + 3)])
        ckf = B['qf'][:, 0:8, :].rearrange("p a b -> p (a b)")
        cvf = B['qf'][:, 8:16, :].rearrange("p a b -> p (a b)")
        for s_ in range(2):
            K.dma('sp', ckf[:, s_ * 256:(s_ + 1) * 256], ck_d[s_], writes=['qf'])
            K.dma('sp', cvf[:, s_ * 256:(s_ + 1) * 256], cv_d[s_], writes=['qf'])
        for s_ in range(2):
            out_stamps.append(K.dma('sp', sk_d[s_, 0:96, :], ckf[32:128, s_ * 256:(s_ + 1) * 256], reads=['qf'], writes=[('sk', s_, 0)]))
            out_stamps.append(K.dma('sp', sv_d[s_, 0:96, :], cvf[32:128, s_ * 256:(s_ + 1) * 256], reads=['qf'], writes=[('sv', s_, 0)]))
            K.op('act', lambda: nc.scalar.copy(out=B['kext'][:, :, 0:64], in_=ckf[:, s_ * 256:(s_ + 1) * 256].rearrange("p (g d) -> p g d", g=4)),
                 reads=['qf'], writes=['kext'])
            kt_store(B, 128, lambda g: B['KTc'][:, g, s_, :], ['kext'], 'KTc')
            K.op('act', lambda: nc.scalar.copy(out=B['VEc'][:, s_, :, 0:64], in_=cvf[:, s_ * 256:(s_ + 1) * 256].rearrange("p (g d) -> p g d", g=4)),
                 reads=['qf'], writes=['VEc'])
        ksm = B['ksm']
        scr = on
        kv8 = ckf.rearrange("p (a d) -> p a d", a=8)
        K.op('dve', lambda: nc.vector.tensor_tensor(out=scr[:, :, 0:64], in0=kv8, in1=kv8, op=ALU.mult), reads=['qf'], writes=ONK)
        K.op('dve', lambda: nc.vector.tensor_reduce(out=ksm[:, 4:6, :].rearrange("p a b -> p (a b)"), in_=scr[:, :, 0:64], axis=AX.X, op=ALU.add),
             reads=ONK, writes=['ksm'])
        K.op('dve', lambda: nc.vector.tensor_reduce(out=ksm[:, 1, 0:1], in_=ksm[:, 4:6, :].rearrange("p a b -> p (a b)"), axis=AX.X, op=ALU.max),
             reads=['ksm'], writes=['ksm'])
        K.op('dve', lambda: nc.vector.tensor_tensor(out=ksm[:, 2, 0:1], in0=ksm[:, 1, 0:1], in1=ksm[:, 1, 0:1], op=ALU.mult), reads=['ksm'], writes=['ksm'])
        K.op('dve', lambda: nc.vector.tensor_tensor(out=ksm[:, 3, 0:1], in0=ksm[:, 2, 0:1], in1=ksm[:, 2, 0:1], op=ALU.mult), reads=['ksm'], writes=['ksm'])
        b = bank()
        K.op('pe', lambda: nc.tensor.matmul(PS[b][:, 0:1], onesf[:, :], ksm[:, 3, 0:1], start=True, stop=True), reads=['onesf', 'ksm'], writes=[pk(b)])
        K.op('act', lambda: nc.scalar.copy(out=B['sum8'][:, NT:NT + 1], in_=PS[b][:, 0:1]), reads=[pk(b)], writes=[('sum8', NT)])

    def rope_inplace(B, v4, nh, T, t, key):
        cos = B['rope'][:T, t, 0, :].unsqueeze(1).broadcast_to([T, nh, 8])
        sin = B['rope'][:T, t, 1, :].unsqueeze(1).broadcast_to([T, nh, 8])
        rt = B['zt'][:, :].rearrange("p (a h d) -> p a h d", a=4, h=16)
        x1 = v4[:, :, 0:8]
        x2 = v4[:, :, 8:16]
        K.op('dve', lambda: nc.vector.tensor_tensor(out=rt[:T, 0, 0:nh, :], in0=x1, in1=cos, op=ALU.mult), reads=[key, 'rope'], writes=['zt'])
        K.op('dve', lambda: nc.vector.tensor_tensor(out=rt[:T, 1, 0:nh, :], in0=x2, in1=sin, op=ALU.mult), reads=[key, 'rope'], writes=[('zt', 1)])
        K.op('dve', lambda: nc.vector.tensor_tensor(out=rt[:T, 2, 0:nh, :], in0=x2, in1=cos, op=ALU.mult), reads=[key, 'rope'], writes=[('zt', 2)])
        K.op('dve', lambda: nc.vector.tensor_tensor(out=rt[:T, 3, 0:nh, :], in0=x1, in1=sin, op=ALU.mult), reads=[key, 'rope'], writes=[('zt', 3)])
        K.op('dve', lambda: nc.vector.tensor_tensor(out=x1, in0=rt[:T, 0, 0:nh, :], in1=rt[:T, 1, 0:nh, :], op=ALU.subtract), reads=['zt', ('zt', 1), ('zt', 2), ('zt', 3)], writes=[key])
        K.op('dve', lambda: nc.vector.tensor_tensor(out=x2, in0=rt[:T, 2, 0:nh, :], in1=rt[:T, 3, 0:nh, :], op=ALU.add), reads=['zt', ('zt', 1), ('zt', 2), ('zt', 3)], writes=[key])

    def b_tile(j, ti, B):
        T, t = ti.T, ti.idx
        wflat = B['wflat']
        KT, VE, QT, qf, qext, kvf, kext = B['KT'], B['VE'], B['QT'], B['qf'], B['qext'], B['kvf'], B['kext']
        tok0 = t * 128
        make_xT(ti)
        tail_step()
        if j == 0:
            b = bank()
            for kc in range(8):
                K.op('pe', lambda: nc.tensor.matmul(PS[b][:T, :], xT[:, kc, :T], wflat[:, 16384 + kc * 512:16384 + (kc + 1) * 512],
                                                    start=(kc == 0), stop=(kc == 7)),
                     reads=XTK + ['wkv'], writes=[pk(b)])
            K.op('act', lambda: nc.scalar.copy(out=kvf[:T, :], in_=PS[b][:T, :]), reads=[pk(b)], writes=['kvf'])
            kv3 = kvf[:T, 0:256].rearrange("p (g d) -> p g d", g=4)
            vv3 = kvf[:T, 256:512].rearrange("p (g d) -> p g d", g=4)
            rope_inplace(B, kv3, 4, T, t, 'kvf')
            K.op('act', lambda: nc.scalar.copy(out=kext[:T, :, 0:64], in_=kv3), reads=['kvf'], writes=['kext'])
            K.op('pool', lambda: nc.gpsimd.tensor_copy(out=VE[:T, t, :, 0:64], in_=vv3), reads=['kvf'], writes=[('VE', t)])
            tail_step()
            kt_store(B, T, lambda g: KT[:, g, tok0:tok0 + T], ['kext'], ('KT', t))
            ksum8(B, kv3, T, t, None)
            if t == NPT - 1:
                out_stamps.append(K.dma('sp', pk_d, kvf[:, 0:256], reads=['kvf'], writes=['pk']))
                out_stamps.append(K.dma('sp', pv_d, kvf[:, 256:512], reads=['kvf'], writes=['pv']))
            if ti.samp:
                for s_ in range(2):
                    out_stamps.append(K.dma('sp', sk_d[s_, 96:128, :], kvf[s_ * 32:(s_ + 1) * 32, 0:256], reads=['kvf'], writes=[('sk', s_, 1)]))
                    out_stamps.append(K.dma('sp', sv_d[s_, 96:128, :], kvf[s_ * 32:(s_ + 1) * 32, 256:512], reads=['kvf'], writes=[('sv', s_, 1)]))
        for h2 in range(2):
            b = bank()
            for kc in range(8):
                K.op('pe', lambda: nc.tensor.matmul(PS[b][:T, :], xT[:, kc, :T], wflat[:, kc * 2048 + h2 * 512:kc * 2048 + (h2 + 1) * 512],
                                                    start=(kc == 0), stop=(kc == 7)),
                     reads=XTK + [('wbin', 'q%d' % h2)], writes=[pk(b)])
            K.op('act', lambda: nc.scalar.activation(out=qf[:T, h2 * 8:h2 * 8 + 8, :], in_=PS[b][:T, :].rearrange("p (h d) -> p h d", h=8),
                                                     func=AF.Copy, scale=0.125),
                 reads=[pk(b)], writes=['qf'])
        tail_step()
        rope_inplace(B, qf[:T, :, :], 16, T, t, 'qf')
        tail_step()
        K.op('pool', lambda: nc.gpsimd.tensor_copy(out=qext[:T, :, 0:64], in_=qf[:T, :, :]), reads=['qf'], writes=['qext'])
        qsm = B['qsm']
        scr = B['ztz'][:, :].rearrange("p (h d) -> p h d", h=16)
        K.op('dve', lambda: nc.vector.tensor_tensor(out=scr[:T, :, :], in0=qf[:T, :, :], in1=qf[:T, :, :], op=ALU.mult), reads=['qf'], writes=['zt', 'zs'])
        K.op('dve', lambda: nc.vector.tensor_reduce(out=qsm[:T, 0, :], in_=scr[:T, :, :], axis=AX.X, op=ALU.add), reads=['zt', 'zs'], writes=['qsm0'])
        K.op('act', lambda: nc.scalar.activation(out=qsm[:T, 0, :], in_=qsm[:T, 0, :], func=AF.Ln, bias=EPS_RMS[:T, :], scale=1.0), reads=['qsm0', 'eps'], writes=['qsm0'])
        K.op('act', lambda: nc.scalar.activation(out=qsm[:T, 0, :], in_=qsm[:T, 0, :], func=AF.Exp, scale=0.5), reads=['qsm0'], writes=['qsm0'])
        tail_step()
        cprev = NT if ti.samp else (t - 1 if t > 0 else NT + 1)
        K.op('dve', lambda: nc.vector.tensor_tensor(out=qsm[:T, 1, 0:1], in0=B['sum8'][:T, t:t + 1], in1=B['sum8'][:T, cprev:cprev + 1], op=ALU.add),
             reads=[('sum8', t), ('sum8', cprev)], writes=['qsm1'])
        K.op('act', lambda: nc.scalar.activation(out=qsm[:T, 1, 0:1], in_=qsm[:T, 1, 0:1], func=AF.Ln), reads=['qsm1'], writes=['qsm1'])
        K.op('act', lambda: nc.scalar.activation(out=qsm[:T, 1, 0:1], in_=qsm[:T, 1, 0:1], func=AF.Exp, scale=0.125), reads=['qsm1'], writes=['qsm1'])
        K.op('dve', lambda: nc.vector.tensor_scalar(out=B['nshb'][:T, :], in0=qsm[:T, 0, :], scalar1=qsm[:T, 1, 0:1], scalar2=-1.0, op0=ALU.mult, op1=ALU.mult),
             reads=['qsm0', 'qsm1'], writes=['nshb'])
        K.op('pool', lambda: nc.gpsimd.tensor_copy(out=qext[:T, :, 64:65], in_=B['nshb'][:T, :].unsqueeze(2)), reads=['nshb'], writes=['qext'])
        K.op('dve', lambda: nc.vector.tensor_tensor(out=qsm[:T, 2, :], in0=B['nshb'][:T, :], in1=B['sinkb'][:T, :], op=ALU.add),
             reads=['nshb', 'sinkb'], writes=['qsm2'])
        K.op('act', lambda: nc.scalar.activation(out=qsm[:T, 2, :], in_=qsm[:T, 2, :], func=AF.Exp), reads=['qsm2'], writes=['qsm2'])
        for h2 in range(2):
            b = bank()
            psb = PS[b][:, :].bitcast(BF16)
            for hq in range(8):
                h = h2 * 8 + hq
                K.op('pe', lambda: nc.tensor.transpose(psb[0:65, hq * T:(hq + 1) * T], qext[:T, h, :], identb[:T, :T]),
                     reads=['qext', 'identb'], writes=[pk(b)])
            K.op('act', lambda: nc.scalar.copy(out=QT[0:65, h2 * 8:h2 * 8 + 8, :T], in_=psb[0:65, 0:8 * T].rearrange("p (a t) -> p a t", a=8)),
                 reads=[pk(b)], writes=[('QT', h2)])
        tail_step()
        tail_flush()
        for h2 in range(2):
            b = bank()
            for kc in range(8):
                K.op('pe', lambda: nc.tensor.matmul(PS[b][:T, :], xT[:, kc, :T], wflat[:, kc * 2048 + 1024 + h2 * 512:kc * 2048 + 1024 + (h2 + 1) * 512],
                                                    start=(kc == 0), stop=(kc == 7)),
                     reads=XTK + [('wbin', 'z')], writes=[pk(b)])
            silu_from(B['zsb'][:T, h2 * 512:(h2 + 1) * 512], PS[b][:T, :], B['ztz'][:T, h2 * 512:(h2 + 1) * 512], [pk(b)], [('zsb', h2)], [['zt', 'zs'][h2]])
        den = B['den']

        def att_norm(g, bo):
            pb_ = g % 2
            po = PS[bo][:T, 0:260].rearrange("p (h d) -> p h d", h=4)
            K.op('dve', lambda: nc.vector.tensor_tensor(out=den[:T, pb_, :], in0=po[:, :, 64], in1=qsm[:T, 2, 4 * g:4 * g + 4], op=ALU.add),
                 reads=[pk(bo), 'qsm2'], writes=[('den', pb_)])
            K.op('dve', lambda: nc.vector.reciprocal(out=den[:T, pb_, :], in_=den[:T, pb_, :]), reads=[('den', pb_)], writes=[('den', pb_)])
            K.op('dve', lambda: nc.vector.tensor_tensor(out=qf[:T, 4 * g:4 * g + 4, :], in0=po[:, :, 0:64],
                                                        in1=den[:T, pb_, :].unsqueeze(2).broadcast_to([T, 4, 64]), op=ALU.mult),
                 reads=[pk(bo), ('den', pb_)], writes=[('ob', g)])

        if not ti.samp:
            sbk = {}

            def att_scores(g):
                qk = ('QT', g // 2)
                rhsq = QT[0:65, 4 * g:4 * g + 4, :T]
                b1 = None
                if t > 0:
                    b1 = bank()
                    reserved.add(b1)
                    K.op('pe', lambda: nc.tensor.matmul(PS[b1][:, 0:4 * T], KT[:, g, tok0 - 128:tok0], rhsq, start=True, stop=True),
                         reads=[('KT', t - 1), qk], writes=[pk(b1)])
                b2 = bank()
                reserved.add(b2)
                K.op('pe', lambda: nc.tensor.matmul(PS[b2][:, 0:4 * T], KT[:, g, tok0:tok0 + 128], rhsq, start=True, stop=True),
                     reads=[('KT', t), qk], writes=[pk(b2)])
                sbk[g] = (b1, b2)

            def att_exp(g):
                pb_ = g % 2
                PTp, PTc = B['PTp'][pb_], B['PTc'][pb_]
                b1, b2 = sbk[g]
                if t > 0:
                    v1 = PS[b1][:, 0:4 * T].rearrange("p (h q) -> p h q", h=4)
                    K.op('act', lambda: nc.scalar.activation(out=PTp[0:64, :, 0:64], in_=v1[0:64, :, 0:64], func=AF.Exp), reads=[pk(b1)], writes=[('PTp', pb_)])
                    K.op('act', lambda: nc.scalar.activation(out=PTp[64:128, :, :], in_=v1[64:128, :, :], func=AF.Exp), reads=[pk(b1)], writes=[('PTp', pb_)])
                    reserved.discard(b1)
                v2 = PS[b2][:, 0:4 * T].rearrange("p (h q) -> p h q", h=4)
                K.op('act', lambda: nc.scalar.activation(out=PTc[0:64, :, :], in_=v2[0:64, :, :], func=AF.Exp), reads=[pk(b2)], writes=[('PTc', pb_)])
                K.op('act', lambda: nc.scalar.activation(out=PTc[64:128, :, 64:128], in_=v2[64:128, :, 64:128], func=AF.Exp), reads=[pk(b2)], writes=[('PTc', pb_)])
                reserved.discard(b2)

            def att_pv(g):
                pb_ = g % 2
                PTp, PTc = B['PTp'][pb_], B['PTc'][pb_]
                bo = bank()
                for hh in range(4):
                    if t > 0:
                        K.op('pe', lambda: nc.tensor.matmul(PS[bo][:T, hh * 65:(hh + 1) * 65], PTp[:, hh, :], VE[:, t - 1, g, :], start=True, stop=False),
                             reads=[('PTp', pb_), ('VE', t - 1), 'VE1'], writes=[pk(bo)])
                    K.op('pe', lambda: nc.tensor.matmul(PS[bo][:T, hh * 65:(hh + 1) * 65], PTc[:, hh, :], VE[:, t, g, :], start=(t == 0), stop=True),
                         reads=[('PTc', pb_), ('VE', t), 'VE1'], writes=[pk(bo)])
                return bo

            att_scores(0)
            for g in range(4):
                if g + 1 < 4:
                    att_scores(g + 1)
                att_exp(g)
                if g > 0:
                    bo = att_pv(g - 1)
                    att_norm(g - 1, bo)
            bo = att_pv(3)
            att_norm(3, bo)
        else:
            for g in range(4):
                pb_ = g % 2
                qk = ('QT', g // 2)
                bo = bank()
                PTs = B['PTs'][pb_]
                PTn = B['PTn']
                for s_ in range(2):
                    b1 = bank()
                    K.op('pe', lambda: nc.tensor.matmul(PS[b1][:, 0:128], B['KTc'][:, g, s_, :], QT[0:65, 4 * g:4 * g + 4, s_ * 32:(s_ + 1) * 32], start=True, stop=True),
                         reads=['KTc', qk], writes=[pk(b1)])
                    K.op('act', lambda: nc.scalar.activation(out=PTs[s_][:, :, s_ * 32:(s_ + 1) * 32], in_=PS[b1][:, 0:128].rearrange("p (h q) -> p h q", h=4), func=AF.Exp),
                         reads=[pk(b1)], writes=[('PTs', pb_, s_)])
                b2 = bank()
                K.op('pe', lambda: nc.tensor.matmul(PS[b2][0:64, 0:256], KT[:, g, tok0:tok0 + 64], QT[0:65, 4 * g:4 * g + 4, 0:64], start=True, stop=True),
                     reads=[('KT', t), qk], writes=[pk(b2)])
                v2 = PS[b2][0:64, 0:256].rearrange("p (h q) -> p h q", h=4)
                K.op('act', lambda: nc.scalar.activation(out=PTn[0:32, :, 0:32], in_=v2[0:32, :, 0:32], func=AF.Exp), reads=[pk(b2)], writes=['PTn'])
                K.op('act', lambda: nc.scalar.activation(out=PTn[32:64, :, 32:64], in_=v2[32:64, :, 32:64], func=AF.Exp), reads=[pk(b2)], writes=['PTn'])
                for hh in range(4):
                    K.op('pe', lambda: nc.tensor.matmul(PS[bo][:T, hh * 65:(hh + 1) * 65], PTs[0][:, hh, :], B['VEc'][:, 0, g, :], start=True, stop=False),
                         reads=[('PTs', pb_, 0), 'VEc'], writes=[pk(bo)])
                    K.op('pe', lambda: nc.tensor.matmul(PS[bo][:T, hh * 65:(hh + 1) * 65], PTs[1][:, hh, :], B['VEc'][:, 1, g, :], start=False, stop=False),
                         reads=[('PTs', pb_, 1), 'VEc'], writes=[pk(bo)])
                    K.op('pe', lambda: nc.tensor.matmul(PS[bo][:T, hh * 65:(hh + 1) * 65], PTn[:, hh, :], VE[0:64, t, g, :], start=False, stop=True),
                         reads=['PTn', ('VE', t), 'VE1'], writes=[pk(bo)])
                att_norm(g, bo)
        obf = qf[:, :, :].rearrange("p a b -> p (a b)")
        for h2 in range(2):
            sl = slice(h2 * 512, (h2 + 1) * 512)
            K.op('dve', lambda: nc.vector.tensor_tensor(out=og[:T, sl], in0=obf[:T, sl], in1=B['zsb'][:T, sl], op=ALU.mult),
                 reads=[('ob', 2 * h2), ('ob', 2 * h2 + 1), ('zsb', h2)], writes=['og'])
        out_proj(ti, j == 1, B['ogT'], ['ogTb'])

    done = False
    build.marks = []
    for l in range(2):
        done = a_layer(l)
        build.marks.append(K.ninstr['pe'])
        if done:
            break
    if not done and stop_after != 'A':
        Bv = b_setup()
        for j in range(2):
            b_layer(j, Bv)
            build.marks.append(K.ninstr['pe'])
    if done or stop_after == 'A':
        for t in range(NT):
            T = 64 if t == NPT else 128
            out_stamps.append(K.dma('sp', y_d[t * 128:t * 128 + T, :], xres[:T, t, :], reads=[('x', t)], writes=[('y', t)]))
    K.wait_all('sp', out_stamps)
    K.barrier()
    es.close()
    build.stats = (dict(K.ninstr), K.nwait)
    build.sim = K.simulate()
    return nc


def prep_inputs(inputs):
    c = host_consts()
    f = lambda a: np.ascontiguousarray(np.asarray(a, dtype=np.float32))
    xp = f(inputs['x_prompt'])
    xs = f(inputs['x_sample'])
    sd = f(inputs['state_delta'])
    sc = f(inputs['state_conv'])
    ck = f(inputs['cache_k'])
    cv = f(inputs['cache_v'])
    acw = f(inputs['a_conv_w'])
    convw = np.ascontiguousarray(acw.reshape(2, 4, 24, 128).transpose(0, 3, 2, 1))
    bc = lambda v, n: np.ascontiguousarray(np.broadcast_to(f(v)[:, None, :], (v.shape[0], 128, n)))
    shared = {
        'a_w_in': f(inputs['a_w_in']), 'a_w_out': f(inputs['a_w_out']), 'b_w_kv': f(inputs['b_w_kv']),
        'b_w_in': f(inputs['b_w_in']), 'b_w_out': f(inputs['b_w_out']),
        'convw': convw, 'alog_bc': bc(inputs['a_log'], 8), 'dtb_bc': bc(inputs['a_dt_bias'], 8),
        'normw_bc': bc(inputs['a_norm_w'], 128),
        'lng_bc': np.ascontiguousarray(np.concatenate([bc(inputs['a_ln_g'], D), bc(inputs['b_ln_g'], D)], 0)),
        'lnb_bc': np.ascontiguousarray(np.concatenate([bc(inputs['a_ln_b'], D), bc(inputs['b_ln_b'], D)], 0)),
        'sink_bc': bc(inputs['b_sinks'], 16),
        'identf': c['identf'], 'onesf': c['onesf'], 'masks': c['masks'], 'rope': c['rope'],
    }
    maps = []
    for i in range(NCORES):
        m = dict(shared)
        m['x'] = np.ascontiguousarray(np.concatenate([xp[i], xs[2 * i], xs[2 * i + 1]], 0))
        m['sdelta'] = np.ascontiguousarray(sd[:, 2 * i:2 * i + 2])
        scc = sc[:, 2 * i:2 * i + 2]
        m['sconv'] = np.ascontiguousarray(scc.reshape(2, 2, 3, 24, 128).transpose(0, 1, 4, 3, 2))
        m['ck'] = np.ascontiguousarray(ck[2 * i:2 * i + 2].reshape(2, 128, 256))
        m['cv'] = np.ascontiguousarray(cv[2 * i:2 * i + 2].reshape(2, 128, 256))
        maps.append(m)
    return maps


_NC_CACHE = {}


def kernel(**inputs):
    maps = prep_inputs(inputs)
    if 'nc' not in _NC_CACHE:
        _NC_CACHE['nc'] = build()
    nc = _NC_CACHE['nc']
    maps = [{n: m[n] for n in build.in_names} for m in maps]
    r = run_bass_kernel_spmd(nc, maps, core_ids=list(range(NCORES))).results
    y = np.stack([r[i]['y'] for i in range(NCORES)])
    y_prompt = np.ascontiguousarray(y[:, :SEQ])
    y_sample = np.ascontiguousarray(y[:, SEQ:].reshape(16, 32, D))
    p_delta = np.stack([r[i]['pdelta'] for i in range(NCORES)], 1)
    s_delta = np.concatenate([r[i]['sdelta_o'] for i in range(NCORES)], 1)
    pc = np.stack([r[i]['pconv'] for i in range(NCORES)], 1)
    p_conv = np.ascontiguousarray(pc.transpose(0, 1, 4, 3, 2).reshape(2, 8, 3, 3072))
    scv = np.concatenate([r[i]['sconv_o'] for i in range(NCORES)], 1)
    s_conv = np.ascontiguousarray(scv.transpose(0, 1, 4, 3, 2).reshape(2, 16, 3, 3072))
    if WITH_B:
        p_k = np.stack([r[i]['pk'] for i in range(NCORES)]).reshape(8, 128, 4, 64)
        p_v = np.stack([r[i]['pv'] for i in range(NCORES)]).reshape(8, 128, 4, 64)
        s_k = np.concatenate([r[i]['sk'] for i in range(NCORES)]).reshape(16, 128, 4, 64)
        s_v = np.concatenate([r[i]['sv'] for i in range(NCORES)]).reshape(16, 128, 4, 64)
    else:
        p_k = np.zeros((8, 128, 4, 64), np.float32)
        p_v = np.zeros((8, 128, 4, 64), np.float32)
        s_k = np.zeros((16, 128, 4, 64), np.float32)
        s_v = np.zeros((16, 128, 4, 64), np.float32)
    outs = (y_prompt, y_sample, p_delta, p_conv, p_k, p_v, s_delta, s_conv, s_k, s_v)
    return tuple(np.ascontiguousarray(o.astype(np.float32)) for o in outs)
```

```python
import numpy as np
import ml_dtypes
from contextlib import ExitStack
import concourse.bass as bass
import concourse.mybir as mybir
from concourse.bass_utils import run_bass_kernel_spmd

F32 = mybir.dt.float32
BF16 = mybir.dt.bfloat16
F32R = mybir.dt.float32r
AF = mybir.ActivationFunctionType
ALU = mybir.AluOpType
AX = mybir.AxisListType

NCORES = 8
D = 1024
SEQ = 2048
NPT = 16
NT = 17
ROWS = SEQ + 64
AIN = 4112
DN_ALPHA = 8.0 ** 0.25
LN_EPS = 1e-5
RMS_EPS = 1e-6
NEG = -30000.0
PAST = 4096


class Trk:
    NDS = 20

    def __init__(self, nc, es):
        self.nc = nc
        self.eng = {'pe': nc.tensor, 'act': nc.scalar, 'dve': nc.vector, 'pool': nc.gpsimd, 'sp': nc.sync}
        self.semh = {}
        for e in ['pe', 'act', 'dve', 'pool']:
            self.semh[e] = es.enter_context(nc.semaphore("s_" + e))
        for i in range(self.NDS):
            self.semh[('d', i)] = es.enter_context(nc.semaphore("s_d%d" % i))
            self.semh[('w', i)] = es.enter_context(nc.semaphore("s_w%d" % i))
        self.cnt = {k: 0 for k in self.semh}
        self.seen = {e: {} for e in self.eng}
        self.clock = {}
        self.lastw = {}
        self.readers = {}
        self.dnext = {'sp': 0, 'pool': 0}
        self.ninstr = {e: 0 for e in self.eng}
        self.nwait = 0
        self.prog = {e: [] for e in self.eng}

    def _merge(self, e, stamp):
        s = self.seen[e]
        k, v = stamp
        if s.get(k, 0) < v:
            s[k] = v
        for kk, vv in self.clock.get(stamp, {}).items():
            if s.get(kk, 0) < vv:
                s[kk] = vv

    def _deps(self, e, reads, writes, extra=()):
        deps = {}

        def add(st):
            if st is None:
                return
            k, v = st
            if k == e and e == 'pe':
                return
            if deps.get(k, 0) < v:
                deps[k] = v
        for r in reads:
            add(self.lastw.get(r))
        for w in writes:
            add(self.lastw.get(w))
            for k, v in self.readers.get(w, {}).items():
                add((k, v))
        for st in extra:
            add(st)
        need = [(k, v) for k, v in deps.items() if self.seen[e].get(k, 0) < v]
        return need

    def _emit(self, e, fn, need):
        eng = self.eng[e]
        for (k, v) in need[:-1]:
            eng.wait_ge(self.semh[k], v)
            self.nwait += 1
        ins = fn()
        if need:
            k, v = need[-1]
            ins._wait_ge(self.semh[k], v)
        self.prog[e].append([list(need), None])
        for st in need:
            self._merge(e, st)
        self.ninstr[e] += 1
        return ins

    def _finish(self, e, stamp, reads, writes):
        self.clock[stamp] = dict(self.seen[e])
        for w in writes:
            self.lastw[w] = stamp
            self.readers[w] = {}
        for r in reads:
            d = self.readers.setdefault(r, {})
            k, v = stamp
            if d.get(k, 0) < v:
                d[k] = v

    def op(self, e, fn, reads=(), writes=()):
        psr = [r for r in reads if isinstance(r, tuple) and r[0] == 'ps' and r not in writes]
        if psr:
            writes = list(writes) + psr
        need = self._deps(e, reads, writes)
        ins = self._emit(e, fn, need)
        self.cnt[e] += 1
        ins.then_inc(self.semh[e], 1)
        self.prog[e][-1][1] = (e, 1)
        stamp = (e, self.cnt[e])
        self.seen[e][e] = self.cnt[e] if e == 'pe' else self.seen[e].get(e, 0)
        self._finish(e, stamp, reads, writes)
        return ins

    def dma(self, q, out, in_, reads=(), writes=(), **kw):
        key = ('d' if q == 'sp' else 'w', self.dnext[q])
        self.dnext[q] = (self.dnext[q] + 1) % self.NDS
        prev = (key, self.cnt[key]) if self.cnt[key] > 0 else None
        need = self._deps(q, reads, writes, extra=(prev,) if prev else ())
        ins = self._emit(q, lambda: self.eng[q].dma_start(out=out, in_=in_, **kw), need)
        self.cnt[key] += 16
        ins.then_inc(self.semh[key], 16)
        self.prog[q][-1][1] = (key, 16)
        stamp = (key, self.cnt[key])
        self._finish(q, stamp, reads, writes)
        return stamp

    def simulate(self):
        sem = {k: 0 for k in self.semh}
        pc = {e: 0 for e in self.eng}
        progress = True
        while progress:
            progress = False
            for e in self.eng:
                while pc[e] < len(self.prog[e]):
                    waits, inc = self.prog[e][pc[e]]
                    if all(sem[k] >= v for k, v in waits):
                        if inc:
                            sem[inc[0]] += inc[1]
                        pc[e] += 1
                        progress = True
                    else:
                        break
        stuck = {e: (pc[e], len(self.prog[e]), self.prog[e][pc[e]][0] if pc[e] < len(self.prog[e]) else None) for e in self.eng}
        return stuck, {str(k): v for k, v in sem.items() if v}

    def full_sync(self):
        for e in self.eng:
            for k in list(self.semh.keys()):
                v = self.cnt[k]
                if v > 0 and k != e and self.seen[e].get(k, 0) < v:
                    self.eng[e].wait_ge(self.semh[k], v)
                    self.prog[e].append([[(k, v)], None])
                    self.seen[e][k] = v

    def barrier(self):
        for e in self.eng:
            for k in list(self.semh.keys()):
                v = self.cnt[k]
                if v > 0 and k != e:
                    self.eng[e].wait_ge(self.semh[k], v)

    def wait_all(self, e, stamps):
        for (k, v) in stamps:
            if self.seen[e].get(k, 0) < v:
                self.eng[e].wait_ge(self.semh[k], v)
                self.seen[e][k] = v


class TileInfo:
    def __init__(self, idx):
        self.idx = idx
        self.samp = idx == NPT
        self.T = 64 if self.samp else 128
        self.nseq = 2 if self.samp else 1
        self.L = self.T // self.nseq
        self.C = 32 if self.samp else 64
        self.nch = self.T // self.C
        self.m = 1 if self.samp else 0


def host_consts():
    c = {}
    c['identf'] = np.eye(128, dtype=np.float32)
    c['onesf'] = np.ones((128, 128), dtype=np.float32)
    masks = np.zeros((2, 5, 128, 128), dtype=np.float32)
    for m, (T, C) in enumerate([(128, 64), (64, 32)]):
        p = np.arange(T)[:, None]
        f = np.arange(T)[None, :]
        same = (p // C) == (f // C)
        masks[m, 0, :T, :T] = np.where(same & (f >= p), 0.0, NEG)
        masks[m, 1, :T, :T] = np.where(same & (f < p), 0.0, NEG)
        masks[m, 2, :T, :T] = np.where(p != f, 1.0, 0.0)
        masks[m, 3, :T, :T] = np.where(same & (p <= f), 1.0, 0.0)
        masks[m, 4, :T, :T] = np.where(same, 1.0, 0.0)
    c['masks'] = masks.transpose(2, 0, 1, 3).copy()
    half = 8
    inv = 500000.0 ** (-np.arange(half, dtype=np.float32) * 2.0 / 16.0)
    pos = np.zeros((NT, 128), dtype=np.float32)
    for t in range(NPT):
        pos[t] = t * 128 + np.arange(128)
    pos[NPT, :64] = PAST + (np.arange(64) % 32)
    ang = pos[:, :, None].astype(np.float32) * inv[None, None, :].astype(np.float32)
    cs = np.stack([np.cos(ang), np.sin(ang)], axis=2).astype(np.float32)
    c['rope'] = cs.transpose(1, 0, 2, 3).copy()
    return c


WITH_B = True
import os as _os
HRATIO = int(_os.environ.get('HRATIO', '1'))
HPER = int(_os.environ.get('HPER', '1'))
SPLITLEV = int(_os.environ.get('SPLITLEV', '1'))
HSKIP = [int(x) for x in _os.environ.get('HSKIP', '').split(',') if x]


def build(stop_after=None, taps=()):
    nc = bass.Bass("TRN2", target_bir_lowering=False)
    es = ExitStack()
    K = Trk(nc, es)
    tapped = {}

    in_names = []
    build.in_names = in_names

    def din(name, shape, dt=F32):
        in_names.append(name)
        return nc.dram_tensor(name, list(shape), dt, kind="ExternalInput").ap()

    def dout(name, shape, dt=F32):
        return nc.dram_tensor(name, list(shape), dt, kind="ExternalOutput").ap()

    x_d = din("x", [ROWS, D])
    a_w_in = din("a_w_in", [2, D, AIN])
    a_w_out = din("a_w_out", [2, D, D])
    if WITH_B:
        b_w_kv = din("b_w_kv", [D, 512])
    if WITH_B:
        b_w_in = din("b_w_in", [2, D, 2048])
    if WITH_B:
        b_w_out = din("b_w_out", [2, D, D])
    convw_d = din("convw", [2, 128, 24, 4])
    alog_d = din("alog_bc", [2, 128, 8])
    dtb_d = din("dtb_bc", [2, 128, 8])
    normw_d = din("normw_bc", [2, 128, 128])
    lng_d = din("lng_bc", [4, 128, D])
    lnb_d = din("lnb_bc", [4, 128, D])
    if WITH_B:
        sink_d = din("sink_bc", [2, 128, 16])
    sdel_d = din("sdelta", [2, 2, 8, 128, 128])
    sconv_d = din("sconv", [2, 2, 128, 24, 3])
    if WITH_B:
        ck_d = din("ck", [2, 128, 256])
    if WITH_B:
        cv_d = din("cv", [2, 128, 256])
    identf_d = din("identf", [128, 128])
    onesf_d = din("onesf", [128, 128])
    masks_d = din("masks", [128, 2, 5, 128])
    if WITH_B:
        rope_d = din("rope", [128, NT, 2, 8])

    y_d = dout("y", [ROWS, D])
    pdel_d = dout("pdelta", [2, 8, 128, 128])
    sdelo_d = dout("sdelta_o", [2, 2, 8, 128, 128])
    pconv_d = dout("pconv", [2, 128, 24, 3])
    sconvo_d = dout("sconv_o", [2, 2, 128, 24, 3])
    if WITH_B:
        pk_d = dout("pk", [128, 256])
    if WITH_B:
        pv_d = dout("pv", [128, 256])
    if WITH_B:
        sk_d = dout("sk", [2, 128, 256])
    if WITH_B:
        sv_d = dout("sv", [2, 128, 256])
    tap_d = {}

    def sb(name, shape, dt=F32):
        return nc.alloc_sbuf_tensor("sb_" + name, list(shape), dt)

    xres = sb("xres", [128, NT, D])
    wbig = sb("wbig", [128, 8, AIN], BF16)
    wring = sb("wring", [128, 4, D], BF16)
    identf = sb("identf", [128, 128])
    identb = sb("identb", [128, 128], BF16)
    onesf = sb("onesf", [128, 128])
    masks = sb("masks", [128, 5, 128])
    lng = sb("lng", [128, D])
    lnb = sb("lnb", [128, D])
    xT = sb("xT", [128, 8, 128], BF16)
    ogT = xT
    og = sb("og", [128, D], BF16)
    on = sb("on", [128, 8, 128])
    res = on[:, :, :].rearrange("p h d -> p (h d)")
    bst = sb("bst", [128, 2, 6])
    mv = sb("mv", [128, 2])
    rstd = sb("rstd", [128, 1])
    Aq = sb("Aq", [128, 4, 128], F32R)
    BPq = sb("BPq", [128, 4, 256], F32R)
    ARENA = 10580
    arena = sb("arena", [128, ARENA])
    apos = [0]

    def ar(shape, dt=F32):
        n = int(np.prod(shape[1:]))
        nf = n if dt == F32 or dt == F32R else (n + 1) // 2
        nf = (nf + 7) // 8 * 8
        o = apos[0]
        apos[0] += nf
        assert apos[0] <= ARENA, ("arena overflow", apos[0])
        v = arena[:, o:o + nf]
        if dt != F32:
            v = v.bitcast(dt)
        v = v[:, 0:n]
        if len(shape) == 3:
            v = v.rearrange("p (a b) -> p a b", a=shape[1])
        elif len(shape) == 4:
            v = v.rearrange("p (a b c) -> p a b c", a=shape[1], b=shape[2])
        return v

    convw = ar([128, 24, 4])
    alog = ar([128, 8])
    nega = ar([128, 8])
    dtb = ar([128, 8])
    normw = ar([128, 128])
    carry = ar([128, 24 * 2 * 3])
    cb = [ar([128, 4 * 132])] * 2
    acc = ar([128, 4, 128])
    ebuf = ar([128, 4, 128])
    yfm = acc
    ssq = ar([128, 16])
    rn = ar([128, 16])
    sc_k = ar([128, 8])
    tmb = ar([128, 4, 128], BF16)
    qT2 = [ar([128, 8, 128], BF16) for _ in range(2)]
    kT2 = [ar([128, 8, 128], BF16) for _ in range(2)]
    kd2 = [ar([128, 8, 128], BF16) for _ in range(2)]
    vp2 = [ar([128, 8, 128], BF16) for _ in range(2)]
    gsm2 = [ar([128, 12, 8]) for _ in range(2)]
    glbc2 = [ar([128, 8, 2]) for _ in range(2)]
    REG = ar([128, 1024])
    DTi = [REG[:, i * 128:(i + 1) * 128] for i in (0, 1)]
    Ds = [REG[:, i * 128:(i + 1) * 128] for i in (2, 3)]
    DTs = [REG[:, i * 128:(i + 1) * 128] for i in (4, 5)]
    _rb = REG[:, :].bitcast(BF16)
    Rp = [_rb[:, i * 512:(i + 1) * 512].rearrange("p (h d) -> p h d", h=4) for i in (0, 1)]
    Yb = [_rb[:, i * 512:(i + 1) * 512].rearrange("p (h d) -> p h d", h=4) for i in (2, 3)]
    intraT = ar([128, 8, 128], BF16)
    XT = ar([128, 8, 128], BF16)
    S = ar([128, 8, 128])
    Sb = ar([128, 8, 128], BF16)
    osq = ebuf[:, 0, :]
    oss = ar([128, 8])
    print("arena used (A):", apos[0])

    PSALL = nc.alloc_psum_tensor("psall", [128, 4096], F32)

    class _Bank:
        def __init__(self, i, n=1):
            self.i, self.n = i, n

        def __getitem__(self, key):
            return PSALL[:, self.i * 512:(self.i + self.n) * 512][key]

    PS = [_Bank(i) for i in range(8)]
    psn = [0]

    reserved = set()

    def bank():
        for _ in range(16):
            i = psn[0]
            psn[0] = (i + 1) % 8
            if i not in reserved:
                return i
        raise RuntimeError("no free PSUM bank: reserved=%s" % sorted(reserved))

    def pk(i):
        return ('ps', i)

    def bank2():
        for _ in range(16):
            i = psn[0]
            if i + 1 < 8 and i not in reserved and (i + 1) not in reserved:
                psn[0] = (i + 2) % 8
                return i
            psn[0] = (i + 1) % 8
        raise RuntimeError("no free PSUM bank pair: reserved=%s" % sorted(reserved))

    def PS2(i):
        return _Bank(i, 2)

    out_stamps = []
    dbg_outs = {}

    def tap(name, ap_sb, keys):
        if name not in taps:
            return
        d = dout("tap_" + name, list(ap_sb.shape), F32 if ap_sb.dtype == F32R else ap_sb.dtype)
        src = ap_sb.bitcast(F32) if ap_sb.dtype == F32R else ap_sb
        out_stamps.append(K.dma('sp', d, src, reads=keys, writes=[('tap', name)]))

    K.dma('sp', identf[:], identf_d, writes=['identf'])
    K.dma('sp', onesf[:], onesf_d, writes=['onesf'])
    K.dma('pool', identb[:], identf_d, writes=['identb'])
    for t in range(NT):
        T = 64 if t == NPT else 128
        K.dma('sp', xres[:T, t, :], x_d[t * 128:t * 128 + T, :], writes=[('x', t)])

    def load_w(dst, src_ap, ncols, key, c0=0):
        for kc in range(8):
            K.dma('pool', dst[:, kc, c0:c0 + ncols], src_ap[kc * 128:(kc + 1) * 128, :], writes=[(key, kc)])

    XTK = [('xT', 0), ('xT', 1)]
    ONK = [('on', h) for h in range(8)]

    def make_xT(ti):
        T = ti.T
        for half in range(2):
            b = bank()
            for q in range(4):
                kc = half * 4 + q
                K.op('pe', lambda: nc.tensor.transpose(PS[b][:, q * T:(q + 1) * T], xres[:T, ti.idx, kc * 128:(kc + 1) * 128],
                                                       identf[:T, :T]),
                     reads=[('x', ti.idx), 'identf'], writes=[pk(b)])
            K.op('act', lambda: nc.scalar.copy(out=xT[:, half * 4:half * 4 + 4, :T],
                                               in_=PS[b][:, 0:4 * T].rearrange("p (a t) -> p a t", a=4)),
                 reads=[pk(b)], writes=[('xT', half)])

    pending_tail = []

    def tail_step():
        if pending_tail:
            pending_tail.pop(0)()

    def tail_flush():
        while pending_tail:
            pending_tail.pop(0)()

    wr_state = {'use': 0, 'iss': 0}
    WOUT_SRC = [a_w_out[0], a_w_out[1]] + ([b_w_out[0], b_w_out[1]] if WITH_B else [])

    def wr_issue():
        c = wr_state['iss']
        if c >= len(WOUT_SRC) * NT * 8:
            return
        wr_state['iss'] += 1
        lay = c // (NT * 8)
        kc = c % 8
        K.dma('pool', wring[:, c % 4, :], WOUT_SRC[lay][kc * 128:(kc + 1) * 128, :], writes=[('wr', c % 4)])

    for _ in range(4):
        wr_issue()

    def out_proj(ti, final, ogT_ap, ogk, immediate=0):
        T = ti.T
        t = ti.idx
        st = {}

        def s1():
            b = bank()
            psb = PS[b][:, :].bitcast(BF16)
            for kc in range(8):
                K.op('pe', lambda: nc.tensor.transpose(psb[:, kc * T:(kc + 1) * T], og[:T, kc * 128:(kc + 1) * 128], identb[:T, :T]),
                     reads=['og', 'identb'], writes=[pk(b)])
            K.op('act', lambda: nc.scalar.copy(out=ogT_ap[:, :, :T], in_=psb[:, 0:8 * T].rearrange("p (a t) -> p a t", a=8)),
                 reads=[pk(b)], writes=ogk)

        def s2half(kcs):
            if 'pb' not in st:
                st['pb'] = [bank(), bank()]
                reserved.update(st['pb'])
            pb = st['pb']
            for kc in kcs:
                c = wr_state['use']
                wr_state['use'] += 1
                slot = c % 4
                for h2 in range(2):
                    K.op('pe', lambda: nc.tensor.matmul(PS[pb[h2]][:T, :], ogT_ap[:, kc, :T], wring[:, slot, h2 * 512:(h2 + 1) * 512],
                                                        start=(kc == 0), stop=(kc == 7)),
                         reads=ogk + [('wr', slot)], writes=[pk(pb[h2])])
            for _ in kcs:
                wr_issue()

        def s2a():
            s2half(range(0, 4))

        def s2b():
            s2half(range(4, 8))

        def s3():
            pb = st['pb']
            for h2 in range(2):
                rk = ONK[h2 * 4:h2 * 4 + 4]
                sl = slice(h2 * 512, (h2 + 1) * 512)
                K.op('dve', lambda: nc.vector.scalar_tensor_tensor(out=res[:T, sl], in0=xres[:T, t, sl], scalar=DN_ALPHA,
                                                                   in1=PS[pb[h2]][:T, :], op0=ALU.mult, op1=ALU.add),
                     reads=[('x', t), pk(pb[h2])], writes=rk)
                K.op('dve', lambda: nc.vector.bn_stats(out=bst[:T, h2, :], in_=res[:T, sl]), reads=rk, writes=[('bst', h2)])
            K.op('dve', lambda: nc.vector.bn_aggr(out=mv[:T, :], in_=bst[:T, :, :]), reads=[('bst', 0), ('bst', 1)], writes=['mv'])
            K.op('act', lambda: nc.scalar.activation(out=rstd[:T, :], in_=mv[:T, 1:2], func=AF.Ln, bias=epsln[:T, 0:1], scale=1.0),
                 reads=['mv', 'eps'], writes=['rstd'])
            K.op('act', lambda: nc.scalar.activation(out=rstd[:T, :], in_=rstd[:T, :], func=AF.Exp, scale=-0.5),
                 reads=['rstd'], writes=['rstd'])
            reserved.discard(st['pb'][0])
            reserved.discard(st['pb'][1])

        def s4():
            for h2 in range(2):
                rk = ONK[h2 * 4:h2 * 4 + 4]
                sl = slice(h2 * 512, (h2 + 1) * 512)
                K.op('dve', lambda: nc.vector.tensor_scalar(out=res[:T, sl], in0=res[:T, sl], scalar1=mv[:T, 0:1], scalar2=rstd[:T, 0:1],
                                                            op0=ALU.subtract, op1=ALU.mult),
                     reads=rk + ['mv', 'rstd'], writes=rk)
                K.op('pool', lambda: nc.gpsimd.tensor_tensor(out=res[:T, sl], in0=res[:T, sl], in1=lng[:T, sl], op=ALU.mult),
                     reads=rk + ['lng'], writes=rk)

        def s5():
            for h2 in range(2):
                rk = ONK[h2 * 4:h2 * 4 + 4]
                sl = slice(h2 * 512, (h2 + 1) * 512)
                K.op('pool', lambda: nc.gpsimd.tensor_tensor(out=xres[:T, t, sl], in0=res[:T, sl], in1=lnb[:T, sl], op=ALU.add),
                     reads=rk + ['lnb'], writes=[('x', t)])
            if final:
                out_stamps.append(K.dma('sp', y_d[t * 128:t * 128 + T, :], xres[:T, t, :], reads=[('x', t)], writes=[('y', t)]))
        stages = [s1, s2a, s2b, s3, s4, s5]
        for f_ in stages[:immediate]:
            f_()
        pending_tail.extend(stages[immediate:])

    def silu_from(out_ap, in_ap, tmp_ap, rkeys, wkeys, tkeys):
        P_ = tmp_ap.shape[0]
        K.op('act', lambda: nc.scalar.activation(out=tmp_ap, in_=in_ap, func=AF.Exp, scale=-1.0), reads=rkeys, writes=tkeys)
        K.op('act', lambda: nc.scalar.activation(out=tmp_ap, in_=tmp_ap, func=AF.Ln, bias=epsln[:P_, 2:3], scale=1.0), reads=tkeys + ['eps'], writes=tkeys)
        K.op('act', lambda: nc.scalar.activation(out=tmp_ap, in_=tmp_ap, func=AF.Exp, scale=-1.0), reads=tkeys, writes=tkeys)
        K.op('dve', lambda: nc.vector.tensor_tensor(out=out_ap, in0=in_ap, in1=tmp_ap, op=ALU.mult), reads=rkeys + tkeys, writes=wkeys)

    epsln = sb("epsln", [128, 4])
    K.op('pool', lambda: nc.gpsimd.memset(epsln[:, 0:1], LN_EPS), writes=['eps'])
    K.op('pool', lambda: nc.gpsimd.memset(epsln[:, 1:2], RMS_EPS), writes=['eps'])
    K.op('pool', lambda: nc.gpsimd.memset(epsln[:, 2:3], 1.0), writes=['eps'])
    K.op('pool', lambda: nc.gpsimd.memset(epsln[:, 3:4], float(np.log(128.0 ** -0.5))), writes=['eps'])
    EPS_LN, EPS_RMS, ONE_B, LOGQ = epsln[:, 0:1], epsln[:, 1:2], epsln[:, 2:3], epsln[:, 3:4]

    def a_layer(l):
        for (bname, c0, c1) in [('g', 4096, 4112), ('q0', 0, 512), ('q1', 512, 1024), ('k', 1024, 2048), ('v', 2048, 3072), ('z', 3072, 4096)]:
            K.dma('pool', wbig[:, :, c0:c1], a_w_in[l][:, c0:c1].rearrange("(kc p) n -> p kc n", p=128), writes=[('wbig', bname)])
        K.dma('sp', masks[:], masks_d[:, 0], writes=['masks'])
        K.dma('sp', convw, convw_d[l], writes=['convw'])
        K.dma('sp', alog, alog_d[l], writes=['alog'])
        K.dma('sp', dtb, dtb_d[l], writes=['dtb'])
        K.dma('sp', normw, normw_d[l], writes=['normw'])
        K.dma('sp', lng[:], lng_d[l], writes=['lng'])
        K.dma('sp', lnb[:], lnb_d[l], writes=['lnb'])
        K.op('act', lambda: nc.scalar.activation(out=nega, in_=alog, func=AF.Exp), reads=['alog'], writes=['nega'])
        K.op('dve', lambda: nc.vector.tensor_scalar(out=nega, in0=nega, scalar1=-1.0, scalar2=None, op0=ALU.mult),
             reads=['nega'], writes=['nega'])
        K.op('pool', lambda: nc.gpsimd.memset(carry, 0.0), writes=['carry'])
        K.op('pool', lambda: nc.gpsimd.memset(S, 0.0), writes=[('S', 0), ('S', 1)])
        K.op('pool', lambda: nc.gpsimd.memset(Sb, 0.0), writes=[('Sb', 0), ('Sb', 1)])
        gens = [a_tile(l, TileInfo(t), t % 2) for t in range(NT)]

        def adv(g, until):
            while True:
                v = next(g)
                if v == until:
                    return

        adv(gens[0], 'Qdone')
        adv(gens[0], 'EZdone')
        npt_ = stop_after[1] if isinstance(stop_after, tuple) else NPT
        for t in range(npt_):
            g, gn = gens[t], (gens[t + 1] if t + 1 < npt_ else None)
            hdone = False
            cnt_ = [0]
            if gn is not None:
                adv(gn, 'F')
                while True:
                    v = next(gn)
                    if v == 'Qdone':
                        break
                    cnt_[0] += 1
                    if cnt_[0] % HRATIO == 0 and (cnt_[0] % 5) not in HSKIP:
                        for _ in range(HPER):
                            if not hdone and next(g) == 'Hdone':
                                hdone = True
            while not hdone:
                hdone = next(g) == 'Hdone'
            adv(g, 'Sdone')
            if gn is not None:
                adv(gn, 'EZdone')
            tail_step()
            tail_step()
        if isinstance(stop_after, tuple):
            tail_flush()
            return True
        gs = gens[NPT]
        adv(gs, 'Sdone')
        tail_flush()
        return False

    def a_tile(l, ti, p):
        T, nseq, L, C, nch, t = ti.T, ti.nseq, ti.L, ti.C, ti.nch, ti.idx
        W = 3 + L
        qT, kT, kd, vp, gsm, glbc = qT2[p], kT2[p], kd2[p], vp2[p], gsm2[p], glbc2[p]
        if ti.samp:
            K.dma('sp', masks[:], masks_d[:, 1], writes=['masks'])
            cview = carry[:, 0:24 * 2 * 3].rearrange("p (c s j) -> p c s j", c=24, s=2)
            for s_ in range(2):
                K.dma('sp', cview[:, :, s_, :], sconv_d[l, s_], writes=['carry'])
        K.op('pool', lambda: nc.gpsimd.memset(ssq, 0.0), writes=[('ssq', g, c_) for g in range(4) for c_ in range(4)])
        make_xT(ti)
        cv4 = carry[:, 0:24 * nseq * 3].rearrange("p (c s j) -> p c s j", c=24, s=nseq)

        bg = bank()
        for kc in range(8):
            K.op('pe', lambda: nc.tensor.matmul(PS[bg][:T, 0:16], xT[:, kc, :T], wbig[:, kc, 4096:4112], start=(kc == 0), stop=(kc == 7)),
                 reads=XTK + [('wbig', 'g')], writes=[pk(bg)])
        G_ = lambda i: gsm[:T, i, :]
        K.op('dve', lambda: nc.vector.tensor_tensor(out=G_(0), in0=PS[bg][:T, 0:8], in1=dtb[:T, :], op=ALU.add),
             reads=[pk(bg), 'dtb'], writes=[('g', p, 0)])
        K.op('act', lambda: nc.scalar.activation(out=G_(0), in_=G_(0), func=AF.Exp), reads=[('g', p, 0)], writes=[('g', p, 0)])
        K.op('act', lambda: nc.scalar.activation(out=G_(0), in_=G_(0), func=AF.Ln, bias=ONE_B[:T, :], scale=1.0), reads=[('g', p, 0), 'eps'], writes=[('g', p, 0)])
        K.op('dve', lambda: nc.vector.tensor_tensor(out=G_(1), in0=G_(0), in1=nega[:T, :], op=ALU.mult), reads=[('g', p, 0), 'nega'], writes=[('g', p, 1)])
        K.op('act', lambda: nc.scalar.activation(out=G_(2), in_=PS[bg][:T, 8:16], func=AF.Exp, scale=-1.0), reads=[pk(bg)], writes=[('g', p, 2)])
        K.op('act', lambda: nc.scalar.activation(out=G_(2), in_=G_(2), func=AF.Ln, bias=ONE_B[:T, :], scale=1.0), reads=[('g', p, 2), 'eps'], writes=[('g', p, 2)])
        K.op('act', lambda: nc.scalar.activation(out=G_(4), in_=G_(2), func=AF.Exp, scale=-0.5), reads=[('g', p, 2)], writes=[('g', p, 4)])
        bG = bank()
        K.op('pe', lambda: nc.tensor.matmul(PS[bG][:T, 0:8], masks[:T, 3, :T], G_(1), start=True, stop=True),
             reads=['masks', ('g', p, 1)], writes=[pk(bG)])
        K.op('pe', lambda: nc.tensor.matmul(PS[bG][:T, 8:16], masks[:T, 4, :T], G_(1), start=True, stop=True),
             reads=['masks', ('g', p, 1)], writes=[pk(bG)])
        K.op('act', lambda: nc.scalar.copy(out=G_(5), in_=PS[bG][:T, 0:8]), reads=[pk(bG)], writes=[('g', p, 5)])
        K.op('dve', lambda: nc.vector.tensor_tensor(out=G_(6), in0=PS[bG][:T, 8:16], in1=G_(5), op=ALU.subtract),
             reads=[pk(bG), ('g', p, 5)], writes=[('g', p, 6)])
        K.op('act', lambda: nc.scalar.activation(out=G_(6), in_=G_(6), func=AF.Exp), reads=[('g', p, 6)], writes=[('g', p, 6)])
        K.op('act', lambda: nc.scalar.activation(out=G_(7), in_=G_(5), func=AF.Exp), reads=[('g', p, 5)], writes=[('g', p, 7)])
        K.op('dve', lambda: nc.vector.tensor_scalar(out=G_(8), in0=G_(7), scalar1=-1.0, scalar2=None, op0=ALU.mult),
             reads=[('g', p, 7)], writes=[('g', p, 8)])
        yield 'F'
        qkv_banks = {}

        def qkv_proj(grp):
            b = bank()
            reserved.add(b)
            qkv_banks[grp] = b
            for c4 in range(4):
                ct = grp * 4 + c4
                for kc in range(8):
                    K.op('pe', lambda: nc.tensor.matmul(PS[b][:, c4 * T:(c4 + 1) * T], wbig[:, kc, ct * 128:(ct + 1) * 128], xT[:, kc, :T],
                                                        start=(kc == 0), stop=(kc == 7)),
                         reads=XTK + [('wbig', ['q0', 'q1', 'k', 'k', 'v', 'v'][grp])], writes=[pk(b)])
        cbk = 'cb'
        cbv = cb[0][:, 0:4 * nseq * W].rearrange("p (c s w) -> p c s w", c=4, s=nseq)
        acck = [('acc', i) for i in range(4)]
        junk = tmb[:, :, :].rearrange("p a b -> p (a b)").bitcast(F32)

        def st1(grp):
            b = qkv_banks[grp]
            K.op('pool', lambda: nc.gpsimd.tensor_copy(out=cbv[:, :, :, 0:3], in_=cv4[:, grp * 4:grp * 4 + 4, :, :]),
                 reads=['carry'], writes=[cbk])
            K.op('act', lambda: nc.scalar.copy(out=cbv[:, :, :, 3:3 + L],
                                               in_=PS[b][:, 0:4 * T].rearrange("p (c s w) -> p c s w", c=4, s=nseq)),
                 reads=[pk(b)], writes=[cbk])
            reserved.discard(b)
            K.op('pool', lambda: nc.gpsimd.tensor_copy(out=cv4[:, grp * 4:grp * 4 + 4, :, :], in_=cbv[:, :, :, L:L + 3]),
                 reads=[cbk], writes=['carry'])

        def st2(grp):
            avs = [acc[:, c4, 0:T].rearrange("p (s w) -> p s w", s=nseq) for c4 in range(4)]
            for c4 in range(4):
                ct = grp * 4 + c4
                K.op('dve', lambda: nc.vector.tensor_scalar(out=avs[c4], in0=cbv[:, c4, :, 0:L], scalar1=convw[:, ct, 0:1], scalar2=None, op0=ALU.mult),
                     reads=[cbk, 'convw'], writes=[('acc', c4)])
            for j in range(1, 4):
                for c4 in range(4):
                    ct = grp * 4 + c4
                    K.op('dve', lambda: nc.vector.scalar_tensor_tensor(out=avs[c4], in0=cbv[:, c4, :, j:j + L], scalar=convw[:, ct, j:j + 1], in1=avs[c4],
                                                                       op0=ALU.mult, op1=ALU.add),
                         reads=[cbk, 'convw', ('acc', c4)], writes=[('acc', c4)])

        def st3(grp):
            silu_from(ebuf[:, :, :T], acc[:, :, :T], ebuf[:, :, :T], acck, ['ebuf'], ['ebuf'])

        st4 = {}
        st4c_b = {}

        def st4c(grp):
            if grp < 0 or grp >= 4:
                return
            b3 = bank()
            reserved.add(b3)
            st4c_b[grp] = b3
            psb = PS[b3][:, :].bitcast(BF16)
            for c4 in range(4):
                K.op('pe', lambda: nc.tensor.transpose(psb[:, c4 * T:(c4 + 1) * T], tmb[:T, c4, :], identb[:T, :T]),
                     reads=['tmb', 'identb'], writes=[pk(b3)])

        def st4d(grp):
            if grp < 0 or grp >= 4:
                return
            b3 = st4c_b[grp]
            psb = PS[b3][:, :].bitcast(BF16)
            isq = grp < 2
            h0 = (grp % 2) * 4
            dst = qT if isq else kT
            dk_ = ('qT' if isq else 'kT', p, grp % 2)
            K.op('act', lambda: nc.scalar.copy(out=dst[:, h0:h0 + 4, :T], in_=psb[:, 0:4 * T].rearrange("p (a t) -> p a t", a=4)),
                 reads=[pk(b3)], writes=[dk_])
            reserved.discard(b3)

        def st4a(grp):
            b2 = bank()
            reserved.add(b2)
            st4[grp] = b2
            for c4 in range(4):
                K.op('pe', lambda: nc.tensor.transpose(PS[b2][:T, c4 * 128:(c4 + 1) * 128], ebuf[:, c4, :T], identf[:, :]),
                     reads=['ebuf', 'identf'], writes=[pk(b2)])
            h0 = (grp % 2) * 4
            if grp < 4:
                isq = grp < 2
                col0 = (0 if isq else 8) + h0
                for c4 in range(4):
                    K.op('act', lambda: nc.scalar.activation(out=junk[:T, (c4 % 2) * 128:(c4 % 2) * 128 + 128], in_=PS[b2][:T, c4 * 128:(c4 + 1) * 128], func=AF.Square,
                                                             accum_out=ssq[:T, col0 + c4:col0 + c4 + 1]),
                         reads=[pk(b2)], writes=[('ssq', grp, c4), ('junk', c4 % 2)] + (['tmb'] if c4 < 2 else []))
                K.op('act', lambda: nc.scalar.activation(out=rn[:T, col0:col0 + 4], in_=ssq[:T, col0:col0 + 4], func=AF.Ln, bias=EPS_RMS[:T, :], scale=1.0),
                     reads=[('ssq', grp, c_) for c_ in range(4)] + ['eps'], writes=[('rn', grp)])
                if isq:
                    K.op('act', lambda: nc.scalar.activation(out=rn[:T, col0:col0 + 4], in_=rn[:T, col0:col0 + 4], func=AF.Exp, scale=-0.5,
                                                             bias=LOGQ[:T, :]),
                         reads=[('rn', grp), 'eps'], writes=[('rn', grp)])
                else:
                    K.op('act', lambda: nc.scalar.activation(out=rn[:T, col0:col0 + 4], in_=rn[:T, col0:col0 + 4], func=AF.Exp, scale=-0.5),
                         reads=[('rn', grp)], writes=[('rn', grp)])

        def st4b(grp):
            b2 = st4[grp]
            h0 = (grp % 2) * 4
            pv3 = PS[b2][:T, :].rearrange("p (h d) -> p h d", h=4)
            if grp < 4:
                isq = grp < 2
                col0 = (0 if isq else 8) + h0
                if isq:
                    scl = rn[:T, col0:col0 + 4]
                    sk_ = [('rn', grp)]
                else:
                    K.op('dve', lambda: nc.vector.tensor_tensor(out=sc_k[:T, h0:h0 + 4], in0=rn[:T, col0:col0 + 4], in1=gsm[:T, 4, h0:h0 + 4], op=ALU.mult),
                         reads=[('rn', grp), ('g', p, 4)], writes=[('sck', grp)])
                    scl = sc_k[:T, h0:h0 + 4]
                    sk_ = [('sck', grp)]
                K.op('dve', lambda: nc.vector.tensor_tensor(out=tmb[:T, :, :], in0=pv3, in1=scl.unsqueeze(2).broadcast_to([T, 4, 128]), op=ALU.mult),
                     reads=[pk(b2)] + sk_, writes=['tmb'])
                reserved.discard(b2)
                if not isq:
                    K.op('pool', lambda: nc.gpsimd.tensor_tensor(out=kd[:T, h0:h0 + 4, :], in0=tmb[:T, :, :],
                                                                 in1=gsm[:T, 6, h0:h0 + 4].unsqueeze(2).broadcast_to([T, 4, 128]), op=ALU.mult),
                         reads=['tmb', ('g', p, 6)], writes=[('kd', p, grp % 2)])
            else:
                K.op('dve', lambda: nc.vector.tensor_tensor(out=vp[:T, h0:h0 + 4, :], in0=pv3,
                                                            in1=gsm[:T, 4, h0:h0 + 4].unsqueeze(2).broadcast_to([T, 4, 128]), op=ALU.mult),
                     reads=[pk(b2), ('g', p, 4)], writes=[('vp', p, grp % 2)])
                reserved.discard(b2)

        qkv_proj(0)
        qkv_proj(1)
        st1(0)
        st2(0)
        for grp in range(6):
            st4c(grp - 1)
            if grp + 2 < 6:
                qkv_proj(grp + 2)
            if grp + 1 < 6:
                st1(grp + 1)
            yield 'Qit'
            st3(grp)
            st4d(grp - 1)
            yield 'Qit'
            st4a(grp)
            yield 'Qit'
            if grp + 1 < 6:
                st2(grp + 1)
            yield 'Qit'
            st4b(grp)
            tail_step()
            yield 'Qit'
        if ti.idx == NPT - 1:
            out_stamps.append(K.dma('sp', pconv_d[l], cv4[:, :, 0, :], reads=['carry'], writes=[('pconv', l)]))
        if ti.samp:
            for s_ in range(2):
                out_stamps.append(K.dma('sp', sconvo_d[l, s_], cv4[:, :, s_, :], reads=['carry'], writes=[('sconvo', l, s_)]))
        tail_flush()
        yield 'Qdone'
        cbf = cb[0]
        for h2 in range(2):
            b = bank()
            for kc in range(8):
                K.op('pe', lambda: nc.tensor.matmul(PS[b][:T, :], xT[:, kc, :T], wbig[:, kc, 3072 + h2 * 512:3072 + (h2 + 1) * 512],
                                                    start=(kc == 0), stop=(kc == 7)),
                     reads=XTK + [('wbig', 'z')], writes=[pk(b)])
            silu_from(og[:T, h2 * 512:(h2 + 1) * 512], PS[b][:T, :], cbf[:T, 0:512], [pk(b)], ['og'], ['cb'])
        yield 'EZdone'
        nlev = 5 if C == 64 else 4
        first_reg = [True]

        def regkeys():
            if first_reg[0]:
                first_reg[0] = False
                return [], ['REG']
            return ['REG'], []
        DTi4 = REG[:, 0:512].rearrange("p (h c) -> p h c", h=4)
        Ds4 = REG[:, 512:1024].rearrange("p (h c) -> p h c", h=4)
        hstate = {}
        AK = [('Aq', i) for i in range(4)]
        BK = [('Bq', i) for i in range(4)]
        PK = [('Pq', i) for i in range(4)]

        def h_prep(gq):
            hs = slice(gq * 4, gq * 4 + 4)
            rr, rw = regkeys()
            K.op('dve', lambda: nc.vector.tensor_tensor(out=DTi4[:T, :, :T], in0=identf[:T, :T].unsqueeze(1).broadcast_to([T, 4, T]),
                                                        in1=gsm[:T, 5, hs].unsqueeze(2).broadcast_to([T, 4, T]), op=ALU.mult),
                 reads=['identf', ('g', p, 5)] + rr, writes=['DTi'] + rw)
            bgq = bank()
            reserved.add(bgq)
            K.op('pe', lambda: nc.tensor.matmul(PS[bgq][:, 0:4 * T], onesf[:T, :], DTi4[:T, :, :T], start=True, stop=True),
                 reads=['onesf', 'DTi', 'REG'], writes=[pk(bgq)])
            gv = PS[bgq][:, 0:4 * T].rearrange("p (h c j) -> p h c j", h=4, c=nch)
            K.op('act', lambda: nc.scalar.activation(out=glbc[:, hs, :nch], in_=gv[:, :, :, C - 1], func=AF.Exp),
                 reads=[pk(bgq)], writes=[('glbc', p, gq)])
            gps = PS[bgq][:T, 0:4 * T].rearrange("p (h c) -> p h c", h=4)
            K.op('dve', lambda: nc.vector.tensor_tensor(out=DTi4[:T, :, :T], in0=gps, in1=gsm[:T, 5, hs].unsqueeze(2).broadcast_to([T, 4, T]), op=ALU.subtract),
                 reads=[pk(bgq), ('g', p, 5), 'REG'], writes=['DTi'])
            reserved.discard(bgq)
            K.op('dve', lambda: nc.vector.tensor_tensor(out=Ds4[:T, :, :T], in0=DTi4[:T, :, :T], in1=masks[:T, 1, :T].unsqueeze(1).broadcast_to([T, 4, T]), op=ALU.subtract),
                 reads=['DTi', 'masks', 'REG'], writes=['Ds'])
            K.op('dve', lambda: nc.vector.tensor_tensor(out=DTi4[:T, :, :T], in0=DTi4[:T, :, :T], in1=masks[:T, 0, :T].unsqueeze(1).broadcast_to([T, 4, T]), op=ALU.add),
                 reads=['DTi', 'masks', 'REG'], writes=['DTi'])
            K.op('act', lambda: nc.scalar.activation(out=Ds4[:T, :, :T], in_=Ds4[:T, :, :T], func=AF.Exp, scale=-1.0), reads=['Ds', 'REG'], writes=['Ds'])
            K.op('act', lambda: nc.scalar.activation(out=DTi4[:T, :, :T], in_=DTi4[:T, :, :T], func=AF.Exp), reads=['DTi', 'REG'], writes=['DTi'])
            bq = bank2()
            for h4 in range(4):
                h = gq * 4 + h4
                K.op('pe', lambda: nc.tensor.matmul(PS2(bq)[:T, h4 * 256:h4 * 256 + T], kT[:, h, :T], kT[:, h, :T], start=True, stop=True),
                     reads=[('kT', p, gq)], writes=[pk(bq), pk(bq + 1)])
                K.op('pe', lambda: nc.tensor.matmul(PS2(bq)[:T, h4 * 256 + T:h4 * 256 + 2 * T], kT[:, h, :T], qT[:, h, :T], start=True, stop=True),
                     reads=[('kT', p, gq), ('qT', p, gq)], writes=[pk(bq), pk(bq + 1)])
            pq = PS2(bq)[:T, :].rearrange("p (h c) -> p h c", h=4)
            K.op('dve', lambda: nc.vector.tensor_tensor(out=intraT[:T, hs, :T], in0=pq[:, :, T:2 * T], in1=DTi4[:T, :, :T], op=ALU.mult),
                 reads=[pk(bq), pk(bq + 1), 'DTi', 'REG'], writes=[('intraT', h_) for h_ in range(gq * 4, gq * 4 + 4)])
            hstate[gq] = (bq, pq)
            reserved.update([bq, bq + 1])

        def h_fin(gq):
            bq, pq = hstate[gq]
            K.op('dve', lambda: nc.vector.scalar_tensor_tensor(out=Aq[:T, :, :T], in0=pq[:, :, 0:T], scalar=-1.0, in1=Ds4[:T, :, :T], op0=ALU.mult, op1=ALU.mult),
                 reads=[pk(bq), pk(bq + 1), 'Ds', 'REG'], writes=AK)
            K.op('dve', lambda: nc.vector.scalar_tensor_tensor(out=BPq[:T, :, 0:T], in0=pq[:, :, 0:T], scalar=-1.0, in1=DTi4[:T, :, :T], op0=ALU.mult, op1=ALU.mult),
                 reads=[pk(bq), pk(bq + 1), 'DTi', 'REG'], writes=BK)
            reserved.discard(bq)
            reserved.discard(bq + 1)
            K.op('dve', lambda: nc.vector.tensor_tensor(out=BPq[:T, :, 0:T], in0=BPq[:T, :, 0:T].bitcast(F32), in1=masks[:T, 2, :T].unsqueeze(1).broadcast_to([T, 4, T]), op=ALU.mult),
                 reads=BK + ['masks'], writes=BK)
            K.op('dve', lambda: nc.vector.tensor_tensor(out=BPq[:T, :, T:2 * T], in0=BPq[:T, :, 0:T].bitcast(F32), in1=identf[:T, :T].unsqueeze(1).broadcast_to([T, 4, T]), op=ALU.add),
                 reads=BK + ['identf'], writes=PK)

        def h_chain(gq, hook=None):
            for lev in range(nlev + 1):
                last = lev == nlev
                if lev == 1 and hook is not None:
                    hook()
                if not last:
                    b2 = bank2()
                    b3 = bank()
                    for h4 in range(4):
                        ncols = T if lev == 0 else 2 * T
                        K.op('pe', lambda: nc.tensor.matmul(PS2(b2)[:T, h4 * 256:h4 * 256 + ncols], Aq[:T, h4, :T], BPq[:T, h4, 0:ncols], start=True, stop=True),
                             reads=[AK[h4], BK[h4]] + ([PK[h4]] if lev > 0 else []), writes=[pk(b2), pk(b2 + 1)])
                        K.op('pe', lambda: nc.tensor.matmul(PS[b3][:T, h4 * T:(h4 + 1) * T], BPq[:T, h4, 0:T], Aq[:T, h4, :T], start=True, stop=True),
                             reads=[AK[h4], BK[h4]], writes=[pk(b3)])
                    if SPLITLEV:
                        reserved.update([b2, b2 + 1, b3])
                        yield 'lev'
                        reserved.difference_update([b2, b2 + 1, b3])
                    pv = PS2(b2)[:T, :].rearrange("p (h c) -> p h c", h=4)
                    if lev > 0:
                        K.op('dve', lambda: nc.vector.tensor_tensor(out=BPq[:T, :, T:2 * T], in0=pv[:, :, T:2 * T], in1=BPq[:T, :, T:2 * T].bitcast(F32), op=ALU.add),
                             reads=[pk(b2), pk(b2 + 1)] + PK, writes=PK)
                    K.op('act', lambda: nc.scalar.copy(out=BPq[:T, :, 0:T], in_=pv[:, :, 0:T]), reads=[pk(b2), pk(b2 + 1)], writes=BK)
                    K.op('act', lambda: nc.scalar.copy(out=Aq[:T, :, :T], in_=PS[b3][:T, 0:4 * T].rearrange("p (h c) -> p h c", h=4)),
                         reads=[pk(b3)], writes=AK)
                    yield 'lev'
                else:
                    b3 = bank()
                    for h4 in range(4):
                        K.op('pe', lambda: nc.tensor.matmul(PS[b3][:T, h4 * T:(h4 + 1) * T], Aq[:T, h4, :T], BPq[:T, h4, T:2 * T], start=True, stop=True),
                             reads=[AK[h4], PK[h4]], writes=[pk(b3)])
                    K.op('dve', lambda: nc.vector.tensor_tensor(out=XT[:T, gq * 4:gq * 4 + 4, :T], in0=PS[b3][:T, 0:4 * T].rearrange("p (h c) -> p h c", h=4),
                                                                in1=BPq[:T, :, T:2 * T].bitcast(F32), op=ALU.add),
                         reads=[pk(b3)] + PK, writes=[('XT', gq)])
                    yield 'lev'

        h_prep(0)
        yield 'H'
        h_fin(0)
        yield 'H'
        for _ in h_chain(0, hook=lambda: h_prep(1)):
            yield 'H'
        h_fin(1)
        yield 'H'
        for _ in h_chain(1):
            yield 'H'
        yield 'Hdone'
        first_scan = [True]
        tail_flush()
        K.op('pool', lambda: nc.gpsimd.memset(oss, 0.0), writes=['oss'] + [('oss', h_) for h_ in range(8)])
        for c in range(nch):
            r0 = c * C
            rs = slice(r0, r0 + C)
            if ti.samp:
                for h in range(8):
                    K.dma('sp', S[:, h, :], sdel_d[l, c, h], writes=[('S', h // 4)])
                for gq in range(2):
                    K.op('act', lambda: nc.scalar.copy(out=Sb[:, gq * 4:gq * 4 + 4, :], in_=S[:, gq * 4:gq * 4 + 4, :]), reads=[('S', gq)], writes=[('Sb', gq)])
            GQ = (0, 1)
            hsl = [slice(gq * 4, gq * 4 + 4) for gq in GQ]
            bk_ = {}
            for gq in GQ:
                ba, bb_ = bank(), bank()
                bk_[('a', gq)], bk_[('b', gq)] = ba, bb_
                for h4 in range(4):
                    h = gq * 4 + h4
                    K.op('pe', lambda: nc.tensor.matmul(PS[ba][:T, h4 * 128:(h4 + 1) * 128], kT[:, h, :T], Sb[:, h, :], start=True, stop=True),
                         reads=[('kT', p, gq), ('Sb', gq)], writes=[pk(ba)])
                for h4 in range(4):
                    h = gq * 4 + h4
                    K.op('pe', lambda: nc.tensor.matmul(PS[bb_][:T, h4 * 128:(h4 + 1) * 128], qT[:, h, :T], Sb[:, h, :], start=True, stop=True),
                         reads=[('qT', p, gq), ('Sb', gq)], writes=[pk(bb_)])
            for gq in GQ:
                ba, bb_ = bk_[('a', gq)], bk_[('b', gq)]
                for h4 in range(4):
                    h = gq * 4 + h4
                    if first_scan[0]:
                        first_scan[0] = False
                        rr, rw = [], ['REG']
                    else:
                        rr, rw = ['REG'], []
                    K.op('dve', lambda: nc.vector.scalar_tensor_tensor(out=Rp[gq][rs, h4, :], in0=PS[ba][rs, h4 * 128:(h4 + 1) * 128], scalar=gsm[rs, 8, h:h + 1],
                                                                       in1=vp[rs, h, :], op0=ALU.mult, op1=ALU.add),
                         reads=[pk(ba), ('g', p, 8), ('vp', p, gq)] + rr, writes=[('Rp', gq, h4)] + rw)
                K.op('dve', lambda: nc.vector.tensor_tensor(out=on[rs, hsl[gq], :], in0=PS[bb_][rs, :].rearrange("p (h d) -> p h d", h=4),
                                                            in1=gsm[rs, 7, hsl[gq]].unsqueeze(2).broadcast_to([C, 4, 128]), op=ALU.mult),
                     reads=[pk(bb_), ('g', p, 7)], writes=ONK[gq * 4:gq * 4 + 4])
            for gq in GQ:
                bc_ = bank()
                bk_[('c', gq)] = bc_
                for h4 in range(4):
                    h = gq * 4 + h4
                    K.op('pe', lambda: nc.tensor.matmul(PS[bc_][:T, h4 * 128:(h4 + 1) * 128], XT[rs, h, :T], Rp[gq][rs, h4, :], start=True, stop=True),
                         reads=[('XT', gq), ('Rp', gq, h4), 'REG'], writes=[pk(bc_)])
            for gq in GQ:
                bc_ = bk_[('c', gq)]
                K.op('act', lambda: nc.scalar.copy(out=Yb[gq][rs, :, :], in_=PS[bc_][rs, :].rearrange("p (h d) -> p h d", h=4)),
                     reads=[pk(bc_), 'REG'], writes=[('Yb', gq)])
                K.op('pool', lambda: nc.gpsimd.tensor_tensor(out=S[:, hsl[gq], :], in0=S[:, hsl[gq], :], in1=glbc[:, hsl[gq], c:c + 1].broadcast_to([128, 4, 128]), op=ALU.mult),
                     reads=[('S', gq), ('glbc', p, gq)], writes=[('S', gq)])
            for gq in GQ:
                bd, be = bank(), bank()
                bk_[('d', gq)], bk_[('e', gq)] = bd, be
                for h4 in range(4):
                    h = gq * 4 + h4
                    K.op('pe', lambda: nc.tensor.matmul(PS[be][:, h4 * 128:(h4 + 1) * 128], kd[rs, h, :], Yb[gq][rs, h4, :], start=True, stop=True),
                         reads=[('kd', p, gq), ('Yb', gq), 'REG'], writes=[pk(be)])
                for h4 in range(4):
                    h = gq * 4 + h4
                    K.op('pe', lambda: nc.tensor.matmul(PS[bd][:T, h4 * 128:(h4 + 1) * 128], intraT[rs, h, :T], Yb[gq][rs, h4, :], start=True, stop=True),
                         reads=[('intraT', h), ('Yb', gq), 'REG'], writes=[pk(bd)])
            for gq in GQ:
                bd, be = bk_[('d', gq)], bk_[('e', gq)]
                K.op('dve', lambda: nc.vector.tensor_tensor(out=S[:, hsl[gq], :], in0=PS[be][:, :].rearrange("p (h d) -> p h d", h=4), in1=S[:, hsl[gq], :], op=ALU.add),
                     reads=[pk(be), ('S', gq)], writes=[('S', gq)])
                K.op('act', lambda: nc.scalar.copy(out=Sb[:, hsl[gq], :], in_=S[:, hsl[gq], :]), reads=[('S', gq)], writes=[('Sb', gq)])
                K.op('dve', lambda: nc.vector.tensor_tensor(out=on[rs, hsl[gq], :], in0=PS[bd][rs, :].rearrange("p (h d) -> p h d", h=4), in1=on[rs, hsl[gq], :], op=ALU.add),
                     reads=[pk(bd)] + ONK[gq * 4:gq * 4 + 4], writes=ONK[gq * 4:gq * 4 + 4])
            if ti.samp:
                for h in range(8):
                    out_stamps.append(K.dma('sp', sdelo_d[l, c, h], S[:, h, :], reads=[('S', h // 4)], writes=[('sdelo', l, c, h)]))
        if ti.idx == NPT - 1:
            for h in range(8):
                out_stamps.append(K.dma('sp', pdel_d[l, h], S[:, h, :], reads=[('S', h // 4)], writes=[('pdel', l, h)]))
        for h in range(8):
            K.op('act', lambda: nc.scalar.activation(out=tmb[:, :, :].rearrange("p a b -> p (a b)").bitcast(F32)[:T, (h % 2) * 128:(h % 2) * 128 + 128], in_=on[:T, h, :], func=AF.Square, accum_out=oss[:T, h:h + 1]),
                 reads=[('on', h)], writes=[('oss', h), ('junk', h % 2)] + (['tmb'] if h < 2 else []))
        K.op('act', lambda: nc.scalar.activation(out=oss[:T, :], in_=oss[:T, :], func=AF.Ln, bias=EPS_RMS[:T, :], scale=1.0 / 128.0),
             reads=[('oss', h_) for h_ in range(8)] + ['eps'], writes=['oss'])
        K.op('act', lambda: nc.scalar.activation(out=oss[:T, :], in_=oss[:T, :], func=AF.Exp, scale=-0.5), reads=['oss'], writes=['oss'])
        K.op('dve', lambda: nc.vector.tensor_tensor(out=on[:T, :, :], in0=on[:T, :, :], in1=oss[:T, :].unsqueeze(2).broadcast_to([T, 8, 128]), op=ALU.mult),
             reads=ONK + ['oss'], writes=ONK)
        K.op('dve', lambda: nc.vector.tensor_tensor(out=on[:T, :, :], in0=on[:T, :, :], in1=normw[:T, :].unsqueeze(1).broadcast_to([T, 8, 128]), op=ALU.mult),
             reads=ONK + ['normw'], writes=ONK)
        for h2 in range(2):
            sl = slice(h2 * 512, (h2 + 1) * 512)
            K.op('dve', lambda: nc.vector.tensor_tensor(out=og[:T, sl], in0=on[:T, h2 * 4:h2 * 4 + 4, :].rearrange("p h d -> p (h d)"), in1=og[:T, sl], op=ALU.mult),
                 reads=ONK + ['og'], writes=['og'])
        out_proj(ti, False, XT, [('XT', 0), ('XT', 1)], immediate=2)
        yield 'Sdone'

    def b_setup():
        K.full_sync()
        apos[0] = 0
        B = {}
        wflat = wbig[:, :, :].rearrange("p a b -> p (a b)")
        B['wflat'] = wflat
        B['KT'] = wflat[0:65, 20480:20480 + 4 * 2112].rearrange("p (g t) -> p g t", g=4)
        B['KTc'] = wflat[0:65, 28928:28928 + 1024].rearrange("p (g s t) -> p g s t", g=4, s=2)
        B['VEc'] = wflat[:, 29952:29952 + 520].rearrange("p (s g d) -> p s g d", s=2, g=4)
        B['PTn'] = wflat[0:64, 30472:30472 + 256].rearrange("p (h q) -> p h q", h=4)
        B['ogT'] = wflat[:, 30728:30728 + 1024].rearrange("p (a t) -> p a t", a=8)
        B['VE'] = ar([128, NT, 4, 65], BF16)
        B['kvf'] = ar([128, 512])
        B['kext'] = ar([128, 4, 65], BF16)
        B['qf'] = ar([128, 16, 64])
        B['qext'] = ar([128, 16, 65], BF16)
        B['QT'] = ar([128, 16, 128], BF16)
        B['rope'] = ar([128, NT, 2, 8])
        B['PTp'] = [ar([128, 4, 128], BF16) for _ in range(2)]
        B['PTc'] = [ar([128, 4, 128], BF16) for _ in range(2)]
        B['PTs'] = [[ar([128, 4, 64], BF16) for _ in range(2)] for _ in range(2)]
        B['ztz'] = ar([128, 1024])
        B['zt'] = B['ztz'][:, 0:512]
        B['zs'] = B['ztz'][:, 512:1024]
        B['sum8'] = ar([128, NT + 3])
        B['ksm'] = ar([128, 8, 4])
        B['qsm'] = ar([128, 6, 16])
        B['nshb'] = ar([128, 16], BF16)
        B['sinkb'] = ar([128, 16])
        B['den'] = ar([128, 2, 4])
        B['zsb'] = ar([128, D], BF16)
        print("arena used (B):", apos[0])
        return B

    def b_layer(j, B):
        wflat = B['wflat']
        wb3 = wflat[:, 0:16384].rearrange("p (kc n) -> p kc n", kc=8)
        if j == 0:
            K.dma('pool', wflat[:, 16384:16384 + 4096].rearrange("p (kc n) -> p kc n", kc=8), b_w_kv.rearrange("(kc p) n -> p kc n", p=128), writes=['wkv'])
        for (bname, c0, c1) in [('q0', 0, 512), ('q1', 512, 1024), ('z', 1024, 2048)]:
            K.dma('pool', wb3[:, :, c0:c1], b_w_in[j][:, c0:c1].rearrange("(kc p) n -> p kc n", p=128), writes=[('wbin', bname)])
        K.dma('sp', lng[:], lng_d[2 + j], writes=['lng'])
        K.dma('sp', lnb[:], lnb_d[2 + j], writes=['lnb'])
        K.dma('sp', B['sinkb'], sink_d[j], writes=['sinkb'])
        if j == 0:
            b_init(B)
        for t in range(NT):
            b_tile(j, TileInfo(t), B)
        tail_flush()

    def ksum8(B, kv3, T, col, scratch):
        ksm = B['ksm']
        scratch = B['zt'][:, 0:256].rearrange("p (g d) -> p g d", g=4)
        K.op('dve', lambda: nc.vector.tensor_tensor(out=scratch[:T, :, :], in0=kv3, in1=kv3, op=ALU.mult),
             reads=['kvf'], writes=['zt'])
        K.op('dve', lambda: nc.vector.tensor_reduce(out=ksm[:T, 0, :], in_=scratch[:T, :, :], axis=AX.X, op=ALU.add),
             reads=['zt'], writes=['ksm'])
        K.op('dve', lambda: nc.vector.tensor_reduce(out=ksm[:T, 1, 0:1], in_=ksm[:T, 0, :], axis=AX.X, op=ALU.max),
             reads=['ksm'], writes=['ksm'])
        K.op('dve', lambda: nc.vector.tensor_tensor(out=ksm[:T, 2, 0:1], in0=ksm[:T, 1, 0:1], in1=ksm[:T, 1, 0:1], op=ALU.mult),
             reads=['ksm'], writes=['ksm'])
        K.op('dve', lambda: nc.vector.tensor_tensor(out=ksm[:T, 3, 0:1], in0=ksm[:T, 2, 0:1], in1=ksm[:T, 2, 0:1], op=ALU.mult),
             reads=['ksm'], writes=['ksm'])
        b = bank()
        K.op('pe', lambda: nc.tensor.matmul(PS[b][:, 0:1], onesf[:T, :], ksm[:T, 3, 0:1], start=True, stop=True),
             reads=['onesf', 'ksm'], writes=[pk(b)])
        K.op('act', lambda: nc.scalar.copy(out=B['sum8'][:, col:col + 1], in_=PS[b][:, 0:1]), reads=[pk(b)], writes=[('sum8', col)])

    def kt_store(B, T, dst_fn, rkeys, wkey):
        b = bank()
        psb = PS[b][:, :].bitcast(BF16)
        for g in range(4):
            K.op('pe', lambda: nc.tensor.transpose(psb[0:65, g * T:(g + 1) * T], B['kext'][:T, g, :], identb[:T, :T]),
                 reads=['kext', 'identb'], writes=[pk(b)])
        for g in range(4):
            K.op('act', lambda: nc.scalar.copy(out=dst_fn(g), in_=psb[0:65, g * T:(g + 1) * T]), reads=[pk(b)], writes=[wkey])

    def b_init(B):
        K.dma('sp', B['rope'], rope_d, writes=['rope'])
        K.op('pool', lambda: nc.gpsimd.memset(B['kext'][:, :, 64:65], 1.0), writes=['kext'])
        K.op('pool', lambda: nc.gpsimd.memset(B['VE'][:, :, :, 64:65], 1.0), writes=['VE1'])
        K.op('pool', lambda: nc.gpsimd.memset(B['VEc'][:, :, :, 64:65], 1.0), writes=['VEc'])
        for i in range(2):
            K.op('pool', lambda: nc.gpsimd.memset(B['PTp'][i], 0.0), writes=[('PTp', i)])
            K.op('pool', lambda: nc.gpsimd.memset(B['PTc'][i], 0.0), writes=[('PTc', i)])
            for s_ in range(2):
                K.op('pool', lambda: nc.gpsimd.memset(B['PTs'][i][s_], 0.0), writes=[('PTs', i, s_)])
        K.op('pool', lambda: nc.gpsimd.memset(B['PTn'], 0.0), writes=['PTn'])
        K.op('pool', lambda: nc.gpsimd.memset(B['sum8'], 0.0), writes=[('sum8', c) for c in range(NT + 3)])
        ckf = B['qf'][:, 0:8, :].rearrange("p a b -> p (a b)")
        cvf = B['qf'][:, 8:16, :].rearrange("p a b -> p (a b)")
        for s_ in range(2):
            K.dma('sp', ckf[:, s_ * 256:(s_ + 1) * 256], ck_d[s_], writes=['qf'])
            K.dma('sp', cvf[:, s_ * 256:(s_ + 1) * 256], cv_d[s_], writes=['qf'])
        for s_ in range(2):
            out_stamps.append(K.dma('sp', sk_d[s_, 0:96, :], ckf[32:128, s_ * 256:(s_ + 1) * 256], reads=['qf'], writes=[('sk', s_, 0)]))
            out_stamps.append(K.dma('sp', sv_d[s_, 0:96, :], cvf[32:128, s_ * 256:(s_ + 1) * 256], reads=['qf'], writes=[('sv', s_, 0)]))
            K.op('act', lambda: nc.scalar.copy(out=B['kext'][:, :, 0:64], in_=ckf[:, s_ * 256:(s_ + 1) * 256].rearrange("p (g d) -> p g d", g=4)),
                 reads=['qf'], writes=['kext'])
            kt_store(B, 128, lambda g: B['KTc'][:, g, s_, :], ['kext'], 'KTc')
            K.op('act', lambda: nc.scalar.copy(out=B['VEc'][:, s_, :, 0:64], in_=cvf[:, s_ * 256:(s_ + 1) * 256].rearrange("p (g d) -> p g d", g=4)),
                 reads=['qf'], writes=['VEc'])
        ksm = B['ksm']
        scr = on
        kv8 = ckf.rearrange("p (a d) -> p a d", a=8)
        K.op('dve', lambda: nc.vector.tensor_tensor(out=scr[:, :, 0:64], in0=kv8, in1=kv8, op=ALU.mult), reads=['qf'], writes=ONK)
        K.op('dve', lambda: nc.vector.tensor_reduce(out=ksm[:, 4:6, :].rearrange("p a b -> p (a b)"), in_=scr[:, :, 0:64], axis=AX.X, op=ALU.add),
             reads=ONK, writes=['ksm'])
        K.op('dve', lambda: nc.vector.tensor_reduce(out=ksm[:, 1, 0:1], in_=ksm[:, 4:6, :].rearrange("p a b -> p (a b)"), axis=AX.X, op=ALU.max),
             reads=['ksm'], writes=['ksm'])
        K.op('dve', lambda: nc.vector.tensor_tensor(out=ksm[:, 2, 0:1], in0=ksm[:, 1, 0:1], in1=ksm[:, 1, 0:1], op=ALU.mult), reads=['ksm'], writes=['ksm'])
        K.op('dve', lambda: nc.vector.tensor_tensor(out=ksm[:, 3, 0:1], in0=ksm[:, 2, 0:1], in1=ksm[:, 2, 0:1], op=ALU.mult), reads=['ksm'], writes=['ksm'])
        b = bank()
        K.op('pe', lambda: nc.tensor.matmul(PS[b][:, 0:1], onesf[:, :], ksm[:, 3, 0:1], start=True, stop=True), reads=['onesf', 'ksm'], writes=[pk(b)])
        K.op('act', lambda: nc.scalar.copy(out=B['sum8'][:, NT:NT + 1], in_=PS[b][:, 0:1]), reads=[pk(b)], writes=[('sum8', NT)])

    def rope_inplace(B, v4, nh, T, t, key):
        cos = B['rope'][:T, t, 0, :].unsqueeze(1).broadcast_to([T, nh, 8])
        sin = B['rope'][:T, t, 1, :].unsqueeze(1).broadcast_to([T, nh, 8])
        rt = B['zt'][:, :].rearrange("p (a h d) -> p a h d", a=4, h=16)
        x1 = v4[:, :, 0:8]
        x2 = v4[:, :, 8:16]
        K.op('dve', lambda: nc.vector.tensor_tensor(out=rt[:T, 0, 0:nh, :], in0=x1, in1=cos, op=ALU.mult), reads=[key, 'rope'], writes=['zt'])
        K.op('dve', lambda: nc.vector.tensor_tensor(out=rt[:T, 1, 0:nh, :], in0=x2, in1=sin, op=ALU.mult), reads=[key, 'rope'], writes=[('zt', 1)])
        K.op('dve', lambda: nc.vector.tensor_tensor(out=rt[:T, 2, 0:nh, :], in0=x2, in1=cos, op=ALU.mult), reads=[key, 'rope'], writes=[('zt', 2)])
        K.op('dve', lambda: nc.vector.tensor_tensor(out=rt[:T, 3, 0:nh, :], in0=x1, in1=sin, op=ALU.mult), reads=[key, 'rope'], writes=[('zt', 3)])
        K.op('dve', lambda: nc.vector.tensor_tensor(out=x1, in0=rt[:T, 0, 0:nh, :], in1=rt[:T, 1, 0:nh, :], op=ALU.subtract), reads=['zt', ('zt', 1), ('zt', 2), ('zt', 3)], writes=[key])
        K.op('dve', lambda: nc.vector.tensor_tensor(out=x2, in0=rt[:T, 2, 0:nh, :], in1=rt[:T, 3, 0:nh, :], op=ALU.add), reads=['zt', ('zt', 1), ('zt', 2), ('zt', 3)], writes=[key])

    def b_tile(j, ti, B):
        T, t = ti.T, ti.idx
        wflat = B['wflat']
        KT, VE, QT, qf, qext, kvf, kext = B['KT'], B['VE'], B['QT'], B['qf'], B['qext'], B['kvf'], B['kext']
        tok0 = t * 128
        make_xT(ti)
        tail_step()
        if j == 0:
            b = bank()
            for kc in range(8):
                K.op('pe', lambda: nc.tensor.matmul(PS[b][:T, :], xT[:, kc, :T], wflat[:, 16384 + kc * 512:16384 + (kc + 1) * 512],
                                                    start=(kc == 0), stop=(kc == 7)),
                     reads=XTK + ['wkv'], writes=[pk(b)])
            K.op('act', lambda: nc.scalar.copy(out=kvf[:T, :], in_=PS[b][:T, :]), reads=[pk(b)], writes=['kvf'])
            kv3 = kvf[:T, 0:256].rearrange("p (g d) -> p g d", g=4)
            vv3 = kvf[:T, 256:512].rearrange("p (g d) -> p g d", g=4)
            rope_inplace(B, kv3, 4, T, t, 'kvf')
            K.op('act', lambda: nc.scalar.copy(out=kext[:T, :, 0:64], in_=kv3), reads=['kvf'], writes=['kext'])
            K.op('pool', lambda: nc.gpsimd.tensor_copy(out=VE[:T, t, :, 0:64], in_=vv3), reads=['kvf'], writes=[('VE', t)])
            tail_step()
            kt_store(B, T, lambda g: KT[:, g, tok0:tok0 + T], ['kext'], ('KT', t))
            ksum8(B, kv3, T, t, None)
            if t == NPT - 1:
                out_stamps.append(K.dma('sp', pk_d, kvf[:, 0:256], reads=['kvf'], writes=['pk']))
                out_stamps.append(K.dma('sp', pv_d, kvf[:, 256:512], reads=['kvf'], writes=['pv']))
            if ti.samp:
                for s_ in range(2):
                    out_stamps.append(K.dma('sp', sk_d[s_, 96:128, :], kvf[s_ * 32:(s_ + 1) * 32, 0:256], reads=['kvf'], writes=[('sk', s_, 1)]))
                    out_stamps.append(K.dma('sp', sv_d[s_, 96:128, :], kvf[s_ * 32:(s_ + 1) * 32, 256:512], reads=['kvf'], writes=[('sv', s_, 1)]))
        for h2 in range(2):
            b = bank()
            for kc in range(8):
                K.op('pe', lambda: nc.tensor.matmul(PS[b][:T, :], xT[:, kc, :T], wflat[:, kc * 2048 + h2 * 512:kc * 2048 + (h2 + 1) * 512],
                                                    start=(kc == 0), stop=(kc == 7)),
                     reads=XTK + [('wbin', 'q%d' % h2)], writes=[pk(b)])
            K.op('act', lambda: nc.scalar.activation(out=qf[:T, h2 * 8:h2 * 8 + 8, :], in_=PS[b][:T, :].rearrange("p (h d) -> p h d", h=8),
                                                     func=AF.Copy, scale=0.125),
                 reads=[pk(b)], writes=['qf'])
        tail_step()
        rope_inplace(B, qf[:T, :, :], 16, T, t, 'qf')
        tail_step()
        K.op('pool', lambda: nc.gpsimd.tensor_copy(out=qext[:T, :, 0:64], in_=qf[:T, :, :]), reads=['qf'], writes=['qext'])
        qsm = B['qsm']
        scr = B['ztz'][:, :].rearrange("p (h d) -> p h d", h=16)
        K.op('dve', lambda: nc.vector.tensor_tensor(out=scr[:T, :, :], in0=qf[:T, :, :], in1=qf[:T, :, :], op=ALU.mult), reads=['qf'], writes=['zt', 'zs'])
        K.op('dve', lambda: nc.vector.tensor_reduce(out=qsm[:T, 0, :], in_=scr[:T, :, :], axis=AX.X, op=ALU.add), reads=['zt', 'zs'], writes=['qsm0'])
        K.op('act', lambda: nc.scalar.activation(out=qsm[:T, 0, :], in_=qsm[:T, 0, :], func=AF.Ln, bias=EPS_RMS[:T, :], scale=1.0), reads=['qsm0', 'eps'], writes=['qsm0'])
        K.op('act', lambda: nc.scalar.activation(out=qsm[:T, 0, :], in_=qsm[:T, 0, :], func=AF.Exp, scale=0.5), reads=['qsm0'], writes=['qsm0'])
        tail_step()
        cprev = NT if ti.samp else (t - 1 if t > 0 else NT + 1)
        K.op('dve', lambda: nc.vector.tensor_tensor(out=qsm[:T, 1, 0:1], in0=B['sum8'][:T, t:t + 1], in1=B['sum8'][:T, cprev:cprev + 1], op=ALU.add),
             reads=[('sum8', t), ('sum8', cprev)], writes=['qsm1'])
        K.op('act', lambda: nc.scalar.activation(out=qsm[:T, 1, 0:1], in_=qsm[:T, 1, 0:1], func=AF.Ln), reads=['qsm1'], writes=['qsm1'])
        K.op('act', lambda: nc.scalar.activation(out=qsm[:T, 1, 0:1], in_=qsm[:T, 1, 0:1], func=AF.Exp, scale=0.125), reads=['qsm1'], writes=['qsm1'])
        K.op('dve', lambda: nc.vector.tensor_scalar(out=B['nshb'][:T, :], in0=qsm[:T, 0, :], scalar1=qsm[:T, 1, 0:1], scalar2=-1.0, op0=ALU.mult, op1=ALU.mult),
             reads=['qsm0', 'qsm1'], writes=['nshb'])
        K.op('pool', lambda: nc.gpsimd.tensor_copy(out=qext[:T, :, 64:65], in_=B['nshb'][:T, :].unsqueeze(2)), reads=['nshb'], writes=['qext'])
        K.op('dve', lambda: nc.vector.tensor_tensor(out=qsm[:T, 2, :], in0=B['nshb'][:T, :], in1=B['sinkb'][:T, :], op=ALU.add),
             reads=['nshb', 'sinkb'], writes=['qsm2'])
        K.op('act', lambda: nc.scalar.activation(out=qsm[:T, 2, :], in_=qsm[:T, 2, :], func=AF.Exp), reads=['qsm2'], writes=['qsm2'])
        for h2 in range(2):
            b = bank()
            psb = PS[b][:, :].bitcast(BF16)
            for hq in range(8):
                h = h2 * 8 + hq
                K.op('pe', lambda: nc.tensor.transpose(psb[0:65, hq * T:(hq + 1) * T], qext[:T, h, :], identb[:T, :T]),
                     reads=['qext', 'identb'], writes=[pk(b)])
            K.op('act', lambda: nc.scalar.copy(out=QT[0:65, h2 * 8:h2 * 8 + 8, :T], in_=psb[0:65, 0:8 * T].rearrange("p (a t) -> p a t", a=8)),
                 reads=[pk(b)], writes=[('QT', h2)])
        tail_step()
        tail_flush()
        for h2 in range(2):
            b = bank()
            for kc in range(8):
                K.op('pe', lambda: nc.tensor.matmul(PS[b][:T, :], xT[:, kc, :T], wflat[:, kc * 2048 + 1024 + h2 * 512:kc * 2048 + 1024 + (h2 + 1) * 512],
                                                    start=(kc == 0), stop=(kc == 7)),
                     reads=XTK + [('wbin', 'z')], writes=[pk(b)])
            silu_from(B['zsb'][:T, h2 * 512:(h2 + 1) * 512], PS[b][:T, :], B['ztz'][:T, h2 * 512:(h2 + 1) * 512], [pk(b)], [('zsb', h2)], [['zt', 'zs'][h2]])
        den = B['den']

        def att_norm(g, bo):
            pb_ = g % 2
            po = PS[bo][:T, 0:260].rearrange("p (h d) -> p h d", h=4)
            K.op('dve', lambda: nc.vector.tensor_tensor(out=den[:T, pb_, :], in0=po[:, :, 64], in1=qsm[:T, 2, 4 * g:4 * g + 4], op=ALU.add),
                 reads=[pk(bo), 'qsm2'], writes=[('den', pb_)])
            K.op('dve', lambda: nc.vector.reciprocal(out=den[:T, pb_, :], in_=den[:T, pb_, :]), reads=[('den', pb_)], writes=[('den', pb_)])
            K.op('dve', lambda: nc.vector.tensor_tensor(out=qf[:T, 4 * g:4 * g + 4, :], in0=po[:, :, 0:64],
                                                        in1=den[:T, pb_, :].unsqueeze(2).broadcast_to([T, 4, 64]), op=ALU.mult),
                 reads=[pk(bo), ('den', pb_)], writes=[('ob', g)])

        if not ti.samp:
            sbk = {}

            def att_scores(g):
                qk = ('QT', g // 2)
                rhsq = QT[0:65, 4 * g:4 * g + 4, :T]
                b1 = None
                if t > 0:
                    b1 = bank()
                    reserved.add(b1)
                    K.op('pe', lambda: nc.tensor.matmul(PS[b1][:, 0:4 * T], KT[:, g, tok0 - 128:tok0], rhsq, start=True, stop=True),
                         reads=[('KT', t - 1), qk], writes=[pk(b1)])
                b2 = bank()
                reserved.add(b2)
                K.op('pe', lambda: nc.tensor.matmul(PS[b2][:, 0:4 * T], KT[:, g, tok0:tok0 + 128], rhsq, start=True, stop=True),
                     reads=[('KT', t), qk], writes=[pk(b2)])
                sbk[g] = (b1, b2)

            def att_exp(g):
                pb_ = g % 2
                PTp, PTc = B['PTp'][pb_], B['PTc'][pb_]
                b1, b2 = sbk[g]
                if t > 0:
                    v1 = PS[b1][:, 0:4 * T].rearrange("p (h q) -> p h q", h=4)
                    K.op('act', lambda: nc.scalar.activation(out=PTp[0:64, :, 0:64], in_=v1[0:64, :, 0:64], func=AF.Exp), reads=[pk(b1)], writes=[('PTp', pb_)])
                    K.op('act', lambda: nc.scalar.activation(out=PTp[64:128, :, :], in_=v1[64:128, :, :], func=AF.Exp), reads=[pk(b1)], writes=[('PTp', pb_)])
                    reserved.discard(b1)
                v2 = PS[b2][:, 0:4 * T].rearrange("p (h q) -> p h q", h=4)
                K.op('act', lambda: nc.scalar.activation(out=PTc[0:64, :, :], in_=v2[0:64, :, :], func=AF.Exp), reads=[pk(b2)], writes=[('PTc', pb_)])
                K.op('act', lambda: nc.scalar.activation(out=PTc[64:128, :, 64:128], in_=v2[64:128, :, 64:128], func=AF.Exp), reads=[pk(b2)], writes=[('PTc', pb_)])
                reserved.discard(b2)

            def att_pv(g):
                pb_ = g % 2
                PTp, PTc = B['PTp'][pb_], B['PTc'][pb_]
                bo = bank()
                for hh in range(4):
                    if t > 0:
                        K.op('pe', lambda: nc.tensor.matmul(PS[bo][:T, hh * 65:(hh + 1) * 65], PTp[:, hh, :], VE[:, t - 1, g, :], start=True, stop=False),
                             reads=[('PTp', pb_), ('VE', t - 1), 'VE1'], writes=[pk(bo)])
                    K.op('pe', lambda: nc.tensor.matmul(PS[bo][:T, hh * 65:(hh + 1) * 65], PTc[:, hh, :], VE[:, t, g, :], start=(t == 0), stop=True),
                         reads=[('PTc', pb_), ('VE', t), 'VE1'], writes=[pk(bo)])
                return bo

            att_scores(0)
            for g in range(4):
                if g + 1 < 4:
                    att_scores(g + 1)
                att_exp(g)
                if g > 0:
                    bo = att_pv(g - 1)
                    att_norm(g - 1, bo)
            bo = att_pv(3)
            att_norm(3, bo)
        else:
            for g in range(4):
                pb_ = g % 2
                qk = ('QT', g // 2)
                bo = bank()
                PTs = B['PTs'][pb_]
                PTn = B['PTn']
                for s_ in range(2):
                    b1 = bank()
                    K.op('pe', lambda: nc.tensor.matmul(PS[b1][:, 0:128], B['KTc'][:, g, s_, :], QT[0:65, 4 * g:4 * g + 4, s_ * 32:(s_ + 1) * 32], start=True, stop=True),
                         reads=['KTc', qk], writes=[pk(b1)])
                    K.op('act', lambda: nc.scalar.activation(out=PTs[s_][:, :, s_ * 32:(s_ + 1) * 32], in_=PS[b1][:, 0:128].rearrange("p (h q) -> p h q", h=4), func=AF.Exp),
                         reads=[pk(b1)], writes=[('PTs', pb_, s_)])
                b2 = bank()
                K.op('pe', lambda: nc.tensor.matmul(PS[b2][0:64, 0:256], KT[:, g, tok0:tok0 + 64], QT[0:65, 4 * g:4 * g + 4, 0:64], start=True, stop=True),
                     reads=[('KT', t), qk], writes=[pk(b2)])
                v2 = PS[b2][0:64, 0:256].rearrange("p (h q) -> p h q", h=4)
                K.op('act', lambda: nc.scalar.activation(out=PTn[0:32, :, 0:32], in_=v2[0:32, :, 0:32], func=AF.Exp), reads=[pk(b2)], writes=['PTn'])
                K.op('act', lambda: nc.scalar.activation(out=PTn[32:64, :, 32:64], in_=v2[32:64, :, 32:64], func=AF.Exp), reads=[pk(b2)], writes=['PTn'])
                for hh in range(4):
                    K.op('pe', lambda: nc.tensor.matmul(PS[bo][:T, hh * 65:(hh + 1) * 65], PTs[0][:, hh, :], B['VEc'][:, 0, g, :], start=True, stop=False),
                         reads=[('PTs', pb_, 0), 'VEc'], writes=[pk(bo)])
                    K.op('pe', lambda: nc.tensor.matmul(PS[bo][:T, hh * 65:(hh + 1) * 65], PTs[1][:, hh, :], B['VEc'][:, 1, g, :], start=False, stop=False),
                         reads=[('PTs', pb_, 1), 'VEc'], writes=[pk(bo)])
                    K.op('pe', lambda: nc.tensor.matmul(PS[bo][:T, hh * 65:(hh + 1) * 65], PTn[:, hh, :], VE[0:64, t, g, :], start=False, stop=True),
                         reads=['PTn', ('VE', t), 'VE1'], writes=[pk(bo)])
                att_norm(g, bo)
        obf = qf[:, :, :].rearrange("p a b -> p (a b)")
        for h2 in range(2):
            sl = slice(h2 * 512, (h2 + 1) * 512)
            K.op('dve', lambda: nc.vector.tensor_tensor(out=og[:T, sl], in0=obf[:T, sl], in1=B['zsb'][:T, sl], op=ALU.mult),
                 reads=[('ob', 2 * h2), ('ob', 2 * h2 + 1), ('zsb', h2)], writes=['og'])
        out_proj(ti, j == 1, B['ogT'], ['ogTb'])

    done = False
    build.marks = []
    for l in range(2):
        done = a_layer(l)
        build.marks.append(K.ninstr['pe'])
        if done:
            break
    if not done and stop_after != 'A':
        Bv = b_setup()
        for j in range(2):
            b_layer(j, Bv)
            build.marks.append(K.ninstr['pe'])
    if done or stop_after == 'A':
        for t in range(NT):
            T = 64 if t == NPT else 128
            out_stamps.append(K.dma('sp', y_d[t * 128:t * 128 + T, :], xres[:T, t, :], reads=[('x', t)], writes=[('y', t)]))
    K.wait_all('sp', out_stamps)
    K.barrier()
    es.close()
    build.stats = (dict(K.ninstr), K.nwait)
    build.sim = K.simulate()
    return nc


def prep_inputs(inputs):
    c = host_consts()
    f = lambda a: np.ascontiguousarray(np.asarray(a, dtype=np.float32))
    xp = f(inputs['x_prompt'])
    xs = f(inputs['x_sample'])
    sd = f(inputs['state_delta'])
    sc = f(inputs['state_conv'])
    ck = f(inputs['cache_k'])
    cv = f(inputs['cache_v'])
    acw = f(inputs['a_conv_w'])
    convw = np.ascontiguousarray(acw.reshape(2, 4, 24, 128).transpose(0, 3, 2, 1))
    bc = lambda v, n: np.ascontiguousarray(np.broadcast_to(f(v)[:, None, :], (v.shape[0], 128, n)))
    shared = {
        'a_w_in': f(inputs['a_w_in']), 'a_w_out': f(inputs['a_w_out']), 'b_w_kv': f(inputs['b_w_kv']),
        'b_w_in': f(inputs['b_w_in']), 'b_w_out': f(inputs['b_w_out']),
        'convw': convw, 'alog_bc': bc(inputs['a_log'], 8), 'dtb_bc': bc(inputs['a_dt_bias'], 8),
        'normw_bc': bc(inputs['a_norm_w'], 128),
        'lng_bc': np.ascontiguousarray(np.concatenate([bc(inputs['a_ln_g'], D), bc(inputs['b_ln_g'], D)], 0)),
        'lnb_bc': np.ascontiguousarray(np.concatenate([bc(inputs['a_ln_b'], D), bc(inputs['b_ln_b'], D)], 0)),
        'sink_bc': bc(inputs['b_sinks'], 16),
        'identf': c['identf'], 'onesf': c['onesf'], 'masks': c['masks'], 'rope': c['rope'],
    }
    maps = []
    for i in range(NCORES):
        m = dict(shared)
        m['x'] = np.ascontiguousarray(np.concatenate([xp[i], xs[2 * i], xs[2 * i + 1]], 0))
        m['sdelta'] = np.ascontiguousarray(sd[:, 2 * i:2 * i + 2])
        scc = sc[:, 2 * i:2 * i + 2]
        m['sconv'] = np.ascontiguousarray(scc.reshape(2, 2, 3, 24, 128).transpose(0, 1, 4, 3, 2))
        m['ck'] = np.ascontiguousarray(ck[2 * i:2 * i + 2].reshape(2, 128, 256))
        m['cv'] = np.ascontiguousarray(cv[2 * i:2 * i + 2].reshape(2, 128, 256))
        maps.append(m)
    return maps


_NC_CACHE = {}


def kernel(**inputs):
    maps = prep_inputs(inputs)
    if 'nc' not in _NC_CACHE:
        _NC_CACHE['nc'] = build()
    nc = _NC_CACHE['nc']
    maps = [{n: m[n] for n in build.in_names} for m in maps]
    r = run_bass_kernel_spmd(nc, maps, core_ids=list(range(NCORES))).results
    y = np.stack([r[i]['y'] for i in range(NCORES)])
    y_prompt = np.ascontiguousarray(y[:, :SEQ])
    y_sample = np.ascontiguousarray(y[:, SEQ:].reshape(16, 32, D))
    p_delta = np.stack([r[i]['pdelta'] for i in range(NCORES)], 1)
    s_delta = np.concatenate([r[i]['sdelta_o'] for i in range(NCORES)], 1)
    pc = np.stack([r[i]['pconv'] for i in range(NCORES)], 1)
    p_conv = np.ascontiguousarray(pc.transpose(0, 1, 4, 3, 2).reshape(2, 8, 3, 3072))
    scv = np.concatenate([r[i]['sconv_o'] for i in range(NCORES)], 1)
    s_conv = np.ascontiguousarray(scv.transpose(0, 1, 4, 3, 2).reshape(2, 16, 3, 3072))
    if WITH_B:
        p_k = np.stack([r[i]['pk'] for i in range(NCORES)]).reshape(8, 128, 4, 64)
        p_v = np.stack([r[i]['pv'] for i in range(NCORES)]).reshape(8, 128, 4, 64)
        s_k = np.concatenate([r[i]['sk'] for i in range(NCORES)]).reshape(16, 128, 4, 64)
        s_v = np.concatenate([r[i]['sv'] for i in range(NCORES)]).reshape(16, 128, 4, 64)
    else:
        p_k = np.zeros((8, 128, 4, 64), np.float32)
        p_v = np.zeros((8, 128, 4, 64), np.float32)
        s_k = np.zeros((16, 128, 4, 64), np.float32)
        s_v = np.zeros((16, 128, 4, 64), np.float32)
    outs = (y_prompt, y_sample, p_delta, p_conv, p_k, p_v, s_delta, s_conv, s_k, s_v)
    return tuple(np.ascontiguousarray(o.astype(np.float32)) for o in outs)
```

```python
import numpy as np
import ml_dtypes
from contextlib import ExitStack
import concourse.bass as bass
import concourse.mybir as mybir
from concourse.bass_utils import run_bass_kernel_spmd

F32 = mybir.dt.float32
BF16 = mybir.dt.bfloat16
F32R = mybir.dt.float32r
AF = mybir.ActivationFunctionType
ALU = mybir.AluOpType
AX = mybir.AxisListType

NCORES = 8
D = 1024
SEQ = 2048
NPT = 16
NT = 17
ROWS = SEQ + 64
AIN = 4112
DN_ALPHA = 8.0 ** 0.25
LN_EPS = 1e-5
RMS_EPS = 1e-6
NEG = -30000.0
PAST = 4096


class Trk:
    NDS = 20

    def __init__(self, nc, es):
        self.nc = nc
        self.eng = {'pe': nc.tensor, 'act': nc.scalar, 'dve': nc.vector, 'pool': nc.gpsimd, 'sp': nc.sync}
        self.semh = {}
        for e in ['pe', 'act', 'dve', 'pool']:
            self.semh[e] = es.enter_context(nc.semaphore("s_" + e))
        for i in range(self.NDS):
            self.semh[('d', i)] = es.enter_context(nc.semaphore("s_d%d" % i))
            self.semh[('w', i)] = es.enter_context(nc.semaphore("s_w%d" % i))
        self.cnt = {k: 0 for k in self.semh}
        self.seen = {e: {} for e in self.eng}
        self.clock = {}
        self.lastw = {}
        self.readers = {}
        self.dnext = {'sp': 0, 'pool': 0}
        self.ninstr = {e: 0 for e in self.eng}
        self.nwait = 0
        self.prog = {e: [] for e in self.eng}

    def _merge(self, e, stamp):
        s = self.seen[e]
        k, v = stamp
        if s.get(k, 0) < v:
            s[k] = v
        for kk, vv in self.clock.get(stamp, {}).items():
            if s.get(kk, 0) < vv:
                s[kk] = vv

    def _deps(self, e, reads, writes, extra=()):
        deps = {}

        def add(st):
            if st is None:
                return
            k, v = st
            if k == e and e == 'pe':
                return
            if deps.get(k, 0) < v:
                deps[k] = v
        for r in reads:
            add(self.lastw.get(r))
        for w in writes:
            add(self.lastw.get(w))
            for k, v in self.readers.get(w, {}).items():
                add((k, v))
        for st in extra:
            add(st)
        need = [(k, v) for k, v in deps.items() if self.seen[e].get(k, 0) < v]
        return need

    def _emit(self, e, fn, need):
        eng = self.eng[e]
        for (k, v) in need[:-1]:
            eng.wait_ge(self.semh[k], v)
            self.nwait += 1
        ins = fn()
        if need:
            k, v = need[-1]
            ins._wait_ge(self.semh[k], v)
        self.prog[e].append([list(need), None])
        for st in need:
            self._merge(e, st)
        self.ninstr[e] += 1
        return ins

    def _finish(self, e, stamp, reads, writes):
        self.clock[stamp] = dict(self.seen[e])
        for w in writes:
            self.lastw[w] = stamp
            self.readers[w] = {}
        for r in reads:
            d = self.readers.setdefault(r, {})
            k, v = stamp
            if d.get(k, 0) < v:
                d[k] = v

    def op(self, e, fn, reads=(), writes=()):
        psr = [r for r in reads if isinstance(r, tuple) and r[0] == 'ps' and r not in writes]
        if psr:
            writes = list(writes) + psr
        need = self._deps(e, reads, writes)
        ins = self._emit(e, fn, need)
        self.cnt[e] += 1
        ins.then_inc(self.semh[e], 1)
        self.prog[e][-1][1] = (e, 1)
        stamp = (e, self.cnt[e])
        self.seen[e][e] = self.cnt[e] if e == 'pe' else self.seen[e].get(e, 0)
        self._finish(e, stamp, reads, writes)
        return ins

    def dma(self, q, out, in_, reads=(), writes=(), **kw):
        key = ('d' if q == 'sp' else 'w', self.dnext[q])
        self.dnext[q] = (self.dnext[q] + 1) % self.NDS
        prev = (key, self.cnt[key]) if self.cnt[key] > 0 else None
        need = self._deps(q, reads, writes, extra=(prev,) if prev else ())
        ins = self._emit(q, lambda: self.eng[q].dma_start(out=out, in_=in_, **kw), need)
        self.cnt[key] += 16
        ins.then_inc(self.semh[key], 16)
        self.prog[q][-1][1] = (key, 16)
        stamp = (key, self.cnt[key])
        self._finish(q, stamp, reads, writes)
        return stamp

    def simulate(self):
        sem = {k: 0 for k in self.semh}
        pc = {e: 0 for e in self.eng}
        progress = True
        while progress:
            progress = False
            for e in self.eng:
                while pc[e] < len(self.prog[e]):
                    waits, inc = self.prog[e][pc[e]]
                    if all(sem[k] >= v for k, v in waits):
                        if inc:
                            sem[inc[0]] += inc[1]
                        pc[e] += 1
                        progress = True
                    else:
                        break
        stuck = {e: (pc[e], len(self.prog[e]), self.prog[e][pc[e]][0] if pc[e] < len(self.prog[e]) else None) for e in self.eng}
        return stuck, {str(k): v for k, v in sem.items() if v}

    def full_sync(self):
        for e in self.eng:
            for k in list(self.semh.keys()):
                v = self.cnt[k]
                if v > 0 and k != e and self.seen[e].get(k, 0) < v:
                    self.eng[e].wait_ge(self.semh[k], v)
                    self.prog[e].append([[(k, v)], None])
                    self.seen[e][k] = v

    def barrier(self):
        for e in self.eng:
            for k in list(self.semh.keys()):
                v = self.cnt[k]
                if v > 0 and k != e:
                    self.eng[e].wait_ge(self.semh[k], v)

    def wait_all(self, e, stamps):
        for (k, v) in stamps:
            if self.seen[e].get(k, 0) < v:
                self.eng[e].wait_ge(self.semh[k], v)
                self.seen[e][k] = v


class TileInfo:
    def __init__(self, idx):
        self.idx = idx
        self.samp = idx == NPT
        self.T = 64 if self.samp else 128
        self.nseq = 2 if self.samp else 1
        self.L = self.T // self.nseq
        self.C = 32 if self.samp else 64
        self.nch = self.T // self.C
        self.m = 1 if self.samp else 0


def host_consts():
    c = {}
    c['identf'] = np.eye(128, dtype=np.float32)
    c['onesf'] = np.ones((128, 128), dtype=np.float32)
    masks = np.zeros((2, 5, 128, 128), dtype=np.float32)
    for m, (T, C) in enumerate([(128, 64), (64, 32)]):
        p = np.arange(T)[:, None]
        f = np.arange(T)[None, :]
        same = (p // C) == (f // C)
        masks[m, 0, :T, :T] = np.where(same & (f >= p), 0.0, NEG)
        masks[m, 1, :T, :T] = np.where(same & (f < p), 0.0, NEG)
        masks[m, 2, :T, :T] = np.where(p != f, 1.0, 0.0)
        masks[m, 3, :T, :T] = np.where(same & (p <= f), 1.0, 0.0)
        masks[m, 4, :T, :T] = np.where(same, 1.0, 0.0)
    c['masks'] = masks.transpose(2, 0, 1, 3).copy()
    half = 8
    inv = 500000.0 ** (-np.arange(half, dtype=np.float32) * 2.0 / 16.0)
    pos = np.zeros((NT, 128), dtype=np.float32)
    for t in range(NPT):
        pos[t] = t * 128 + np.arange(128)
    pos[NPT, :64] = PAST + (np.arange(64) % 32)
    ang = pos[:, :, None].astype(np.float32) * inv[None, None, :].astype(np.float32)
    cs = np.stack([np.cos(ang), np.sin(ang)], axis=2).astype(np.float32)
    c['rope'] = cs.transpose(1, 0, 2, 3).copy()
    return c


WITH_B = True
import os as _os
HRATIO = int(_os.environ.get('HRATIO', '1'))
HPER = int(_os.environ.get('HPER', '1'))
SPLITLEV = int(_os.environ.get('SPLITLEV', '1'))
HSKIP = [int(x) for x in _os.environ.get('HSKIP', '').split(',') if x]


def build(stop_after=None, taps=()):
    nc = bass.Bass("TRN2", target_bir_lowering=False)
    es = ExitStack()
    K = Trk(nc, es)
    tapped = {}

    in_names = []
    build.in_names = in_names

    def din(name, shape, dt=F32):
        in_names.append(name)
        return nc.dram_tensor(name, list(shape), dt, kind="ExternalInput").ap()

    def dout(name, shape, dt=F32):
        return nc.dram_tensor(name, list(shape), dt, kind="ExternalOutput").ap()

    x_d = din("x", [ROWS, D])
    a_w_in = din("a_w_in", [2, D, AIN])
    a_w_out = din("a_w_out", [2, D, D])
    if WITH_B:
        b_w_kv = din("b_w_kv", [D, 512])
    if WITH_B:
        b_w_in = din("b_w_in", [2, D, 2048])
    if WITH_B:
        b_w_out = din("b_w_out", [2, D, D])
    convw_d = din("convw", [2, 128, 24, 4])
    alog_d = din("alog_bc", [2, 128, 8])
    dtb_d = din("dtb_bc", [2, 128, 8])
    normw_d = din("normw_bc", [2, 128, 128])
    lng_d = din("lng_bc", [4, 128, D])
    lnb_d = din("lnb_bc", [4, 128, D])
    if WITH_B:
        sink_d = din("sink_bc", [2, 128, 16])
    sdel_d = din("sdelta", [2, 2, 8, 128, 128])
    sconv_d = din("sconv", [2, 2, 128, 24, 3])
    if WITH_B:
        ck_d = din("ck", [2, 128, 256])
    if WITH_B:
        cv_d = din("cv", [2, 128, 256])
    identf_d = din("identf", [128, 128])
    onesf_d = din("onesf", [128, 128])
    masks_d = din("masks", [128, 2, 5, 128])
    if WITH_B:
        rope_d = din("rope", [128, NT, 2, 8])

    y_d = dout("y", [ROWS, D])
    pdel_d = dout("pdelta", [2, 8, 128, 128])
    sdelo_d = dout("sdelta_o", [2, 2, 8, 128, 128])
    pconv_d = dout("pconv", [2, 128, 24, 3])
    sconvo_d = dout("sconv_o", [2, 2, 128, 24, 3])
    if WITH_B:
        pk_d = dout("pk", [128, 256])
    if WITH_B:
        pv_d = dout("pv", [128, 256])
    if WITH_B:
        sk_d = dout("sk", [2, 128, 256])
    if WITH_B:
        sv_d = dout("sv", [2, 128, 256])
    tap_d = {}

    def sb(name, shape, dt=F32):
        return nc.alloc_sbuf_tensor("sb_" + name, list(shape), dt)

    xres = sb("xres", [128, NT, D])
    wbig = sb("wbig", [128, 8, AIN], BF16)
    wring = sb("wring", [128, 4, D], BF16)
    identf = sb("identf", [128, 128])
    identb = sb("identb", [128, 128], BF16)
    onesf = sb("onesf", [128, 128])
    masks = sb("masks", [128, 5, 128])
    lng = sb("lng", [128, D])
    lnb = sb("lnb", [128, D])
    xT = sb("xT", [128, 8, 128], BF16)
    ogT = xT
    og = sb("og", [128, D], BF16)
    on = sb("on", [128, 8, 128])
    res = on[:, :, :].rearrange("p h d -> p (h d)")
    bst = sb("bst", [128, 2, 6])
    mv = sb("mv", [128, 2])
    rstd = sb("rstd", [128, 1])
    Aq = sb("Aq", [128, 4, 128], F32R)
    BPq = sb("BPq", [128, 4, 256], F32R)
    ARENA = 10580
    arena = sb("arena", [128, ARENA])
    apos = [0]

    def ar(shape, dt=F32):
        n = int(np.prod(shape[1:]))
        nf = n if dt == F32 or dt == F32R else (n + 1) // 2
        nf = (nf + 7) // 8 * 8
        o = apos[0]
        apos[0] += nf
        assert apos[0] <= ARENA, ("arena overflow", apos[0])
        v = arena[:, o:o + nf]
        if dt != F32:
            v = v.bitcast(dt)
        v = v[:, 0:n]
        if len(shape) == 3:
            v = v.rearrange("p (a b) -> p a b", a=shape[1])
        elif len(shape) == 4:
            v = v.rearrange("p (a b c) -> p a b c", a=shape[1], b=shape[2])
        return v

    convw = ar([128, 24, 4])
    alog = ar([128, 8])
    nega = ar([128, 8])
    dtb = ar([128, 8])
    normw = ar([128, 128])
    carry = ar([128, 24 * 2 * 3])
    cb = [ar([128, 4 * 132])] * 2
    acc = ar([128, 4, 128])
    ebuf = ar([128, 4, 128])
    yfm = acc
    ssq = ar([128, 16])
    rn = ar([128, 16])
    sc_k = ar([128, 8])
    tmb = ar([128, 4, 128], BF16)
    qT2 = [ar([128, 8, 128], BF16) for _ in range(2)]
    kT2 = [ar([128, 8, 128], BF16) for _ in range(2)]
    kd2 = [ar([128, 8, 128], BF16) for _ in range(2)]
    vp2 = [ar([128, 8, 128], BF16) for _ in range(2)]
    gsm2 = [ar([128, 12, 8]) for _ in range(2)]
    glbc2 = [ar([128, 8, 2]) for _ in range(2)]
    REG = ar([128, 1024])
    DTi = [REG[:, i * 128:(i + 1) * 128] for i in (0, 1)]
    Ds = [REG[:, i * 128:(i + 1) * 128] for i in (2, 3)]
    DTs = [REG[:, i * 128:(i + 1) * 128] for i in (4, 5)]
    _rb = REG[:, :].bitcast(BF16)
    Rp = [_rb[:, i * 512:(i + 1) * 512].rearrange("p (h d) -> p h d", h=4) for i in (0, 1)]
    Yb = [_rb[:, i * 512:(i + 1) * 512].rearrange("p (h d) -> p h d", h=4) for i in (2, 3)]
    intraT = ar([128, 8, 128], BF16)
    XT = ar([128, 8, 128], BF16)
    S = ar([128, 8, 128])
    Sb = ar([128, 8, 128], BF16)
    osq = ebuf[:, 0, :]
    oss = ar([128, 8])
    print("arena used (A):", apos[0])

    PSALL = nc.alloc_psum_tensor("psall", [128, 4096], F32)

    class _Bank:
        def __init__(self, i, n=1):
            self.i, self.n = i, n

        def __getitem__(self, key):
            return PSALL[:, self.i * 512:(self.i + self.n) * 512][key]

    PS = [_Bank(i) for i in range(8)]
    psn = [0]

    reserved = set()

    def bank():
        for _ in range(16):
            i = psn[0]
            psn[0] = (i + 1) % 8
            if i not in reserved:
                return i
        raise RuntimeError("no free PSUM bank: reserved=%s" % sorted(reserved))

    def pk(i):
        return ('ps', i)

    def bank2():
        for _ in range(16):
            i = psn[0]
            if i + 1 < 8 and i not in reserved and (i + 1) not in reserved:
                psn[0] = (i + 2) % 8
                return i
            psn[0] = (i + 1) % 8
        raise RuntimeError("no free PSUM bank pair: reserved=%s" % sorted(reserved))

    def PS2(i):
        return _Bank(i, 2)

    out_stamps = []
    dbg_outs = {}

    def tap(name, ap_sb, keys):
        if name not in taps:
            return
        d = dout("tap_" + name, list(ap_sb.shape), F32 if ap_sb.dtype == F32R else ap_sb.dtype)
        src = ap_sb.bitcast(F32) if ap_sb.dtype == F32R else ap_sb
        out_stamps.append(K.dma('sp', d, src, reads=keys, writes=[('tap', name)]))

    K.dma('sp', identf[:], identf_d, writes=['identf'])
    K.dma('sp', onesf[:], onesf_d, writes=['onesf'])
    K.dma('pool', identb[:], identf_d, writes=['identb'])
    for t in range(NT):
        T = 64 if t == NPT else 128
        K.dma('sp', xres[:T, t, :], x_d[t * 128:t * 128 + T, :], writes=[('x', t)])

    def load_w(dst, src_ap, ncols, key, c0=0):
        for kc in range(8):
            K.dma('pool', dst[:, kc, c0:c0 + ncols], src_ap[kc * 128:(kc + 1) * 128, :], writes=[(key, kc)])

    XTK = [('xT', 0), ('xT', 1)]
    ONK = [('on', h) for h in range(8)]

    def make_xT(ti):
        T = ti.T
        for half in range(2):
            b = bank()
            for q in range(4):
                kc = half * 4 + q
                K.op('pe', lambda: nc.tensor.transpose(PS[b][:, q * T:(q + 1) * T], xres[:T, ti.idx, kc * 128:(kc + 1) * 128],
                                                       identf[:T, :T]),
                     reads=[('x', ti.idx), 'identf'], writes=[pk(b)])
            K.op('act', lambda: nc.scalar.copy(out=xT[:, half * 4:half * 4 + 4, :T],
                                               in_=PS[b][:, 0:4 * T].rearrange("p (a t) -> p a t", a=4)),
                 reads=[pk(b)], writes=[('xT', half)])

    pending_tail = []

    def tail_step():
        if pending_tail:
            pending_tail.pop(0)()

    def tail_flush():
        while pending_tail:
            pending_tail.pop(0)()

    wr_state = {'use': 0, 'iss': 0}
    WOUT_SRC = [a_w_out[0], a_w_out[1]] + ([b_w_out[0], b_w_out[1]] if WITH_B else [])

    def wr_issue():
        c = wr_state['iss']
        if c >= len(WOUT_SRC) * NT * 8:
            return
        wr_state['iss'] += 1
        lay = c // (NT * 8)
        kc = c % 8
        K.dma('pool', wring[:, c % 4, :], WOUT_SRC[lay][kc * 128:(kc + 1) * 128, :], writes=[('wr', c % 4)])

    for _ in range(4):
        wr_issue()

    def out_proj(ti, final, ogT_ap, ogk, immediate=0):
        T = ti.T
        t = ti.idx
        st = {}

        def s1():
            b = bank()
            psb = PS[b][:, :].bitcast(BF16)
            for kc in range(8):
                K.op('pe', lambda: nc.tensor.transpose(psb[:, kc * T:(kc + 1) * T], og[:T, kc * 128:(kc + 1) * 128], identb[:T, :T]),
                     reads=['og', 'identb'], writes=[pk(b)])
            K.op('act', lambda: nc.scalar.copy(out=ogT_ap[:, :, :T], in_=psb[:, 0:8 * T].rearrange("p (a t) -> p a t", a=8)),
                 reads=[pk(b)], writes=ogk)

        def s2half(kcs):
            if 'pb' not in st:
                st['pb'] = [bank(), bank()]
                reserved.update(st['pb'])
            pb = st['pb']
            for kc in kcs:
                c = wr_state['use']
                wr_state['use'] += 1
                slot = c % 4
                for h2 in range(2):
                    K.op('pe', lambda: nc.tensor.matmul(PS[pb[h2]][:T, :], ogT_ap[:, kc, :T], wring[:, slot, h2 * 512:(h2 + 1) * 512],
                                                        start=(kc == 0), stop=(kc == 7)),
                         reads=ogk + [('wr', slot)], writes=[pk(pb[h2])])
            for _ in kcs:
                wr_issue()

        def s2a():
            s2half(range(0, 4))

        def s2b():
            s2half(range(4, 8))

        def s3():
            pb = st['pb']
            for h2 in range(2):
                rk = ONK[h2 * 4:h2 * 4 + 4]
                sl = slice(h2 * 512, (h2 + 1) * 512)
                K.op('dve', lambda: nc.vector.scalar_tensor_tensor(out=res[:T, sl], in0=xres[:T, t, sl], scalar=DN_ALPHA,
                                                                   in1=PS[pb[h2]][:T, :], op0=ALU.mult, op1=ALU.add),
                     reads=[('x', t), pk(pb[h2])], writes=rk)
                K.op('dve', lambda: nc.vector.bn_stats(out=bst[:T, h2, :], in_=res[:T, sl]), reads=rk, writes=[('bst', h2)])
            K.op('dve', lambda: nc.vector.bn_aggr(out=mv[:T, :], in_=bst[:T, :, :]), reads=[('bst', 0), ('bst', 1)], writes=['mv'])
            K.op('act', lambda: nc.scalar.activation(out=rstd[:T, :], in_=mv[:T, 1:2], func=AF.Ln, bias=epsln[:T, 0:1], scale=1.0),
                 reads=['mv', 'eps'], writes=['rstd'])
            K.op('act', lambda: nc.scalar.activation(out=rstd[:T, :], in_=rstd[:T, :], func=AF.Exp, scale=-0.5),
                 reads=['rstd'], writes=['rstd'])
            reserved.discard(st['pb'][0])
            reserved.discard(st['pb'][1])

        def s4():
            for h2 in range(2):
                rk = ONK[h2 * 4:h2 * 4 + 4]
                sl = slice(h2 * 512, (h2 + 1) * 512)
                K.op('dve', lambda: nc.vector.tensor_scalar(out=res[:T, sl], in0=res[:T, sl], scalar1=mv[:T, 0:1], scalar2=rstd[:T, 0:1],
                                                            op0=ALU.subtract, op1=ALU.mult),
                     reads=rk + ['mv', 'rstd'], writes=rk)
                K.op('pool', lambda: nc.gpsimd.tensor_tensor(out=res[:T, sl], in0=res[:T, sl], in1=lng[:T, sl], op=ALU.mult),
                     reads=rk + ['lng'], writes=rk)

        def s5():
            for h2 in range(2):
                rk = ONK[h2 * 4:h2 * 4 + 4]
                sl = slice(h2 * 512, (h2 + 1) * 512)
                K.op('pool', lambda: nc.gpsimd.tensor_tensor(out=xres[:T, t, sl], in0=res[:T, sl], in1=lnb[:T, sl], op=ALU.add),
                     reads=rk + ['lnb'], writes=[('x', t)])
            if final:
                out_stamps.append(K.dma('sp', y_d[t * 128:t * 128 + T, :], xres[:T, t, :], reads=[('x', t)], writes=[('y', t)]))
        stages = [s1, s2a, s2b, s3, s4, s5]
        for f_ in stages[:immediate]:
            f_()
        pending_tail.extend(stages[immediate:])

    def silu_from(out_ap, in_ap, tmp_ap, rkeys, wkeys, tkeys):
        P_ = tmp_ap.shape[0]
        K.op('act', lambda: nc.scalar.activation(out=tmp_ap, in_=in_ap, func=AF.Exp, scale=-1.0), reads=rkeys, writes=tkeys)
        K.op('act', lambda: nc.scalar.activation(out=tmp_ap, in_=tmp_ap, func=AF.Ln, bias=epsln[:P_, 2:3], scale=1.0), reads=tkeys + ['eps'], writes=tkeys)
        K.op('act', lambda: nc.scalar.activation(out=tmp_ap, in_=tmp_ap, func=AF.Exp, scale=-1.0), reads=tkeys, writes=tkeys)
        K.op('dve', lambda: nc.vector.tensor_tensor(out=out_ap, in0=in_ap, in1=tmp_ap, op=ALU.mult), reads=rkeys + tkeys, writes=wkeys)

    epsln = sb("epsln", [128, 4])
    K.op('pool', lambda: nc.gpsimd.memset(epsln[:, 0:1], LN_EPS), writes=['eps'])
    K.op('pool', lambda: nc.gpsimd.memset(epsln[:, 1:2], RMS_EPS), writes=['eps'])
    K.op('pool', lambda: nc.gpsimd.memset(epsln[:, 2:3], 1.0), writes=['eps'])
    K.op('pool', lambda: nc.gpsimd.memset(epsln[:, 3:4], float(np.log(128.0 ** -0.5))), writes=['eps'])
    EPS_LN, EPS_RMS, ONE_B, LOGQ = epsln[:, 0:1], epsln[:, 1:2], epsln[:, 2:3], epsln[:, 3:4]

    def a_layer(l):
        for (bname, c0, c1) in [('g', 4096, 4112), ('q0', 0, 512), ('q1', 512, 1024), ('k', 1024, 2048), ('v', 2048, 3072), ('z', 3072, 4096)]:
            K.dma('pool', wbig[:, :, c0:c1], a_w_in[l][:, c0:c1].rearrange("(kc p) n -> p kc n", p=128), writes=[('wbig', bname)])
        K.dma('sp', masks[:], masks_d[:, 0], writes=['masks'])
        K.dma('sp', convw, convw_d[l], writes=['convw'])
        K.dma('sp', alog, alog_d[l], writes=['alog'])
        K.dma('sp', dtb, dtb_d[l], writes=['dtb'])
        K.dma('sp', normw, normw_d[l], writes=['normw'])
        K.dma('sp', lng[:], lng_d[l], writes=['lng'])
        K.dma('sp', lnb[:], lnb_d[l], writes=['lnb'])
        K.op('act', lambda: nc.scalar.activation(out=nega, in_=alog, func=AF.Exp), reads=['alog'], writes=['nega'])
        K.op('dve', lambda: nc.vector.tensor_scalar(out=nega, in0=nega, scalar1=-1.0, scalar2=None, op0=ALU.mult),
             reads=['nega'], writes=['nega'])
        K.op('pool', lambda: nc.gpsimd.memset(carry, 0.0), writes=['carry'])
        K.op('pool', lambda: nc.gpsimd.memset(S, 0.0), writes=[('S', 0), ('S', 1)])
        K.op('pool', lambda: nc.gpsimd.memset(Sb, 0.0), writes=[('Sb', 0), ('Sb', 1)])
        gens = [a_tile(l, TileInfo(t), t % 2) for t in range(NT)]

        def adv(g, until):
            while True:
                v = next(g)
                if v == until:
                    return

        adv(gens[0], 'Qdone')
        adv(gens[0], 'EZdone')
        npt_ = stop_after[1] if isinstance(stop_after, tuple) else NPT
        for t in range(npt_):
            g, gn = gens[t], (gens[t + 1] if t + 1 < npt_ else None)
            hdone = False
            cnt_ = [0]
            if gn is not None:
                adv(gn, 'F')
                while True:
                    v = next(gn)
                    if v == 'Qdone':
                        break
                    cnt_[0] += 1
                    if cnt_[0] % HRATIO == 0 and (cnt_[0] % 5) not in HSKIP:
                        for _ in range(HPER):
                            if not hdone and next(g) == 'Hdone':
                                hdone = True
            while not hdone:
                hdone = next(g) == 'Hdone'
            adv(g, 'Sdone')
            if gn is not None:
                adv(gn, 'EZdone')
            tail_step()
            tail_step()
        if isinstance(stop_after, tuple):
            tail_flush()
            return True
        gs = gens[NPT]
        adv(gs, 'Sdone')
        tail_flush()
        return False

    def a_tile(l, ti, p):
        T, nseq, L, C, nch, t = ti.T, ti.nseq, ti.L, ti.C, ti.nch, ti.idx
        W = 3 + L
        qT, kT, kd, vp, gsm, glbc = qT2[p], kT2[p], kd2[p], vp2[p], gsm2[p], glbc2[p]
        if ti.samp:
            K.dma('sp', masks[:], masks_d[:, 1], writes=['masks'])
            cview = carry[:, 0:24 * 2 * 3].rearrange("p (c s j) -> p c s j", c=24, s=2)
            for s_ in range(2):
                K.dma('sp', cview[:, :, s_, :], sconv_d[l, s_], writes=['carry'])
        K.op('pool', lambda: nc.gpsimd.memset(ssq, 0.0), writes=[('ssq', g, c_) for g in range(4) for c_ in range(4)])
        make_xT(ti)
        cv4 = carry[:, 0:24 * nseq * 3].rearrange("p (c s j) -> p c s j", c=24, s=nseq)

        bg = bank()
        for kc in range(8):
            K.op('pe', lambda: nc.tensor.matmul(PS[bg][:T, 0:16], xT[:, kc, :T], wbig[:, kc, 4096:4112], start=(kc == 0), stop=(kc == 7)),
                 reads=XTK + [('wbig', 'g')], writes=[pk(bg)])
        G_ = lambda i: gsm[:T, i, :]
        K.op('dve', lambda: nc.vector.tensor_tensor(out=G_(0), in0=PS[bg][:T, 0:8], in1=dtb[:T, :], op=ALU.add),
             reads=[pk(bg), 'dtb'], writes=[('g', p, 0)])
        K.op('act', lambda: nc.scalar.activation(out=G_(0), in_=G_(0), func=AF.Exp), reads=[('g', p, 0)], writes=[('g', p, 0)])
        K.op('act', lambda: nc.scalar.activation(out=G_(0), in_=G_(0), func=AF.Ln, bias=ONE_B[:T, :], scale=1.0), reads=[('g', p, 0), 'eps'], writes=[('g', p, 0)])
        K.op('dve', lambda: nc.vector.tensor_tensor(out=G_(1), in0=G_(0), in1=nega[:T, :], op=ALU.mult), reads=[('g', p, 0), 'nega'], writes=[('g', p, 1)])
        K.op('act', lambda: nc.scalar.activation(out=G_(2), in_=PS[bg][:T, 8:16], func=AF.Exp, scale=-1.0), reads=[pk(bg)], writes=[('g', p, 2)])
        K.op('act', lambda: nc.scalar.activation(out=G_(2), in_=G_(2), func=AF.Ln, bias=ONE_B[:T, :], scale=1.0), reads=[('g', p, 2), 'eps'], writes=[('g', p, 2)])
        K.op('act', lambda: nc.scalar.activation(out=G_(4), in_=G_(2), func=AF.Exp, scale=-0.5), reads=[('g', p, 2)], writes=[('g', p, 4)])
        bG = bank()
        K.op('pe', lambda: nc.tensor.matmul(PS[bG][:T, 0:8], masks[:T, 3, :T], G_(1), start=True, stop=True),
             reads=['masks', ('g', p, 1)], writes=[pk(bG)])
        K.op('pe', lambda: nc.tensor.matmul(PS[bG][:T, 8:16], masks[:T, 4, :T], G_(1), start=True, stop=True),
             reads=['masks', ('g', p, 1)], writes=[pk(bG)])
        K.op('act', lambda: nc.scalar.copy(out=G_(5), in_=PS[bG][:T, 0:8]), reads=[pk(bG)], writes=[('g', p, 5)])
        K.op('dve', lambda: nc.vector.tensor_tensor(out=G_(6), in0=PS[bG][:T, 8:16], in1=G_(5), op=ALU.subtract),
             reads=[pk(bG), ('g', p, 5)], writes=[('g', p, 6)])
        K.op('act', lambda: nc.scalar.activation(out=G_(6), in_=G_(6), func=AF.Exp), reads=[('g', p, 6)], writes=[('g', p, 6)])
        K.op('act', lambda: nc.scalar.activation(out=G_(7), in_=G_(5), func=AF.Exp), reads=[('g', p, 5)], writes=[('g', p, 7)])
        K.op('dve', lambda: nc.vector.tensor_scalar(out=G_(8), in0=G_(7), scalar1=-1.0, scalar2=None, op0=ALU.mult),
             reads=[('g', p, 7)], writes=[('g', p, 8)])
        yield 'F'
        qkv_banks = {}

        def qkv_proj(grp):
            b = bank()
            reserved.add(b)
            qkv_banks[grp] = b
            for c4 in range(4):
                ct = grp * 4 + c4
                for kc in range(8):
                    K.op('pe', lambda: nc.tensor.matmul(PS[b][:, c4 * T:(c4 + 1) * T], wbig[:, kc, ct * 128:(ct + 1) * 128], xT[:, kc, :T],
                                                        start=(kc == 0), stop=(kc == 7)),
                         reads=XTK + [('wbig', ['q0', 'q1', 'k', 'k', 'v', 'v'][grp])], writes=[pk(b)])
        cbk = 'cb'
        cbv = cb[0][:, 0:4 * nseq * W].rearrange("p (c s w) -> p c s w", c=4, s=nseq)
        acck = [('acc', i) for i in range(4)]
        junk = tmb[:, :, :].rearrange("p a b -> p (a b)").bitcast(F32)

        def st1(grp):
            b = qkv_banks[grp]
            K.op('pool', lambda: nc.gpsimd.tensor_copy(out=cbv[:, :, :, 0:3], in_=cv4[:, grp * 4:grp * 4 + 4, :, :]),
                 reads=['carry'], writes=[cbk])
            K.op('act', lambda: nc.scalar.copy(out=cbv[:, :, :, 3:3 + L],
                                               in_=PS[b][:, 0:4 * T].rearrange("p (c s w) -> p c s w", c=4, s=nseq)),
                 reads=[pk(b)], writes=[cbk])
            reserved.discard(b)
            K.op('pool', lambda: nc.gpsimd.tensor_copy(out=cv4[:, grp * 4:grp * 4 + 4, :, :], in_=cbv[:, :, :, L:L + 3]),
                 reads=[cbk], writes=['carry'])

        def st2(grp):
            avs = [acc[:, c4, 0:T].rearrange("p (s w) -> p s w", s=nseq) for c4 in range(4)]
            for c4 in range(4):
                ct = grp * 4 + c4
                K.op('dve', lambda: nc.vector.tensor_scalar(out=avs[c4], in0=cbv[:, c4, :, 0:L], scalar1=convw[:, ct, 0:1], scalar2=None, op0=ALU.mult),
                     reads=[cbk, 'convw'], writes=[('acc', c4)])
            for j in range(1, 4):
                for c4 in range(4):
                    ct = grp * 4 + c4
                    K.op('dve', lambda: nc.vector.scalar_tensor_tensor(out=avs[c4], in0=cbv[:, c4, :, j:j + L], scalar=convw[:, ct, j:j + 1], in1=avs[c4],
                                                                       op0=ALU.mult, op1=ALU.add),
                         reads=[cbk, 'convw', ('acc', c4)], writes=[('acc', c4)])

        def st3(grp):
            silu_from(ebuf[:, :, :T], acc[:, :, :T], ebuf[:, :, :T], acck, ['ebuf'], ['ebuf'])

        st4 = {}
        st4c_b = {}

        def st4c(grp):
            if grp < 0 or grp >= 4:
                return
            b3 = bank()
            reserved.add(b3)
            st4c_b[grp] = b3
            psb = PS[b3][:, :].bitcast(BF16)
            for c4 in range(4):
                K.op('pe', lambda: nc.tensor.transpose(psb[:, c4 * T:(c4 + 1) * T], tmb[:T, c4, :], identb[:T, :T]),
                     reads=['tmb', 'identb'], writes=[pk(b3)])

        def st4d(grp):
            if grp < 0 or grp >= 4:
                return
            b3 = st4c_b[grp]
            psb = PS[b3][:, :].bitcast(BF16)
            isq = grp < 2
            h0 = (grp % 2) * 4
            dst = qT if isq else kT
            dk_ = ('qT' if isq else 'kT', p, grp % 2)
            K.op('act', lambda: nc.scalar.copy(out=dst[:, h0:h0 + 4, :T], in_=psb[:, 0:4 * T].rearrange("p (a t) -> p a t", a=4)),
                 reads=[pk(b3)], writes=[dk_])
            reserved.discard(b3)

        def st4a(grp):
            b2 = bank()
            reserved.add(b2)
            st4[grp] = b2
            for c4 in range(4):
                K.op('pe', lambda: nc.tensor.transpose(PS[b2][:T, c4 * 128:(c4 + 1) * 128], ebuf[:, c4, :T], identf[:, :]),
                     reads=['ebuf', 'identf'], writes=[pk(b2)])
            h0 = (grp % 2) * 4
            if grp < 4:
                isq = grp < 2
                col0 = (0 if isq else 8) + h0
                for c4 in range(4):
                    K.op('act', lambda: nc.scalar.activation(out=junk[:T, (c4 % 2) * 128:(c4 % 2) * 128 + 128], in_=PS[b2][:T, c4 * 128:(c4 + 1) * 128], func=AF.Square,
                                                             accum_out=ssq[:T, col0 + c4:col0 + c4 + 1]),
                         reads=[pk(b2)], writes=[('ssq', grp, c4), ('junk', c4 % 2)] + (['tmb'] if c4 < 2 else []))
                K.op('act', lambda: nc.scalar.activation(out=rn[:T, col0:col0 + 4], in_=ssq[:T, col0:col0 + 4], func=AF.Ln, bias=EPS_RMS[:T, :], scale=1.0),
                     reads=[('ssq', grp, c_) for c_ in range(4)] + ['eps'], writes=[('rn', grp)])
                if isq:
                    K.op('act', lambda: nc.scalar.activation(out=rn[:T, col0:col0 + 4], in_=rn[:T, col0:col0 + 4], func=AF.Exp, scale=-0.5,
                                                             bias=LOGQ[:T, :]),
                         reads=[('rn', grp), 'eps'], writes=[('rn', grp)])
                else:
                    K.op('act', lambda: nc.scalar.activation(out=rn[:T, col0:col0 + 4], in_=rn[:T, col0:col0 + 4], func=AF.Exp, scale=-0.5),
                         reads=[('rn', grp)], writes=[('rn', grp)])

        def st4b(grp):
            b2 = st4[grp]
            h0 = (grp % 2) * 4
            pv3 = PS[b2][:T, :].rearrange("p (h d) -> p h d", h=4)
            if grp < 4:
                isq = grp < 2
                col0 = (0 if isq else 8) + h0
                if isq:
                    scl = rn[:T, col0:col0 + 4]
                    sk_ = [('rn', grp)]
                else:
                    K.op('dve', lambda: nc.vector.tensor_tensor(out=sc_k[:T, h0:h0 + 4], in0=rn[:T, col0:col0 + 4], in1=gsm[:T, 4, h0:h0 + 4], op=ALU.mult),
                         reads=[('rn', grp), ('g', p, 4)], writes=[('sck', grp)])
                    scl = sc_k[:T, h0:h0 + 4]
                    sk_ = [('sck', grp)]
                K.op('dve', lambda: nc.vector.tensor_tensor(out=tmb[:T, :, :], in0=pv3, in1=scl.unsqueeze(2).broadcast_to([T, 4, 128]), op=ALU.mult),
                     reads=[pk(b2)] + sk_, writes=['tmb'])
                reserved.discard(b2)
                if not isq:
                    K.op('pool', lambda: nc.gpsimd.tensor_tensor(out=kd[:T, h0:h0 + 4, :], in0=tmb[:T, :, :],
                                                                 in1=gsm[:T, 6, h0:h0 + 4].unsqueeze(2).broadcast_to([T, 4, 128]), op=ALU.mult),
                         reads=['tmb', ('g', p, 6)], writes=[('kd', p, grp % 2)])
            else:
                K.op('dve', lambda: nc.vector.tensor_tensor(out=vp[:T, h0:h0 + 4, :], in0=pv3,
                                                            in1=gsm[:T, 4, h0:h0 + 4].unsqueeze(2).broadcast_to([T, 4, 128]), op=ALU.mult),
                     reads=[pk(b2), ('g', p, 4)], writes=[('vp', p, grp % 2)])
                reserved.discard(b2)

        qkv_proj(0)
        qkv_proj(1)
        st1(0)
        st2(0)
        for grp in range(6):
            st4c(grp - 1)
            if grp + 2 < 6:
                qkv_proj(grp + 2)
            if grp + 1 < 6:
                st1(grp + 1)
            yield 'Qit'
            st3(grp)
            st4d(grp - 1)
            yield 'Qit'
            st4a(grp)
            yield 'Qit'
            if grp + 1 < 6:
                st2(grp + 1)
            yield 'Qit'
            st4b(grp)
            tail_step()
            yield 'Qit'
        if ti.idx == NPT - 1:
            out_stamps.append(K.dma('sp', pconv_d[l], cv4[:, :, 0, :], reads=['carry'], writes=[('pconv', l)]))
        if ti.samp:
            for s_ in range(2):
                out_stamps.append(K.dma('sp', sconvo_d[l, s_], cv4[:, :, s_, :], reads=['carry'], writes=[('sconvo', l, s_)]))
        tail_flush()
        yield 'Qdone'
        cbf = cb[0]
        for h2 in range(2):
            b = bank()
            for kc in range(8):
                K.op('pe', lambda: nc.tensor.matmul(PS[b][:T, :], xT[:, kc, :T], wbig[:, kc, 3072 + h2 * 512:3072 + (h2 + 1) * 512],
                                                    start=(kc == 0), stop=(kc == 7)),
                     reads=XTK + [('wbig', 'z')], writes=[pk(b)])
            silu_from(og[:T, h2 * 512:(h2 + 1) * 512], PS[b][:T, :], cbf[:T, 0:512], [pk(b)], ['og'], ['cb'])
        yield 'EZdone'
        nlev = 5 if C == 64 else 4
        first_reg = [True]

        def regkeys():
            if first_reg[0]:
                first_reg[0] = False
                return [], ['REG']
            return ['REG'], []
        DTi4 = REG[:, 0:512].rearrange("p (h c) -> p h c", h=4)
        Ds4 = REG[:, 512:1024].rearrange("p (h c) -> p h c", h=4)
        hstate = {}
        AK = [('Aq', i) for i in range(4)]
        BK = [('Bq', i) for i in range(4)]
        PK = [('Pq', i) for i in range(4)]

        def h_prep(gq):
            hs = slice(gq * 4, gq * 4 + 4)
            rr, rw = regkeys()
            K.op('dve', lambda: nc.vector.tensor_tensor(out=DTi4[:T, :, :T], in0=identf[:T, :T].unsqueeze(1).broadcast_to([T, 4, T]),
                                                        in1=gsm[:T, 5, hs].unsqueeze(2).broadcast_to([T, 4, T]), op=ALU.mult),
                 reads=['identf', ('g', p, 5)] + rr, writes=['DTi'] + rw)
            bgq = bank()
            reserved.add(bgq)
            K.op('pe', lambda: nc.tensor.matmul(PS[bgq][:, 0:4 * T], onesf[:T, :], DTi4[:T, :, :T], start=True, stop=True),
                 reads=['onesf', 'DTi', 'REG'], writes=[pk(bgq)])
            gv = PS[bgq][:, 0:4 * T].rearrange("p (h c j) -> p h c j", h=4, c=nch)
            K.op('act', lambda: nc.scalar.activation(out=glbc[:, hs, :nch], in_=gv[:, :, :, C - 1], func=AF.Exp),
                 reads=[pk(bgq)], writes=[('glbc', p, gq)])
            gps = PS[bgq][:T, 0:4 * T].rearrange("p (h c) -> p h c", h=4)
            K.op('dve', lambda: nc.vector.tensor_tensor(out=DTi4[:T, :, :T], in0=gps, in1=gsm[:T, 5, hs].unsqueeze(2).broadcast_to([T, 4, T]), op=ALU.subtract),
                 reads=[pk(bgq), ('g', p, 5), 'REG'], writes=['DTi'])
            reserved.discard(bgq)
            K.op('dve', lambda: nc.vector.tensor_tensor(out=Ds4[:T, :, :T], in0=DTi4[:T, :, :T], in1=masks[:T, 1, :T].unsqueeze(1).broadcast_to([T, 4, T]), op=ALU.subtract),
                 reads=['DTi', 'masks', 'REG'], writes=['Ds'])
            K.op('dve', lambda: nc.vector.tensor_tensor(out=DTi4[:T, :, :T], in0=DTi4[:T, :, :T], in1=masks[:T, 0, :T].unsqueeze(1).broadcast_to([T, 4, T]), op=ALU.add),
                 reads=['DTi', 'masks', 'REG'], writes=['DTi'])
            K.op('act', lambda: nc.scalar.activation(out=Ds4[:T, :, :T], in_=Ds4[:T, :, :T], func=AF.Exp, scale=-1.0), reads=['Ds', 'REG'], writes=['Ds'])
            K.op('act', lambda: nc.scalar.activation(out=DTi4[:T, :, :T], in_=DTi4[:T, :, :T], func=AF.Exp), reads=['DTi', 'REG'], writes=['DTi'])
            bq = bank2()
            for h4 in range(4):
                h = gq * 4 + h4
                K.op('pe', lambda: nc.tensor.matmul(PS2(bq)[:T, h4 * 256:h4 * 256 + T], kT[:, h, :T], kT[:, h, :T], start=True, stop=True),
                     reads=[('kT', p, gq)], writes=[pk(bq), pk(bq + 1)])
                K.op('pe', lambda: nc.tensor.matmul(PS2(bq)[:T, h4 * 256 + T:h4 * 256 + 2 * T], kT[:, h, :T], qT[:, h, :T], start=True, stop=True),
                     reads=[('kT', p, gq), ('qT', p, gq)], writes=[pk(bq), pk(bq + 1)])
            pq = PS2(bq)[:T, :].rearrange("p (h c) -> p h c", h=4)
            K.op('dve', lambda: nc.vector.tensor_tensor(out=intraT[:T, hs, :T], in0=pq[:, :, T:2 * T], in1=DTi4[:T, :, :T], op=ALU.mult),
                 reads=[pk(bq), pk(bq + 1), 'DTi', 'REG'], writes=[('intraT', h_) for h_ in range(gq * 4, gq * 4 + 4)])
            hstate[gq] = (bq, pq)
            reserved.update([bq, bq + 1])

        def h_fin(gq):
            bq, pq = hstate[gq]
            K.op('dve', lambda: nc.vector.scalar_tensor_tensor(out=Aq[:T, :, :T], in0=pq[:, :, 0:T], scalar=-1.0, in1=Ds4[:T, :, :T], op0=ALU.mult, op1=ALU.mult),
                 reads=[pk(bq), pk(bq + 1), 'Ds', 'REG'], writes=AK)
            K.op('dve', lambda: nc.vector.scalar_tensor_tensor(out=BPq[:T, :, 0:T], in0=pq[:, :, 0:T], scalar=-1.0, in1=DTi4[:T, :, :T], op0=ALU.mult, op1=ALU.mult),
                 reads=[pk(bq), pk(bq + 1), 'DTi', 'REG'], writes=BK)
            reserved.discard(bq)
            reserved.discard(bq + 1)
            K.op('dve', lambda: nc.vector.tensor_tensor(out=BPq[:T, :, 0:T], in0=BPq[:T, :, 0:T].bitcast(F32), in1=masks[:T, 2, :T].unsqueeze(1).broadcast_to([T, 4, T]), op=ALU.mult),
                 reads=BK + ['masks'], writes=BK)
            K.op('dve', lambda: nc.vector.tensor_tensor(out=BPq[:T, :, T:2 * T], in0=BPq[:T, :, 0:T].bitcast(F32), in1=identf[:T, :T].unsqueeze(1).broadcast_to([T, 4, T]), op=ALU.add),
                 reads=BK + ['identf'], writes=PK)

        def h_chain(gq, hook=None):
            for lev in range(nlev + 1):
                last = lev == nlev
                if lev == 1 and hook is not None:
                    hook()
                if not last:
                    b2 = bank2()
                    b3 = bank()
                    for h4 in range(4):
                        ncols = T if lev == 0 else 2 * T
                        K.op('pe', lambda: nc.tensor.matmul(PS2(b2)[:T, h4 * 256:h4 * 256 + ncols], Aq[:T, h4, :T], BPq[:T, h4, 0:ncols], start=True, stop=True),
                             reads=[AK[h4], BK[h4]] + ([PK[h4]] if lev > 0 else []), writes=[pk(b2), pk(b2 + 1)])
                        K.op('pe', lambda: nc.tensor.matmul(PS[b3][:T, h4 * T:(h4 + 1) * T], BPq[:T, h4, 0:T], Aq[:T, h4, :T], start=True, stop=True),
                             reads=[AK[h4], BK[h4]], writes=[pk(b3)])
                    if SPLITLEV:
                        reserved.update([b2, b2 + 1, b3])
                        yield 'lev'
                        reserved.difference_update([b2, b2 + 1, b3])
                    pv = PS2(b2)[:T, :].rearrange("p (h c) -> p h c", h=4)
                    if lev > 0:
                        K.op('dve', lambda: nc.vector.tensor_tensor(out=BPq[:T, :, T:2 * T], in0=pv[:, :, T:2 * T], in1=BPq[:T, :, T:2 * T].bitcast(F32), op=ALU.add),
                             reads=[pk(b2), pk(b2 + 1)] + PK, writes=PK)
                    K.op('act', lambda: nc.scalar.copy(out=BPq[:T, :, 0:T], in_=pv[:, :, 0:T]), reads=[pk(b2), pk(b2 + 1)], writes=BK)
                    K.op('act', lambda: nc.scalar.copy(out=Aq[:T, :, :T], in_=PS[b3][:T, 0:4 * T].rearrange("p (h c) -> p h c", h=4)),
                         reads=[pk(b3)], writes=AK)
                    yield 'lev'
                else:
                    b3 = bank()
                    for h4 in range(4):
                        K.op('pe', lambda: nc.tensor.matmul(PS[b3][:T, h4 * T:(h4 + 1) * T], Aq[:T, h4, :T], BPq[:T, h4, T:2 * T], start=True, stop=True),
                             reads=[AK[h4], PK[h4]], writes=[pk(b3)])
                    K.op('dve', lambda: nc.vector.tensor_tensor(out=XT[:T, gq * 4:gq * 4 + 4, :T], in0=PS[b3][:T, 0:4 * T].rearrange("p (h c) -> p h c", h=4),
                                                                in1=BPq[:T, :, T:2 * T].bitcast(F32), op=ALU.add),
                         reads=[pk(b3)] + PK, writes=[('XT', gq)])
                    yield 'lev'

        h_prep(0)
        yield 'H'
        h_fin(0)
        yield 'H'
        for _ in h_chain(0, hook=lambda: h_prep(1)):
            yield 'H'
        h_fin(1)
        yield 'H'
        for _ in h_chain(1):
            yield 'H'
        yield 'Hdone'
        first_scan = [True]
        tail_flush()
        K.op('pool', lambda: nc.gpsimd.memset(oss, 0.0), writes=['oss'] + [('oss', h_) for h_ in range(8)])
        for c in range(nch):
            r0 = c * C
            rs = slice(r0, r0 + C)
            if ti.samp:
                for h in range(8):
                    K.dma('sp', S[:, h, :], sdel_d[l, c, h], writes=[('S', h // 4)])
                for gq in range(2):
                    K.op('act', lambda: nc.scalar.copy(out=Sb[:, gq * 4:gq * 4 + 4, :], in_=S[:, gq * 4:gq * 4 + 4, :]), reads=[('S', gq)], writes=[('Sb', gq)])
            GQ = (0, 1)
            hsl = [slice(gq * 4, gq * 4 + 4) for gq in GQ]
            bk_ = {}
            for gq in GQ:
                ba, bb_ = bank(), bank()
                bk_[('a', gq)], bk_[('b', gq)] = ba, bb_
                for h4 in range(4):
                    h = gq * 4 + h4
                    K.op('pe', lambda: nc.tensor.matmul(PS[ba][:T, h4 * 128:(h4 + 1) * 128], kT[:, h, :T], Sb[:, h, :], start=True, stop=True),
                         reads=[('kT', p, gq), ('Sb', gq)], writes=[pk(ba)])
                for h4 in range(4):
                    h = gq * 4 + h4
                    K.op('pe', lambda: nc.tensor.matmul(PS[bb_][:T, h4 * 128:(h4 + 1) * 128], qT[:, h, :T], Sb[:, h, :], start=True, stop=True),
                         reads=[('qT', p, gq), ('Sb', gq)], writes=[pk(bb_)])
            for gq in GQ:
                ba, bb_ = bk_[('a', gq)], bk_[('b', gq)]
                for h4 in range(4):
                    h = gq * 4 + h4
                    if first_scan[0]:
                        first_scan[0] = False
                        rr, rw = [], ['REG']
                    else:
                        rr, rw = ['REG'], []
                    K.op('dve', lambda: nc.vector.scalar_tensor_tensor(out=Rp[gq][rs, h4, :], in0=PS[ba][rs, h4 * 128:(h4 + 1) * 128], scalar=gsm[rs, 8, h:h + 1],
                                                                       in1=vp[rs, h, :], op0=ALU.mult, op1=ALU.add),
                         reads=[pk(ba), ('g', p, 8), ('vp', p, gq)] + rr, writes=[('Rp', gq, h4)] + rw)
                K.op('dve', lambda: nc.vector.tensor_tensor(out=on[rs, hsl[gq], :], in0=PS[bb_][rs, :].rearrange("p (h d) -> p h d", h=4),
                                                            in1=gsm[rs, 7, hsl[gq]].unsqueeze(2).broadcast_to([C, 4, 128]), op=ALU.mult),
                     reads=[pk(bb_), ('g', p, 7)], writes=ONK[gq * 4:gq * 4 + 4])
            for gq in GQ:
                bc_ = bank()
                bk_[('c', gq)] = bc_
                for h4 in range(4):
                    h = gq * 4 + h4
                    K.op('pe', lambda: nc.tensor.matmul(PS[bc_][:T, h4 * 128:(h4 + 1) * 128], XT[rs, h, :T], Rp[gq][rs, h4, :], start=True, stop=True),
                         reads=[('XT', gq), ('Rp', gq, h4), 'REG'], writes=[pk(bc_)])
            for gq in GQ:
                bc_ = bk_[('c', gq)]
                K.op('act', lambda: nc.scalar.copy(out=Yb[gq][rs, :, :], in_=PS[bc_][rs, :].rearrange("p (h d) -> p h d", h=4)),
                     reads=[pk(bc_), 'REG'], writes=[('Yb', gq)])
                K.op('pool', lambda: nc.gpsimd.tensor_tensor(out=S[:, hsl[gq], :], in0=S[:, hsl[gq], :], in1=glbc[:, hsl[gq], c:c + 1].broadcast_to([128, 4, 128]), op=ALU.mult),
                     reads=[('S', gq), ('glbc', p, gq)], writes=[('S', gq)])
            for gq in GQ:
                bd, be = bank(), bank()
                bk_[('d', gq)], bk_[('e', gq)] = bd, be
                for h4 in range(4):
                    h = gq * 4 + h4
                    K.op('pe', lambda: nc.tensor.matmul(PS[be][:, h4 * 128:(h4 + 1) * 128], kd[rs, h, :], Yb[gq][rs, h4, :], start=True, stop=True),
                         reads=[('kd', p, gq), ('Yb', gq), 'REG'], writes=[pk(be)])
                for h4 in range(4):
                    h = gq * 4 + h4
                    K.op('pe', lambda: nc.tensor.matmul(PS[bd][:T, h4 * 128:(h4 + 1) * 128], intraT[rs, h, :T], Yb[gq][rs, h4, :], start=True, stop=True),
                         reads=[('intraT', h), ('Yb', gq), 'REG'], writes=[pk(bd)])
            for gq in GQ:
                bd, be = bk_[('d', gq)], bk_[('e', gq)]
                K.op('dve', lambda: nc.vector.tensor_tensor(out=S[:, hsl[gq], :], in0=PS[be][:, :].rearrange("p (h d) -> p h d", h=4), in1=S[:, hsl[gq], :], op=ALU.add),
                     reads=[pk(be), ('S', gq)], writes=[('S', gq)])
                K.op('act', lambda: nc.scalar.copy(out=Sb[:, hsl[gq], :], in_=S[:, hsl[gq], :]), reads=[('S', gq)], writes=[('Sb', gq)])
                K.op('dve', lambda: nc.vector.tensor_tensor(out=on[rs, hsl[gq], :], in0=PS[bd][rs, :].rearrange("p (h d) -> p h d", h=4), in1=on[rs, hsl[gq], :], op=ALU.add),
                     reads=[pk(bd)] + ONK[gq * 4:gq * 4 + 4], writes=ONK[gq * 4:gq * 4 + 4])
            if ti.samp:
                for h in range(8):
                    out_stamps.append(K.dma('sp', sdelo_d[l, c, h], S[:, h, :], reads=[('S', h // 4)], writes=[('sdelo', l, c, h)]))
        if ti.idx == NPT - 1:
            for h in range(8):
                out_stamps.append(K.dma('sp', pdel_d[l, h], S[:, h, :], reads=[('S', h // 4)], writes=[('pdel', l, h)]))
        for h in range(8):
            K.op('act', lambda: nc.scalar.activation(out=tmb[:, :, :].rearrange("p a b -> p (a b)").bitcast(F32)[:T, (h % 2) * 128:(h % 2) * 128 + 128], in_=on[:T, h, :], func=AF.Square, accum_out=oss[:T, h:h + 1]),
                 reads=[('on', h)], writes=[('oss', h), ('junk', h % 2)] + (['tmb'] if h < 2 else []))
        K.op('act', lambda: nc.scalar.activation(out=oss[:T, :], in_=oss[:T, :], func=AF.Ln, bias=EPS_RMS[:T, :], scale=1.0 / 128.0),
             reads=[('oss', h_) for h_ in range(8)] + ['eps'], writes=['oss'])
        K.op('act', lambda: nc.scalar.activation(out=oss[:T, :], in_=oss[:T, :], func=AF.Exp, scale=-0.5), reads=['oss'], writes=['oss'])
        K.op('dve', lambda: nc.vector.tensor_tensor(out=on[:T, :, :], in0=on[:T, :, :], in1=oss[:T, :].unsqueeze(2).broadcast_to([T, 8, 128]), op=ALU.mult),
             reads=ONK + ['oss'], writes=ONK)
        K.op('dve', lambda: nc.vector.tensor_tensor(out=on[:T, :, :], in0=on[:T, :, :], in1=normw[:T, :].unsqueeze(1).broadcast_to([T, 8, 128]), op=ALU.mult),
             reads=ONK + ['normw'], writes=ONK)
        for h2 in range(2):
            sl = slice(h2 * 512, (h2 + 1) * 512)
            K.op('dve', lambda: nc.vector.tensor_tensor(out=og[:T, sl], in0=on[:T, h2 * 4:h2 * 4 + 4, :].rearrange("p h d -> p (h d)"), in1=og[:T, sl], op=ALU.mult),
                 reads=ONK + ['og'], writes=['og'])
        out_proj(ti, False, XT, [('XT', 0), ('XT', 1)], immediate=2)
        yield 'Sdone'

    def b_setup():
        K.full_sync()
        apos[0] = 0
        B = {}
        wflat = wbig[:, :, :].rearrange("p a b -> p (a b)")
        B['wflat'] = wflat
        B['KT'] = wflat[0:65, 20480:20480 + 4 * 2112].rearrange("p (g t) -> p g t", g=4)
        B['KTc'] = wflat[0:65, 28928:28928 + 1024].rearrange("p (g s t) -> p g s t", g=4, s=2)
        B['VEc'] = wflat[:, 29952:29952 + 520].rearrange("p (s g d) -> p s g d", s=2, g=4)
        B['PTn'] = wflat[0:64, 30472:30472 + 256].rearrange("p (h q) -> p h q", h=4)
        B['ogT'] = wflat[:, 30728:30728 + 1024].rearrange("p (a t) -> p a t", a=8)
        B['VE'] = ar([128, NT, 4, 65], BF16)
        B['kvf'] = ar([128, 512])
        B['kext'] = ar([128, 4, 65], BF16)
        B['qf'] = ar([128, 16, 64])
        B['qext'] = ar([128, 16, 65], BF16)
        B['QT'] = ar([128, 16, 128], BF16)
        B['rope'] = ar([128, NT, 2, 8])
        B['PTp'] = [ar([128, 4, 128], BF16) for _ in range(2)]
        B['PTc'] = [ar([128, 4, 128], BF16) for _ in range(2)]
        B['PTs'] = [[ar([128, 4, 64], BF16) for _ in range(2)] for _ in range(2)]
        B['ztz'] = ar([128, 1024])
        B['zt'] = B['ztz'][:, 0:512]
        B['zs'] = B['ztz'][:, 512:1024]
        B['sum8'] = ar([128, NT + 3])
        B['ksm'] = ar([128, 8, 4])
        B['qsm'] = ar([128, 6, 16])
        B['nshb'] = ar([128, 16], BF16)
        B['sinkb'] = ar([128, 16])
        B['den'] = ar([128, 2, 4])
        B['zsb'] = ar([128, D], BF16)
        print("arena used (B):", apos[0])
        return B

    def b_layer(j, B):
        wflat = B['wflat']
        wb3 = wflat[:, 0:16384].rearrange("p (kc n) -> p kc n", kc=8)
        if j == 0:
            K.dma('pool', wflat[:, 16384:16384 + 4096].rearrange("p (kc n) -> p kc n", kc=8), b_w_kv.rearrange("(kc p) n -> p kc n", p=128), writes=['wkv'])
        for (bname, c0, c1) in [('q0', 0, 512), ('q1', 512, 1024), ('z', 1024, 2048)]:
            K.dma('pool', wb3[:, :, c0:c1], b_w_in[j][:, c0:c1].rearrange("(kc p) n -> p kc n", p=128), writes=[('wbin', bname)])
        K.dma('sp', lng[:], lng_d[2 + j], writes=['lng'])
        K.dma('sp', lnb[:], lnb_d[2 + j], writes=['lnb'])
        K.dma('sp', B['sinkb'], sink_d[j], writes=['sinkb'])
        if j == 0:
            b_init(B)
        for t in range(NT):
            b_tile(j, TileInfo(t), B)
        tail_flush()

    def ksum8(B, kv3, T, col, scratch):
        ksm = B['ksm']
        scratch = B['zt'][:, 0:256].rearrange("p (g d) -> p g d", g=4)
        K.op('dve', lambda: nc.vector.tensor_tensor(out=scratch[:T, :, :], in0=kv3, in1=kv3, op=ALU.mult),
             reads=['kvf'], writes=['zt'])
        K.op('dve', lambda: nc.vector.tensor_reduce(out=ksm[:T, 0, :], in_=scratch[:T, :, :], axis=AX.X, op=ALU.add),
             reads=['zt'], writes=['ksm'])
        K.op('dve', lambda: nc.vector.tensor_reduce(out=ksm[:T, 1, 0:1], in_=ksm[:T, 0, :], axis=AX.X, op=ALU.max),
             reads=['ksm'], writes=['ksm'])
        K.op('dve', lambda: nc.vector.tensor_tensor(out=ksm[:T, 2, 0:1], in0=ksm[:T, 1, 0:1], in1=ksm[:T, 1, 0:1], op=ALU.mult),
             reads=['ksm'], writes=['ksm'])
        K.op('dve', lambda: nc.vector.tensor_tensor(out=ksm[:T, 3, 0:1], in0=ksm[:T, 2, 0:1], in1=ksm[:T, 2, 0:1], op=ALU.mult),
             reads=['ksm'], writes=['ksm'])
        b = bank()
        K.op('pe', lambda: nc.tensor.matmul(PS[b][:, 0:1], onesf[:T, :], ksm[:T, 3, 0:1], start=True, stop=True),
             reads=['onesf', 'ksm'], writes=[pk(b)])
        K.op('act', lambda: nc.scalar.copy(out=B['sum8'][:, col:col + 1], in_=PS[b][:, 0:1]), reads=[pk(b)], writes=[('sum8', col)])

    def kt_store(B, T, dst_fn, rkeys, wkey):
        b = bank()
        psb = PS[b][:, :].bitcast(BF16)
        for g in range(4):
            K.op('pe', lambda: nc.tensor.transpose(psb[0:65, g * T:(g + 1) * T], B['kext'][:T, g, :], identb[:T, :T]),
                 reads=['kext', 'identb'], writes=[pk(b)])
        for g in range(4):
            K.op('act', lambda: nc.scalar.copy(out=dst_fn(g), in_=psb[0:65, g * T:(g + 1) * T]), reads=[pk(b)], writes=[wkey])

    def b_init(B):
        K.dma('sp', B['rope'], rope_d, writes=['rope'])
        K.op('pool', lambda: nc.gpsimd.memset(B['kext'][:, :, 64:65], 1.0), writes=['kext'])
        K.op('pool', lambda: nc.gpsimd.memset(B['VE'][:, :, :, 64:65], 1.0), writes=['VE1'])
        K.op('pool', lambda: nc.gpsimd.memset(B['VEc'][:, :, :, 64:65], 1.0), writes=['VEc'])
        for i in range(2):
            K.op('pool', lambda: nc.gpsimd.memset(B['PTp'][i], 0.0), writes=[('PTp', i)])
            K.op('pool', lambda: nc.gpsimd.memset(B['PTc'][i], 0.0), writes=[('PTc', i)])
            for s_ in range(2):
                K.op('pool', lambda: nc.gpsimd.memset(B['PTs'][i][s_], 0.0), writes=[('PTs', i, s_)])
        K.op('pool', lambda: nc.gpsimd.memset(B['PTn'], 0.0), writes=['PTn'])
        K.op('pool', lambda: nc.gpsimd.memset(B['sum8'], 0.0), writes=[('sum8', c) for c in range(NT + 3)])
        ckf = B['qf'][:, 0:8, :].rearrange("p a b -> p (a b)")
        cvf = B['qf'][:, 8:16, :].rearrange("p a b -> p (a b)")
        for s_ in range(2):
            K.dma('sp', ckf[:, s_ * 256:(s_ + 1) * 256], ck_d[s_], writes=['qf'])
            K.dma('sp', cvf[:, s_ * 256:(s_ + 1) * 256], cv_d[s_], writes=['qf'])
        for s_ in range(2):
            out_stamps.append(K.dma('sp', sk_d[s_, 0:96, :], ckf[32:128, s_ * 256:(s_ + 1) * 256], reads=['qf'], writes=[('sk', s_, 0)]))
            out_stamps.append(K.dma('sp', sv_d[s_, 0:96, :], cvf[32:128, s_ * 256:(s_ + 1) * 256], reads=['qf'], writes=[('sv', s_, 0)]))
            K.op('act', lambda: nc.scalar.copy(out=B['kext'][:, :, 0:64], in_=ckf[:, s_ * 256:(s_ + 1) * 256].rearrange("p (g d) -> p g d", g=4)),
                 reads=['qf'], writes=['kext'])
            kt_store(B, 128, lambda g: B['KTc'][:, g, s_, :], ['kext'], 'KTc')
            K.op('act', lambda: nc.scalar.copy(out=B['VEc'][:, s_, :, 0:64], in_=cvf[:, s_ * 256:(s_ + 1) * 256].rearrange("p (g d) -> p g d", g=4)),
                 reads=['qf'], writes=['VEc'])
        ksm = B['ksm']
        scr = on
        kv8 = ckf.rearrange("p (a d) -> p a d", a=8)
        K.op('dve', lambda: nc.vector.tensor_tensor(out=scr[:, :, 0:64], in0=kv8, in1=kv8, op=ALU.mult), reads=['qf'], writes=ONK)
        K.op('dve', lambda: nc.vector.tensor_reduce(out=ksm[:, 4:6, :].rearrange("p a b -> p (a b)"), in_=scr[:, :, 0:64], axis=AX.X, op=ALU.add),
             reads=ONK, writes=['ksm'])
        K.op('dve', lambda: nc.vector.tensor_reduce(out=ksm[:, 1, 0:1], in_=ksm[:, 4:6, :].rearrange("p a b -> p (a b)"), axis=AX.X, op=ALU.max),
             reads=['ksm'], writes=['ksm'])
        K.op('dve', lambda: nc.vector.tensor_tensor(out=ksm[:, 2, 0:1], in0=ksm[:, 1, 0:1], in1=ksm[:, 1, 0:1], op=ALU.mult), reads=['ksm'], writes=['ksm'])
        K.op('dve', lambda: nc.vector.tensor_tensor(out=ksm[:, 3, 0:1], in0=ksm[:, 2, 0:1], in1=ksm[:, 2, 0:1], op=ALU.mult), reads=['ksm'], writes=['ksm'])
        b = bank()
        K.op('pe', lambda: nc.tensor.matmul(PS[b][:, 0:1], onesf[:, :], ksm[:, 3, 0:1], start=True, stop=True), reads=['onesf', 'ksm'], writes=[pk(b)])
        K.op('act', lambda: nc.scalar.copy(out=B['sum8'][:, NT:NT + 1], in_=PS[b][:, 0:1]), reads=[pk(b)], writes=[('sum8', NT)])

    def rope_inplace(B, v4, nh, T, t, key):
        cos = B['rope'][:T, t, 0, :].unsqueeze(1).broadcast_to([T, nh, 8])
        sin = B['rope'][:T, t, 1, :].unsqueeze(1).broadcast_to([T, nh, 8])
        rt = B['zt'][:, :].rearrange("p (a h d) -> p a h d", a=4, h=16)
        x1 = v4[:, :, 0:8]
        x2 = v4[:, :, 8:16]
        K.op('dve', lambda: nc.vector.tensor_tensor(out=rt[:T, 0, 0:nh, :], in0=x1, in1=cos, op=ALU.mult), reads=[key, 'rope'], writes=['zt'])
        K.op('dve', lambda: nc.vector.tensor_tensor(out=rt[:T, 1, 0:nh, :], in0=x2, in1=sin, op=ALU.mult), reads=[key, 'rope'], writes=[('zt', 1)])
        K.op('dve', lambda: nc.vector.tensor_tensor(out=rt[:T, 2, 0:nh, :], in0=x2, in1=cos, op=ALU.mult), reads=[key, 'rope'], writes=[('zt', 2)])
        K.op('dve', lambda: nc.vector.tensor_tensor(out=rt[:T, 3, 0:nh, :], in0=x1, in1=sin, op=ALU.mult), reads=[key, 'rope'], writes=[('zt', 3)])
        K.op('dve', lambda: nc.vector.tensor_tensor(out=x1, in0=rt[:T, 0, 0:nh, :], in1=rt[:T, 1, 0:nh, :], op=ALU.subtract), reads=['zt', ('zt', 1), ('zt', 2), ('zt', 3)], writes=[key])
        K.op('dve', lambda: nc.vector.tensor_tensor(out=x2, in0=rt[:T, 2, 0:nh, :], in1=rt[:T, 3, 0:nh, :], op=ALU.add), reads=['zt', ('zt', 1), ('zt', 2), ('zt', 3)], writes=[key])

    def b_tile(j, ti, B):
        T, t = ti.T, ti.idx
        wflat = B['wflat']
        KT, VE, QT, qf, qext, kvf, kext = B['KT'], B['VE'], B['QT'], B['qf'], B['qext'], B['kvf'], B['kext']
        tok0 = t * 128
        make_xT(ti)
        tail_step()
        if j == 0:
            b = bank()
            for kc in range(8):
                K.op('pe', lambda: nc.tensor.matmul(PS[b][:T, :], xT[:, kc, :T], wflat[:, 16384 + kc * 512:16384 + (kc + 1) * 512],
                                                    start=(kc == 0), stop=(kc == 7)),
                     reads=XTK + ['wkv'], writes=[pk(b)])
            K.op('act', lambda: nc.scalar.copy(out=kvf[:T, :], in_=PS[b][:T, :]), reads=[pk(b)], writes=['kvf'])
            kv3 = kvf[:T, 0:256].rearrange("p (g d) -> p g d", g=4)
            vv3 = kvf[:T, 256:512].rearrange("p (g d) -> p g d", g=4)
            rope_inplace(B, kv3, 4, T, t, 'kvf')
            K.op('act', lambda: nc.scalar.copy(out=kext[:T, :, 0:64], in_=kv3), reads=['kvf'], writes=['kext'])
            K.op('pool', lambda: nc.gpsimd.tensor_copy(out=VE[:T, t, :, 0:64], in_=vv3), reads=['kvf'], writes=[('VE', t)])
            tail_step()
            kt_store(B, T, lambda g: KT[:, g, tok0:tok0 + T], ['kext'], ('KT', t))
            ksum8(B, kv3, T, t, None)
            if t == NPT - 1:
                out_stamps.append(K.dma('sp', pk_d, kvf[:, 0:256], reads=['kvf'], writes=['pk']))
                out_stamps.append(K.dma('sp', pv_d, kvf[:, 256:512], reads=['kvf'], writes=['pv']))
            if ti.samp:
                for s_ in range(2):
                    out_stamps.append(K.dma('sp', sk_d[s_, 96:128, :], kvf[s_ * 32:(s_ + 1) * 32, 0:256], reads=['kvf'], writes=[('sk', s_, 1)]))
                    out_stamps.append(K.dma('sp', sv_d[s_, 96:128, :], kvf[s_ * 32:(s_ + 1) * 32, 256:512], reads=['kvf'], writes=[('sv', s_, 1)]))
        for h2 in range(2):
            b = bank()
            for kc in range(8):
                K.op('pe', lambda: nc.tensor.matmul(PS[b][:T, :], xT[:, kc, :T], wflat[:, kc * 2048 + h2 * 512:kc * 2048 + (h2 + 1) * 512],
                                                    start=(kc == 0), stop=(kc == 7)),
                     reads=XTK + [('wbin', 'q%d' % h2)], writes=[pk(b)])
            K.op('act', lambda: nc.scalar.activation(out=qf[:T, h2 * 8:h2 * 8 + 8, :], in_=PS[b][:T, :].rearrange("p (h d) -> p h d", h=8),
                                                     func=AF.Copy, scale=0.125),
                 reads=[pk(b)], writes=['qf'])
        tail_step()
        rope_inplace(B, qf[:T, :, :], 16, T, t, 'qf')
        tail_step()
        K.op('act', lambda: nc.scalar.copy(out=qext[:T, :, 0:64], in_=qf[:T, :, :]), reads=['qf'], writes=['qext'])
        qsm = B['qsm']
        scr = B['ztz'][:, :].rearrange("p (h d) -> p h d", h=16)
        K.op('dve', lambda: nc.vector.tensor_tensor(out=scr[:T, :, :], in0=qf[:T, :, :], in1=qf[:T, :, :], op=ALU.mult), reads=['qf'], writes=['zt', 'zs'])
        K.op('dve', lambda: nc.vector.tensor_reduce(out=qsm[:T, 0, :], in_=scr[:T, :, :], axis=AX.X, op=ALU.add), reads=['zt', 'zs'], writes=['qsm0'])
        K.op('act', lambda: nc.scalar.activation(out=qsm[:T, 0, :], in_=qsm[:T, 0, :], func=AF.Ln, bias=EPS_RMS[:T, :], scale=1.0), reads=['qsm0', 'eps'], writes=['qsm0'])
        K.op('act', lambda: nc.scalar.activation(out=qsm[:T, 0, :], in_=qsm[:T, 0, :], func=AF.Exp, scale=0.5), reads=['qsm0'], writes=['qsm0'])
        tail_step()
        cprev = NT if ti.samp else (t - 1 if t > 0 else NT + 1)
        K.op('dve', lambda: nc.vector.tensor_tensor(out=qsm[:T, 1, 0:1], in0=B['sum8'][:T, t:t + 1], in1=B['sum8'][:T, cprev:cprev + 1], op=ALU.add),
             reads=[('sum8', t), ('sum8', cprev)], writes=['qsm1'])
        K.op('act', lambda: nc.scalar.activation(out=qsm[:T, 1, 0:1], in_=qsm[:T, 1, 0:1], func=AF.Ln), reads=['qsm1'], writes=['qsm1'])
        K.op('act', lambda: nc.scalar.activation(out=qsm[:T, 1, 0:1], in_=qsm[:T, 1, 0:1], func=AF.Exp, scale=0.125), reads=['qsm1'], writes=['qsm1'])
        K.op('dve', lambda: nc.vector.tensor_scalar(out=B['nshb'][:T, :], in0=qsm[:T, 0, :], scalar1=qsm[:T, 1, 0:1], scalar2=-1.0, op0=ALU.mult, op1=ALU.mult),
             reads=['qsm0', 'qsm1'], writes=['nshb'])
        K.op('pool', lambda: nc.gpsimd.tensor_copy(out=qext[:T, :, 64:65], in_=B['nshb'][:T, :].unsqueeze(2)), reads=['nshb'], writes=['qext'])
        K.op('dve', lambda: nc.vector.tensor_tensor(out=qsm[:T, 2, :], in0=B['nshb'][:T, :], in1=B['sinkb'][:T, :], op=ALU.add),
             reads=['nshb', 'sinkb'], writes=['qsm2'])
        K.op('act', lambda: nc.scalar.activation(out=qsm[:T, 2, :], in_=qsm[:T, 2, :], func=AF.Exp), reads=['qsm2'], writes=['qsm2'])
        for h2 in range(2):
            b = bank()
            psb = PS[b][:, :].bitcast(BF16)
            for hq in range(8):
                h = h2 * 8 + hq
                K.op('pe', lambda: nc.tensor.transpose(psb[0:65, hq * T:(hq + 1) * T], qext[:T, h, :], identb[:T, :T]),
                     reads=['qext', 'identb'], writes=[pk(b)])
            K.op('act', lambda: nc.scalar.copy(out=QT[0:65, h2 * 8:h2 * 8 + 8, :T], in_=psb[0:65, 0:8 * T].rearrange("p (a t) -> p a t", a=8)),
                 reads=[pk(b)], writes=[('QT', h2)])
        tail_step()
        tail_flush()
        for h2 in range(2):
            b = bank()
            for kc in range(8):
                K.op('pe', lambda: nc.tensor.matmul(PS[b][:T, :], xT[:, kc, :T], wflat[:, kc * 2048 + 1024 + h2 * 512:kc * 2048 + 1024 + (h2 + 1) * 512],
                                                    start=(kc == 0), stop=(kc == 7)),
                     reads=XTK + [('wbin', 'z')], writes=[pk(b)])
            silu_from(B['zsb'][:T, h2 * 512:(h2 + 1) * 512], PS[b][:T, :], B['ztz'][:T, h2 * 512:(h2 + 1) * 512], [pk(b)], [('zsb', h2)], [['zt', 'zs'][h2]])
        den = B['den']

        def att_norm(g, bo):
            pb_ = g % 2
            po = PS[bo][:T, 0:260].rearrange("p (h d) -> p h d", h=4)
            K.op('dve', lambda: nc.vector.tensor_tensor(out=den[:T, pb_, :], in0=po[:, :, 64], in1=qsm[:T, 2, 4 * g:4 * g + 4], op=ALU.add),
                 reads=[pk(bo), 'qsm2'], writes=[('den', pb_)])
            K.op('dve', lambda: nc.vector.reciprocal(out=den[:T, pb_, :], in_=den[:T, pb_, :]), reads=[('den', pb_)], writes=[('den', pb_)])
            K.op('dve', lambda: nc.vector.tensor_tensor(out=qf[:T, 4 * g:4 * g + 4, :], in0=po[:, :, 0:64],
                                                        in1=den[:T, pb_, :].unsqueeze(2).broadcast_to([T, 4, 64]), op=ALU.mult),
                 reads=[pk(bo), ('den', pb_)], writes=[('ob', g)])

        if not ti.samp:
            sbk = {}

            def att_scores(g):
                qk = ('QT', g // 2)
                rhsq = QT[0:65, 4 * g:4 * g + 4, :T]
                b1 = None
                if t > 0:
                    b1 = bank()
                    reserved.add(b1)
                    K.op('pe', lambda: nc.tensor.matmul(PS[b1][:, 0:4 * T], KT[:, g, tok0 - 128:tok0], rhsq, start=True, stop=True),
                         reads=[('KT', t - 1), qk], writes=[pk(b1)])
                b2 = bank()
                reserved.add(b2)
                K.op('pe', lambda: nc.tensor.matmul(PS[b2][:, 0:4 * T], KT[:, g, tok0:tok0 + 128], rhsq, start=True, stop=True),
                     reads=[('KT', t), qk], writes=[pk(b2)])
                sbk[g] = (b1, b2)

            def att_exp(g):
                pb_ = g % 2
                PTp, PTc = B['PTp'][pb_], B['PTc'][pb_]
                b1, b2 = sbk[g]
                if t > 0:
                    v1 = PS[b1][:, 0:4 * T].rearrange("p (h q) -> p h q", h=4)
                    K.op('act', lambda: nc.scalar.activation(out=PTp[0:64, :, 0:64], in_=v1[0:64, :, 0:64], func=AF.Exp), reads=[pk(b1)], writes=[('PTp', pb_)])
                    K.op('act', lambda: nc.scalar.activation(out=PTp[64:128, :, :], in_=v1[64:128, :, :], func=AF.Exp), reads=[pk(b1)], writes=[('PTp', pb_)])
                    reserved.discard(b1)
                v2 = PS[b2][:, 0:4 * T].rearrange("p (h q) -> p h q", h=4)
                K.op('act', lambda: nc.scalar.activation(out=PTc[0:64, :, :], in_=v2[0:64, :, :], func=AF.Exp), reads=[pk(b2)], writes=[('PTc', pb_)])
                K.op('act', lambda: nc.scalar.activation(out=PTc[64:128, :, 64:128], in_=v2[64:128, :, 64:128], func=AF.Exp), reads=[pk(b2)], writes=[('PTc', pb_)])
                reserved.discard(b2)

            def att_pv(g):
                pb_ = g % 2
                PTp, PTc = B['PTp'][pb_], B['PTc'][pb_]
                bo = bank()
                for hh in range(4):
                    if t > 0:
                        K.op('pe', lambda: nc.tensor.matmul(PS[bo][:T, hh * 65:(hh + 1) * 65], PTp[:, hh, :], VE[:, t - 1, g, :], start=True, stop=False),
                             reads=[('PTp', pb_), ('VE', t - 1), 'VE1'], writes=[pk(bo)])
                    K.op('pe', lambda: nc.tensor.matmul(PS[bo][:T, hh * 65:(hh + 1) * 65], PTc[:, hh, :], VE[:, t, g, :], start=(t == 0), stop=True),
                         reads=[('PTc', pb_), ('VE', t), 'VE1'], writes=[pk(bo)])
                return bo

            att_scores(0)
            for g in range(4):
                if g + 1 < 4:
                    att_scores(g + 1)
                att_exp(g)
                if g > 0:
                    bo = att_pv(g - 1)
                    att_norm(g - 1, bo)
            bo = att_pv(3)
            att_norm(3, bo)
        else:
            for g in range(4):
                pb_ = g % 2
                qk = ('QT', g // 2)
                bo = bank()
                PTs = B['PTs'][pb_]
                PTn = B['PTn']
                for s_ in range(2):
                    b1 = bank()
                    K.op('pe', lambda: nc.tensor.matmul(PS[b1][:, 0:128], B['KTc'][:, g, s_, :], QT[0:65, 4 * g:4 * g + 4, s_ * 32:(s_ + 1) * 32], start=True, stop=True),
                         reads=['KTc', qk], writes=[pk(b1)])
                    K.op('act', lambda: nc.scalar.activation(out=PTs[s_][:, :, s_ * 32:(s_ + 1) * 32], in_=PS[b1][:, 0:128].rearrange("p (h q) -> p h q", h=4), func=AF.Exp),
                         reads=[pk(b1)], writes=[('PTs', pb_, s_)])
                b2 = bank()
                K.op('pe', lambda: nc.tensor.matmul(PS[b2][0:64, 0:256], KT[:, g, tok0:tok0 + 64], QT[0:65, 4 * g:4 * g + 4, 0:64], start=True, stop=True),
                     reads=[('KT', t), qk], writes=[pk(b2)])
                v2 = PS[b2][0:64, 0:256].rearrange("p (h q) -> p h q", h=4)
                K.op('act', lambda: nc.scalar.activation(out=PTn[0:32, :, 0:32], in_=v2[0:32, :, 0:32], func=AF.Exp), reads=[pk(b2)], writes=['PTn'])
                K.op('act', lambda: nc.scalar.activation(out=PTn[32:64, :, 32:64], in_=v2[32:64, :, 32:64], func=AF.Exp), reads=[pk(b2)], writes=['PTn'])
                for hh in range(4):
                    K.op('pe', lambda: nc.tensor.matmul(PS[bo][:T, hh * 65:(hh + 1) * 65], PTs[0][:, hh, :], B['VEc'][:, 0, g, :], start=True, stop=False),
                         reads=[('PTs', pb_, 0), 'VEc'], writes=[pk(bo)])
                    K.op('pe', lambda: nc.tensor.matmul(PS[bo][:T, hh * 65:(hh + 1) * 65], PTs[1][:, hh, :], B['VEc'][:, 1, g, :], start=False, stop=False),
                         reads=[('PTs', pb_, 1), 'VEc'], writes=[pk(bo)])
                    K.op('pe', lambda: nc.tensor.matmul(PS[bo][:T, hh * 65:(hh + 1) * 65], PTn[:, hh, :], VE[0:64, t, g, :], start=False, stop=True),
                         reads=['PTn', ('VE', t), 'VE1'], writes=[pk(bo)])
                att_norm(g, bo)
        obf = qf[:, :, :].rearrange("p a b -> p (a b)")
        for h2 in range(2):
            sl = slice(h2 * 512, (h2 + 1) * 512)
            K.op('dve', lambda: nc.vector.tensor_tensor(out=og[:T, sl], in0=obf[:T, sl], in1=B['zsb'][:T, sl], op=ALU.mult),
                 reads=[('ob', 2 * h2), ('ob', 2 * h2 + 1), ('zsb', h2)], writes=['og'])
        out_proj(ti, j == 1, B['ogT'], ['ogTb'])

    done = False
    build.marks = []
    for l in range(2):
        done = a_layer(l)
        build.marks.append(K.ninstr['pe'])
        if done:
            break
    if not done and stop_after != 'A':
        Bv = b_setup()
        for j in range(2):
            b_layer(j, Bv)
            build.marks.append(K.ninstr['pe'])
    if done or stop_after == 'A':
        for t in range(NT):
            T = 64 if t == NPT else 128
            out_stamps.append(K.dma('sp', y_d[t * 128:t * 128 + T, :], xres[:T, t, :], reads=[('x', t)], writes=[('y', t)]))
    K.wait_all('sp', out_stamps)
    K.barrier()
    es.close()
    build.stats = (dict(K.ninstr), K.nwait)
    build.sim = K.simulate()
    return nc


def prep_inputs(inputs):
    c = host_consts()
    f = lambda a: np.ascontiguousarray(np.asarray(a, dtype=np.float32))
    xp = f(inputs['x_prompt'])
    xs = f(inputs['x_sample'])
    sd = f(inputs['state_delta'])
    sc = f(inputs['state_conv'])
    ck = f(inputs['cache_k'])
    cv = f(inputs['cache_v'])
    acw = f(inputs['a_conv_w'])
    convw = np.ascontiguousarray(acw.reshape(2, 4, 24, 128).transpose(0, 3, 2, 1))
    bc = lambda v, n: np.ascontiguousarray(np.broadcast_to(f(v)[:, None, :], (v.shape[0], 128, n)))
    shared = {
        'a_w_in': f(inputs['a_w_in']), 'a_w_out': f(inputs['a_w_out']), 'b_w_kv': f(inputs['b_w_kv']),
        'b_w_in': f(inputs['b_w_in']), 'b_w_out': f(inputs['b_w_out']),
        'convw': convw, 'alog_bc': bc(inputs['a_log'], 8), 'dtb_bc': bc(inputs['a_dt_bias'], 8),
        'normw_bc': bc(inputs['a_norm_w'], 128),
        'lng_bc': np.ascontiguousarray(np.concatenate([bc(inputs['a_ln_g'], D), bc(inputs['b_ln_g'], D)], 0)),
        'lnb_bc': np.ascontiguousarray(np.concatenate([bc(inputs['a_ln_b'], D), bc(inputs['b_ln_b'], D)], 0)),
        'sink_bc': bc(inputs['b_sinks'], 16),
        'identf': c['identf'], 'onesf': c['onesf'], 'masks': c['masks'], 'rope': c['rope'],
    }
    maps = []
    for i in range(NCORES):
        m = dict(shared)
        m['x'] = np.ascontiguousarray(np.concatenate([xp[i], xs[2 * i], xs[2 * i + 1]], 0))
        m['sdelta'] = np.ascontiguousarray(sd[:, 2 * i:2 * i + 2])
        scc = sc[:, 2 * i:2 * i + 2]
        m['sconv'] = np.ascontiguousarray(scc.reshape(2, 2, 3, 24, 128).transpose(0, 1, 4, 3, 2))
        m['ck'] = np.ascontiguousarray(ck[2 * i:2 * i + 2].reshape(2, 128, 256))
        m['cv'] = np.ascontiguousarray(cv[2 * i:2 * i + 2].reshape(2, 128, 256))
        maps.append(m)
    return maps


_NC_CACHE = {}


def kernel(**inputs):
    maps = prep_inputs(inputs)
    if 'nc' not in _NC_CACHE:
        _NC_CACHE['nc'] = build()
    nc = _NC_CACHE['nc']
    maps = [{n: m[n] for n in build.in_names} for m in maps]
    r = run_bass_kernel_spmd(nc, maps, core_ids=list(range(NCORES))).results
    y = np.stack([r[i]['y'] for i in range(NCORES)])
    y_prompt = np.ascontiguousarray(y[:, :SEQ])
    y_sample = np.ascontiguousarray(y[:, SEQ:].reshape(16, 32, D))
    p_delta = np.stack([r[i]['pdelta'] for i in range(NCORES)], 1)
    s_delta = np.concatenate([r[i]['sdelta_o'] for i in range(NCORES)], 1)
    pc = np.stack([r[i]['pconv'] for i in range(NCORES)], 1)
    p_conv = np.ascontiguousarray(pc.transpose(0, 1, 4, 3, 2).reshape(2, 8, 3, 3072))
    scv = np.concatenate([r[i]['sconv_o'] for i in range(NCORES)], 1)
    s_conv = np.ascontiguousarray(scv.transpose(0, 1, 4, 3, 2).reshape(2, 16, 3, 3072))
    if WITH_B:
        p_k = np.stack([r[i]['pk'] for i in range(NCORES)]).reshape(8, 128, 4, 64)
        p_v = np.stack([r[i]['pv'] for i in range(NCORES)]).reshape(8, 128, 4, 64)
        s_k = np.concatenate([r[i]['sk'] for i in range(NCORES)]).reshape(16, 128, 4, 64)
        s_v = np.concatenate([r[i]['sv'] for i in range(NCORES)]).reshape(16, 128, 4, 64)
    else:
        p_k = np.zeros((8, 128, 4, 64), np.float32)
        p_v = np.zeros((8, 128, 4, 64), np.float32)
        s_k = np.zeros((16, 128, 4, 64), np.float32)
        s_v = np.zeros((16, 128, 4, 64), np.float32)
    outs = (y_prompt, y_sample, p_delta, p_conv, p_k, p_v, s_delta, s_conv, s_k, s_v)
    return tuple(np.ascontiguousarray(o.astype(np.float32)) for o in outs)
```
